# Optimizing a Trainium2 kernel written in Bass

```python
import jax, jax.numpy as jnp
from jax import lax
import numpy as np

D_MODEL = 1024
BATCH = 8
SEQ = 2048
DEPTH = 4

CTX_LEN = 256
GRID_W = 64
N_MIXERS = 4
BRANCH_W = D_MODEL // 2
MIX_W = N_MIXERS * BRANCH_W
HEAD_DIM = 64
N_HEADS_A = BRANCH_W // HEAD_DIM
A_CONV = 4
RG_C = 8.0
B_CONV = 3
C_CONV = 31
POOL_WINDOWS = (2, 4, 8, 16)
POOL_GROUP = BRANCH_W // len(POOL_WINDOWS)
N_IN_SLOTS = 11
IN_W = N_IN_SLOTS * BRANCH_W
RMS_EPS = 1e-6
LN_EPS = 1e-5

kernel_name = "hybrid_parallel_group_flow_backbone"


def rms_norm(x, g):
    xf = x.astype(jnp.float32)
    y = xf * lax.rsqrt(jnp.mean(xf * xf, axis=-1, keepdims=True) + RMS_EPS)
    return (y * g.astype(jnp.float32)).astype(x.dtype)


def layer_norm(x, g, b):
    xf = x.astype(jnp.float32)
    mu = jnp.mean(xf, axis=-1, keepdims=True)
    xc = xf - mu
    var = jnp.mean(xc * xc, axis=-1, keepdims=True)
    y = xc * lax.rsqrt(var + LN_EPS) * g.astype(jnp.float32) + b.astype(jnp.float32)
    return y.astype(x.dtype)


def depthwise_conv(x, w, pad_left, pad_right):
    k = w[:, None, :].astype(x.dtype)
    return lax.conv_general_dilated(
        x, k, window_strides=(1,), padding=[(pad_left, pad_right)],
        dimension_numbers=('NWC', 'WIO', 'NWC'), feature_group_count=x.shape[-1])


def _rev_bwd(z):
    return jnp.stack([z[0], jnp.flip(z[1], axis=1)])


def rglru(v, lp, h0):
    bsz, t, _ = v.shape
    xc = jnp.stack([depthwise_conv(v, lp['a_conv'][0], A_CONV - 1, 0),
                    depthwise_conv(v, lp['a_conv'][1], 0, A_CONV - 1)])
    xh = xc.reshape(2, bsz, t, N_HEADS_A, HEAD_DIM)
    r = jax.nn.sigmoid(jnp.einsum('nbthi,nhij->nbthj', xh, lp['a_wr']).reshape(2, bsz, t, BRANCH_W)
                       + lp['a_br'][:, None, None, :])
    ig = jax.nn.sigmoid(jnp.einsum('nbthi,nhij->nbthj', xh, lp['a_wi']).reshape(2, bsz, t, BRANCH_W)
                        + lp['a_bi'][:, None, None, :])
    log_a = -RG_C * r.astype(jnp.float32) * jax.nn.softplus(-lp['a_lam'].astype(jnp.float32))[:, None, None, :]
    a = jnp.exp(log_a)
    b = jnp.sqrt(-jnp.expm1(2.0 * log_a)) * (ig * xc).astype(jnp.float32)
    a = _rev_bwd(a)
    b = _rev_bwd(b)

    def step(h, ab):
        at, bt = ab
        h = at * h + bt
        return h, h

    h_final, ys = lax.scan(step, h0, (jnp.moveaxis(a, 2, 0), jnp.moveaxis(b, 2, 0)))
    ys = _rev_bwd(jnp.moveaxis(ys, 0, 2))
    return (ys[0] + ys[1]).astype(v.dtype), h_final


def pool_grid(u):
    bsz, s, ch = u.shape
    rows = s // GRID_W
    g = u.reshape(bsz, rows, GRID_W, ch).astype(jnp.float32)
    P = jnp.pad(jnp.cumsum(jnp.cumsum(g, axis=1), axis=2), ((0, 0), (1, 0), (1, 0), (0, 0)))
    r = jnp.arange(rows)
    cidx = jnp.arange(GRID_W)
    outs = []
    for gi, w in enumerate(POOL_WINDOWS):
        Pg = P[..., gi * POOL_GROUP:(gi + 1) * POOL_GROUP]
        r0 = jnp.clip(r - w // 2, 0, rows)
        r1 = jnp.clip(r + w // 2, 0, rows)
        c0 = jnp.clip(cidx - w // 2, 0, GRID_W)
        c1 = jnp.clip(cidx + w // 2, 0, GRID_W)
        Pr1 = Pg[:, r1]
        Pr0 = Pg[:, r0]
        total = Pr1[:, :, c1] - Pr1[:, :, c0] - Pr0[:, :, c1] + Pr0[:, :, c0]
        cnt = ((r1 - r0)[:, None] * (c1 - c0)[None, :]).astype(jnp.float32)
        outs.append(total / cnt[None, :, :, None])
    mean = jnp.concatenate(outs, axis=-1).reshape(bsz, s, ch)
    return mean.astype(u.dtype) - u


def pool_seq(u):
    bsz, t, ch = u.shape
    P = jnp.pad(jnp.cumsum(u.astype(jnp.float32), axis=1), ((0, 0), (1, 0), (0, 0)))
    pos = jnp.arange(t)
    outs = []
    for gi, w in enumerate(POOL_WINDOWS):
        Pg = P[..., gi * POOL_GROUP:(gi + 1) * POOL_GROUP]
        lo = jnp.clip(pos - w // 2, 0, t)
        hi = jnp.clip(pos + w // 2, 0, t)
        outs.append((Pg[:, hi] - Pg[:, lo]) / (hi - lo).astype(jnp.float32)[None, :, None])
    mean = jnp.concatenate(outs, axis=-1)
    return mean.astype(u.dtype) - u


def mix(h, lp, h0, pool_fn):
    proj = jnp.einsum('btd,de->bte', h, lp['w_in'])
    (va, za, bb, bc, bv, zb, ca, cg, zc, dv, zd) = jnp.split(proj, N_IN_SLOTS, axis=-1)
    ya, h_final = rglru(va, lp, h0)
    yb = bb * depthwise_conv(bc * bv, lp['b_conv'], B_CONV // 2, B_CONV // 2)
    u = ca * jax.nn.sigmoid(cg)
    u = depthwise_conv(u, lp['c_conv'], C_CONV // 2, C_CONV // 2)
    u = layer_norm(u, lp['c_ln_g'], lp['c_ln_b'])
    yc = jnp.einsum('btc,ce->bte', jax.nn.silu(u), lp['c_pw']) + lp['c_pw_b']
    p = pool_fn(dv)
    bsz, t, _ = p.shape
    pg = p.reshape(bsz, t, len(POOL_WINDOWS), POOL_GROUP)
    yd = (jnp.einsum('btgi,gij->btgj', pg, lp['d_w']) + lp['d_b']).reshape(bsz, t, BRANCH_W) * lp['d_scale']
    cat = jnp.concatenate([ya * jax.nn.silu(za), yb * jax.nn.silu(zb),
                           yc * jax.nn.silu(zc), yd * jax.nn.silu(zd)], axis=-1)
    return jnp.einsum('bte,ed->btd', cat, lp['w_out']), h_final


def setup_inputs(seed: int = 0) -> dict:
    key = jax.random.key(seed)
    ks = jax.random.split(key, 26)
    f32 = jnp.float32
    nrm = lambda k, shape, s: jax.random.normal(k, shape, f32) * s
    u = jax.random.uniform(ks[12], (DEPTH, 2, BRANCH_W), f32, minval=0.9, maxval=0.999)
    pa = u ** (1.0 / RG_C)
    a_lam = jnp.log(pa) - jnp.log1p(-pa)
    return {
        "x": nrm(ks[0], (BATCH, SEQ, D_MODEL), 1.0),
        "c": nrm(ks[1], (BATCH, D_MODEL), 1.0),
        "ctx": nrm(ks[2], (BATCH, CTX_LEN, D_MODEL), 1.0),
        "c_ctx": nrm(ks[3], (D_MODEL,), 1.0),
        "mod_w": nrm(ks[4], (DEPTH, D_MODEL, 3 * D_MODEL), 0.5 * D_MODEL ** -0.5),
        "mod_b": nrm(ks[5], (DEPTH, 3 * D_MODEL), 0.02),
        "norm_g": 1.0 + nrm(ks[6], (DEPTH, D_MODEL), 0.05),
        "w_in": nrm(ks[7], (DEPTH, D_MODEL, IN_W), D_MODEL ** -0.5),
        "w_out": nrm(ks[8], (DEPTH, MIX_W, D_MODEL), MIX_W ** -0.5),
        "a_conv": nrm(ks[9], (DEPTH, 2, A_CONV, BRANCH_W), A_CONV ** -0.5),
        "a_wr": nrm(ks[10], (DEPTH, 2, N_HEADS_A, HEAD_DIM, HEAD_DIM), HEAD_DIM ** -0.5),
        "a_br": nrm(ks[11], (DEPTH, 2, BRANCH_W), 0.1),
        "a_wi": nrm(ks[13], (DEPTH, 2, N_HEADS_A, HEAD_DIM, HEAD_DIM), HEAD_DIM ** -0.5),
        "a_bi": nrm(ks[14], (DEPTH, 2, BRANCH_W), 0.1),
        "a_lam": a_lam,
        "b_conv": nrm(ks[15], (DEPTH, B_CONV, BRANCH_W), B_CONV ** -0.5),
        "c_conv": nrm(ks[16], (DEPTH, C_CONV, BRANCH_W), C_CONV ** -0.5),
        "c_ln_g": 1.0 + nrm(ks[17], (DEPTH, BRANCH_W), 0.05),
        "c_ln_b": nrm(ks[18], (DEPTH, BRANCH_W), 0.02),
        "c_pw": nrm(ks[19], (DEPTH, BRANCH_W, BRANCH_W), BRANCH_W ** -0.5),
        "c_pw_b": nrm(ks[20], (DEPTH, BRANCH_W), 0.02),
        "d_w": nrm(ks[21], (DEPTH, len(POOL_WINDOWS), POOL_GROUP, POOL_GROUP), POOL_GROUP ** -0.5),
        "d_b": nrm(ks[22], (DEPTH, len(POOL_WINDOWS), POOL_GROUP), 0.02),
        "d_scale": 1.0 + nrm(ks[23], (DEPTH, BRANCH_W), 0.1),
        "final_g": 1.0 + nrm(ks[24], (D_MODEL,), 0.05),
    }


def reference(x, c, ctx, c_ctx, mod_w, mod_b, norm_g, w_in, w_out, a_conv, a_wr, a_br, a_wi, a_bi,
              a_lam, b_conv, c_conv, c_ln_g, c_ln_b, c_pw, c_pw_b, d_w, d_b, d_scale, final_g):
    bsz = x.shape[0]
    for l in range(DEPTH):
        last = l == DEPTH - 1
        lp = dict(w_in=w_in[l], w_out=w_out[l], a_conv=a_conv[l], a_wr=a_wr[l], a_br=a_br[l],
                  a_wi=a_wi[l], a_bi=a_bi[l], a_lam=a_lam[l], b_conv=b_conv[l], c_conv=c_conv[l],
                  c_ln_g=c_ln_g[l], c_ln_b=c_ln_b[l], c_pw=c_pw[l], c_pw_b=c_pw_b[l],
                  d_w=d_w[l], d_b=d_b[l], d_scale=d_scale[l])
        mod = jax.nn.silu(c) @ mod_w[l] + mod_b[l]
        shift, scale, gate = jnp.split(mod[:, None, :], 3, axis=-1)
        mod_k = jax.nn.silu(c_ctx) @ mod_w[l] + mod_b[l]
        shift_k, scale_k, gate_k = jnp.split(mod_k, 3)
        hk = rms_norm(ctx, norm_g[l]) * (1.0 + scale_k) + shift_k
        h = rms_norm(x, norm_g[l]) * (1.0 + scale) + shift
        h0 = jnp.zeros((2, bsz, BRANCH_W), jnp.float32)
        if last:
            vk = jnp.einsum('btd,de->bte', hk, lp['w_in'][:, :BRANCH_W])
            _, state_k = rglru(vk, lp, h0)
        else:
            yk, state_k = mix(hk, lp, h0, pool_seq)
            ctx = ctx + gate_k * yk
        y, _ = mix(h, lp, state_k, pool_grid)
        x = x + gate * y
    return rms_norm(x, final_g)
```

```python
from contextlib import ExitStack, contextmanager
import numpy as np
import ml_dtypes
import concourse.bass as bass
import concourse.mybir as mybir
from concourse.bass_utils import run_bass_kernel_spmd

F32 = mybir.dt.float32
BF16 = mybir.dt.bfloat16
AF = mybir.ActivationFunctionType
ALU = mybir.AluOpType

DEPTH = 4
D = 1024
SEQ = 2048
CTX = 256
TT = CTX + SEQ
BW = 512
IN_W = 5632
GRID_W = 64
ROWS = SEQ // GRID_W
WINS = (2, 4, 8, 16)
NV = 53
NVX = 64
ENGS = ["tensor", "vector", "scalar", "gpsimd", "sync"]
TILES = [(0, 256), (256, 512), (768, 512), (1280, 512), (1792, 512)]
S_VA, S_ZA, S_BB, S_BC, S_BV, S_ZB, S_CA, S_CG, S_ZC, S_DV, S_ZD = range(11)


def _pool_plan():
    mats = []
    key2idx = {}
    plan = []
    goff = []
    gcnt = []
    icc = np.zeros((4, 64), np.float32)
    irc = np.zeros((4, ROWS), np.float64)
    tabc = np.zeros((4, CTX), np.float32)
    for g, w in enumerate(WINS):
        h = w // 2
        start = len(mats)
        key2idx = {}
        pl = {}
        pos = np.arange(CTX)
        lo = np.clip(pos - h, 0, CTX)
        hi = np.clip(pos + h, 0, CTX)
        cnt = (hi - lo)
        tabc[g] = 1.0 / cnt
        M = np.zeros((CTX, CTX), np.float32)
        for tp in range(CTX):
            M[lo[tp]:hi[tp], tp] = 1.0
            M[tp, tp] -= cnt[tp]
        for j in range(2):
            lst = []
            for i in range(2):
                blk = M[i * 128:(i + 1) * 128, j * 128:(j + 1) * 128]
                if not blk.any():
                    continue
                k = blk.tobytes()
                if k not in key2idx:
                    key2idx[k] = len(mats)
                    mats.append(blk.copy())
                lst.append((i, key2idx[k]))
            pl[j] = lst
        r = np.arange(ROWS)
        r0 = np.clip(r - h, 0, ROWS)
        r1 = np.clip(r + h, 0, ROWS)
        c = np.arange(GRID_W)
        c0 = np.clip(c - h, 0, GRID_W)
        c1 = np.clip(c + h, 0, GRID_W)
        icc[g] = 1.0 / (c1 - c0)
        irc[g] = 1.0 / (r1 - r0)
        colbox = np.zeros((GRID_W, GRID_W), np.float32)
        for cp in range(GRID_W):
            colbox[c0[cp]:c1[cp], cp] = 1.0
        for j in range(ROWS // 2):
            lst = []
            for i in range(ROWS // 2):
                blk = np.zeros((128, 128), np.float32)
                for a in range(2):
                    rr = 2 * i + a
                    for b in range(2):
                        rp = 2 * j + b
                        if r0[rp] <= rr < r1[rp]:
                            blk[a * 64:(a + 1) * 64, b * 64:(b + 1) * 64] = colbox
                if i == j:
                    for b in range(2):
                        rp = 2 * j + b
                        cn = (r1[rp] - r0[rp]) * (c1 - c0)
                        blk[b * 64 + np.arange(64), b * 64 + np.arange(64)] -= cn
                if not blk.any():
                    continue
                k = blk.tobytes()
                if k not in key2idx:
                    key2idx[k] = len(mats)
                    mats.append(blk.copy())
                lst.append((i + 2, key2idx[k]))
            pl[j + 2] = lst
        plan.append(pl)
        goff.append(start)
        gcnt.append(len(mats) - start)
    pm = np.stack(mats, 0)
    pm = np.ascontiguousarray(pm.transpose(1, 0, 2)).astype(ml_dtypes.bfloat16)
    return pm, plan, goff, gcnt, icc, irc, tabc


PM, PLAN, GOFF, GCNT, ICC, IRC, TABC = _pool_plan()
NM = PM.shape[1]
NMG = max(GCNT)


class Buf:
    def __init__(self, name, t, nseg, dma_sem, init_rd):
        self.name = name
        self.t = t
        self.nseg = nseg
        self.lw = [None] * nseg
        self.rd = [list(init_rd) for _ in range(nseg)]
        self.dma_sem = dma_sem


class Prog:
    def __init__(self, nc, stack):
        self.nc = nc
        self.stack = stack
        self.sems = {}
        self.count = {}
        self.seen = {e: {} for e in ENGS}
        self.pending = {}
        self.phase_bufs = None
        self.psb = []
        self.psi = 0
        for e in ENGS:
            self.new_sem("E_" + e)

    def new_sem(self, key):
        if key not in self.sems:
            self.sems[key] = self.stack.enter_context(self.nc.semaphore(key))
            self.count[key] = 0
        return key

    def buf(self, name, shape, dtype, nseg=1, dma=False, psum=False, stack=None):
        st = stack if stack is not None else self.stack
        self.uid = getattr(self, "uid", 0) + 1
        tname = "s%d_%s" % (self.uid, name)
        if psum:
            t = st.enter_context(self.nc.psum_tensor(tname, shape, dtype))
        else:
            t = st.enter_context(self.nc.sbuf_tensor(tname, shape, dtype))
        ds = self.new_sem("D_" + name) if dma else None
        init = list(self.pending.items()) if stack is not None else []
        b = Buf(name, t, nseg, ds, init)
        if stack is not None and self.phase_bufs is not None:
            self.phase_bufs.append(b)
        return b

    @contextmanager
    def phase(self):
        old = self.phase_bufs
        self.phase_bufs = []
        with ExitStack() as st:
            yield st
            for b in self.phase_bufs:
                for s in range(b.nseg):
                    for tok in [b.lw[s]] + b.rd[s]:
                        if tok is None:
                            continue
                        k, v = tok
                        if self.pending.get(k, 0) < v:
                            self.pending[k] = v
        self.phase_bufs = old

    def ps(self):
        b = self.psb[self.psi % len(self.psb)]
        self.psi += 1
        return b

    def _expand(self, lst):
        out = []
        for b, s in lst:
            if s is None:
                s = range(b.nseg)
            if isinstance(s, int):
                out.append((b, s))
            else:
                for x in s:
                    out.append((b, x))
        return out

    def _need(self, eng, toks):
        best = {}
        for tok in toks:
            if tok is None:
                continue
            k, v = tok
            if eng == "tensor" and k == "E_tensor":
                continue
            if best.get(k, 0) < v:
                best[k] = v
        e = getattr(self.nc, eng)
        for k, v in best.items():
            if self.seen[eng].get(k, 0) >= v:
                continue
            self.seen[eng][k] = v
            e.wait_ge(self.sems[k], v)

    def _deps(self, eng, reads, writes):
        toks = []
        rl = self._expand(reads)
        wl = self._expand(writes)
        for b, s in rl:
            toks.append(b.lw[s])
        for b, s in wl:
            toks.append(b.lw[s])
            toks.extend(b.rd[s])
        self._need(eng, toks)
        return rl, wl

    def _record(self, tok, rl, wl):
        for b, s in rl:
            b.rd[s].append(tok)
        for b, s in wl:
            b.lw[s] = tok
            b.rd[s] = []

    def op(self, eng, fn, reads=(), writes=(), inc=True):
        rl, wl = self._deps(eng, reads, writes)
        key = "E_" + eng
        ins = fn(getattr(self.nc, eng))
        if inc:
            self.count[key] += 1
            ins.then_inc(self.sems[key], 1)
            v = self.count[key]
        else:
            v = self.count[key] + 1
        self._record((key, v), rl, wl)

    def dma(self, eng, out_ap, in_ap, reads=(), writes=(), sem_buf=None, **kw):
        rl, wl = self._deps(eng, reads, writes)
        key = sem_buf.dma_sem
        self.count[key] += 16
        getattr(self.nc, eng).dma_start(out=out_ap, in_=in_ap, **kw).then_inc(self.sems[key], 16)
        self._record((key, self.count[key]), rl, wl)

    def wait_all(self, eng, bufs):
        toks = []
        for b in bufs:
            for s in range(b.nseg):
                toks.append(b.lw[s])
                toks.extend(b.rd[s])
        self._need(eng, toks)


def hseg(k, tile):
    return k * 5 + tile


def build(NL=DEPTH):
    nc = bass.Bass("TRN2", target_bir_lowering=False, dynamic_dma_scratch_size=8192)
    dt = lambda n, s, d=F32, k="ExternalInput": nc.dram_tensor(n, s, d, kind=k).ap()
    x_d = dt("x", [SEQ, D])
    ctx_d = dt("ctx", [CTX, D])
    vecD_d = dt("vecD", [3 + 4 * DEPTH, D])
    vec5_d = dt("vec5", [DEPTH, NV, BW])
    modw_d = dt("mod_w", [DEPTH, D, 3 * D])
    win_d = dt("w_in", [DEPTH, D, IN_W])
    wout_d = dt("w_out", [DEPTH, 2048, D])
    cpw_d = dt("c_pw", [DEPTH, BW, BW])
    dw_d = dt("d_w", [DEPTH, 4, 128, 128])
    gm_d = dt("gm", [DEPTH, 4, 128, 4, 128])
    pm_d = dt("pmats", [128, NM, 128], BF16)
    icc_d = dt("icc", [128, 4 * 64])
    tabc_d = dt("tabc", [128, 4 * CTX])
    out_d = dt("out", [SEQ, D], F32, "ExternalOutput")
    import os as _os
    DBG = _os.environ.get("KDBG")
    dbg_d = dt("dbg", [128, 8, TT], F32, "ExternalOutput") if DBG else None

    with ExitStack() as st:
        P = Prog(nc, st)
        for i in range(8):
            P.psb.append(P.buf("psb%d" % i, [128, 512], F32, psum=True))

        res = P.buf("res", [128, 8, TT], F32, nseg=40)
        hT = P.buf("hT", [128, 8, TT], BF16, nseg=40)
        cat = P.buf("cat", [128, 8, TT], BF16, nseg=40)
        ident_f = P.buf("ident_f", [128, 128], F32)
        ident_b = P.buf("ident_b", [128, 128], BF16)
        ones1024 = P.buf("ones1024", [128, 128], BF16)
        ones512 = P.buf("ones512", [128, 128], BF16)
        cst = P.buf("cst", [128, 4], F32)
        VD = P.buf("VD", [128, 8, 3 + 4 * DEPTH], F32)
        V = P.buf("V", [128, 4, NVX], F32)
        MT = P.buf("MT", [128, DEPTH, 3, 8, 2], F32, nseg=DEPTH)
        sc = P.buf("sc", [128, 8, 2], BF16)
        mring = [P.buf("mring%d" % i, [128, 8, 128], BF16, dma=True) for i in range(2)]
        modacc = P.buf("modacc", [128, 48], F32)
        mstate = {"pend": [], "ri": 0}
        NR = 8
        ring = [P.buf("ring%d" % i, [128, 8, 128], BF16, dma=True) for i in range(NR)]
        woring = [P.buf("wor%d" % i, [128, 8, 128], BF16, dma=True) for i in range(2)]
        gmring = [P.buf("gmr%d" % i, [128, 4, 128], BF16, dma=True) for i in range(2)]
        dw = P.buf("dw", [128, 4, 128], BF16, dma=True)
        ringi = [0]
        gmi = [0]
        woi = [0]

        EPS6 = cst.t[:, 0:1]
        EPS5 = cst.t[:, 1:2]
        ONE = cst.t[:, 2:3]
        QUART = cst.t[:, 3:4]

        P.op("gpsimd", lambda e: e.memset(ident_f.t[:], 1.0), writes=[(ident_f, 0)])
        P.op("gpsimd", lambda e: e.affine_select(out=ident_f.t[:], in_=ident_f.t[:], pattern=[[-1, 128]],
                                                  compare_op=ALU.is_equal, fill=0.0, base=0, channel_multiplier=1),
             reads=[(ident_f, 0)], writes=[(ident_f, 0)])
        P.op("gpsimd", lambda e: e.tensor_copy(out=ident_b.t[:], in_=ident_f.t[:]), reads=[(ident_f, 0)],
             writes=[(ident_b, 0)])
        P.op("gpsimd", lambda e: e.memset(ones1024.t[:], 1.0 / 1024.0), writes=[(ones1024, 0)])
        P.op("gpsimd", lambda e: e.memset(ones512.t[:], 1.0 / 512.0), writes=[(ones512, 0)])
        P.op("gpsimd", lambda e: e.memset(cst.t[:, 0:1], 1e-6), writes=[(cst, 0)])
        P.op("gpsimd", lambda e: e.memset(cst.t[:, 1:2], 1e-5), writes=[(cst, 0)])
        P.op("gpsimd", lambda e: e.memset(cst.t[:, 2:3], 1.0), writes=[(cst, 0)])
        P.op("gpsimd", lambda e: e.memset(cst.t[:, 3:4], 0.25 + 1.2e-7), writes=[(cst, 0)])

        def kp(ap):
            return ap.rearrange("(k p) c -> p k c", p=128)

        def load_w(l, slot, j):
            b = ring[ringi[0] % NR]
            ringi[0] += 1
            c0 = slot * BW + j * 128
            P.dma("gpsimd", b.t[:], kp(win_d[l, :, c0:c0 + 128]), writes=[(b, 0)], sem_buf=b)
            return b

        def load_gm(l, j):
            b = gmring[gmi[0] % 2]
            gmi[0] += 1
            P.dma("gpsimd", b.t[:], gm_d[l, j], writes=[(b, 0)], sem_buf=b)
            return b

        def mod_issue(l, mc):
            b = mring[mstate["ri"] % 2]
            mstate["ri"] += 1
            P.dma("gpsimd", b.t[:], kp(modw_d[l, :, mc * 128:(mc + 1) * 128]), writes=[(b, 0)], sem_buf=b)
            mstate["pend"].append((mc, b))

        def mod_consume():
            for (mc, b) in mstate["pend"]:
                psm = P.ps()
                for k in range(8):
                    P.op("tensor", lambda e: e.matmul(psm.t[:, 0:2], lhsT=b.t[:, k, :], rhs=sc.t[:, k, 0:2],
                                                      start=(k == 0), stop=(k == 7)),
                         reads=[(b, 0), (sc, 0)], writes=[(psm, 0)], inc=(k == 7))
                P.op("vector", lambda e: e.tensor_copy(out=modacc.t[:, 2 * mc:2 * mc + 2], in_=psm.t[:, 0:2]),
                     reads=[(psm, 0)], writes=[(modacc, 0)])
            mstate["pend"] = []

        def mod_finish(l):
            for j in range(3):
                for col in range(2):
                    tt("vector", MT.t[:, l, j, :, col], modacc.t[:, j * 16 + col:j * 16 + 16:2],
                       VD.t[:, :, 3 + 4 * l + 1 + j], ALU.add, [(modacc, 0), (VD, 0)], [(MT, l)])
            for col in range(2):
                P.op("vector", lambda e: e.scalar_tensor_tensor(out=MT.t[:, l, 1, :, col], in0=MT.t[:, l, 1, :, col],
                                                                scalar=1.0, in1=VD.t[:, :, 3 + 4 * l],
                                                                op0=ALU.add, op1=ALU.mult),
                     reads=[(MT, l), (VD, 0)], writes=[(MT, l)])

        def inproj(wb, tile, ps):
            off, wd = TILES[tile]
            for k in range(8):
                P.op("tensor", lambda e: e.matmul(ps.t[:, 0:wd], lhsT=wb.t[:, k, :], rhs=hT.t[:, k, off:off + wd],
                                                  start=(k == 0), stop=(k == 7)),
                     reads=[(wb, 0), (hT, hseg(k, tile))], writes=[(ps, 0)], inc=(k == 7))

        def act(out, in_, func, reads, writes, bias=None, scale=None):
            kw = {}
            if bias is not None:
                kw["bias"] = bias
            if scale is not None:
                kw["scale"] = scale
            P.op("scalar", lambda e: e.activation(out=out, in_=in_, func=func, **kw), reads=reads, writes=writes)

        def tt(eng, out, in0, in1, op, reads, writes):
            P.op(eng, lambda e: e.tensor_tensor(out=out, in0=in0, in1=in1, op=op), reads=reads, writes=writes)

        def rms_alloc(ph, tag):
            sq = [P.buf("%s_sq%d" % (tag, i), [128, 512], BF16, stack=ph) for i in range(2)]
            rt = P.buf(tag + "_rt", [128, 512], F32, stack=ph)
            rs = [P.buf(tag + "_rs%d" % i, [128, 512], F32, stack=ph) for i in range(2)]
            return sq, rt, rs

        def rms_stats(tile, bufs, it):
            off, wd = TILES[tile]
            sq, rt, rsl = bufs
            rs = rsl[it % 2]
            pss = P.ps()
            for k in range(8):
                s = sq[k % 2]
                act(s.t[:, :wd], res.t[:, k, off:off + wd], AF.Square, [(res, hseg(k, tile))], [(s, 0)])
                P.op("tensor", lambda e: e.matmul(pss.t[:, :wd], lhsT=ones1024.t[:], rhs=s.t[:, :wd],
                                                  start=(k == 0), stop=(k == 7)),
                     reads=[(ones1024, 0), (s, 0)], writes=[(pss, 0)])
            act(rt.t[:, :wd], pss.t[:, :wd], AF.Sqrt, [(pss, 0), (cst, 0)], [(rt, 0)], bias=EPS6)
            P.op("vector", lambda e: e.reciprocal(out=rs.t[:, :wd], in_=rt.t[:, :wd]), reads=[(rt, 0)],
                 writes=[(rs, 0)])
            return rs

        with P.phase() as ph:
            vds = P.buf("vds", [3 + 4 * DEPTH, D], F32, dma=True, stack=ph)
            nvd = 3 + 4 * DEPTH
            P.dma("sync", vds.t[:], vecD_d, writes=[(vds, 0)], sem_buf=vds)
            for k in range(8):
                ps = P.ps()
                P.op("tensor", lambda e: e.transpose(out=ps.t[:, 0:nvd], in_=vds.t[0:nvd, k * 128:(k + 1) * 128],
                                                     identity=ident_f.t[0:nvd, 0:nvd]),
                     reads=[(vds, 0), (ident_f, 0)], writes=[(ps, 0)])
                P.op("vector", lambda e: e.tensor_copy(out=VD.t[:, k, :], in_=ps.t[:, 0:nvd]), reads=[(ps, 0)],
                     writes=[(VD, 0)])
            act(sc.t[:], VD.t[:, :, 0:2], AF.Silu, [(VD, 0)], [(sc, 0)])
            mod_issue(0, 0)
            for mc in range(24):
                if mc + 1 < 24:
                    mod_issue(0, mc + 1)
                pend = mstate["pend"]
                mstate["pend"] = pend[:1]
                mod_consume()
                mstate["pend"] = pend[1:]
            mod_finish(0)
            xst = [P.buf("xst%d" % i, [128, D], F32, dma=True, stack=ph) for i in range(2)]
            for t128 in range(18):
                s_ = xst[t128 % 2]
                src = ctx_d[t128 * 128:(t128 + 1) * 128, :] if t128 < 2 else x_d[(t128 - 2) * 128:(t128 - 1) * 128, :]
                P.dma("sync", s_.t[:], src, writes=[(s_, 0)], sem_buf=s_)
                tile = 0 if t128 < 2 else 1 + (t128 - 2) // 4
                for half in range(2):
                    ps = P.ps()
                    for q in range(4):
                        k = half * 4 + q
                        P.op("tensor", lambda e: e.transpose(out=ps.t[:, q * 128:(q + 1) * 128],
                                                             in_=s_.t[:, k * 128:(k + 1) * 128], identity=ident_f.t[:]),
                             reads=[(s_, 0), (ident_f, 0)], writes=[(ps, 0)], inc=(q == 3))
                    dst = res.t[:, half * 4:(half + 1) * 4, t128 * 128:(t128 + 1) * 128]
                    srcp = ps.t[:, 0:512].rearrange("p (q c) -> p q c", q=4)
                    wr = [(res, hseg(half * 4 + q, tile)) for q in range(4)]
                    if half == 0:
                        P.op("vector", lambda e: e.tensor_copy(out=dst, in_=srcp), reads=[(ps, 0)], writes=wr)
                    else:
                        act(dst, srcp, AF.Identity, [(ps, 0)], wr)

        for l in range(NL):
            last = (l == NL - 1)
            tiles_all = [0, 1, 2, 3, 4]
            tiles_out = [1, 2, 3, 4] if last else tiles_all

            with P.phase() as ph:
                v5s = P.buf("v5s", [NV, BW], F32, dma=True, stack=ph)
                tv = P.buf("tv", [128, 4, 8], F32, stack=ph)
                P.dma("sync", v5s.t[:], vec5_d[l], writes=[(v5s, 0)], sem_buf=v5s)
                for j in range(4):
                    ps = P.ps()
                    P.op("tensor", lambda e: e.transpose(out=ps.t[:, 0:NV], in_=v5s.t[0:NV, j * 128:(j + 1) * 128],
                                                         identity=ident_f.t[0:NV, 0:NV]),
                         reads=[(v5s, 0), (ident_f, 0)], writes=[(ps, 0)])
                    P.op("vector", lambda e: e.tensor_copy(out=V.t[:, j, 0:NV], in_=ps.t[:, 0:NV]), reads=[(ps, 0)],
                         writes=[(V, 0)])
                z = tv.t[:, :, 0:2]
                az = tv.t[:, :, 2:4]
                ee = tv.t[:, :, 4:6]
                P.op("vector", lambda e: e.tensor_scalar(out=z, in0=V.t[:, :, 12:14], scalar1=-1.0, scalar2=None,
                                                         op0=ALU.mult), reads=[(V, 0)], writes=[(tv, 0)])
                act(az, z, AF.Abs, [(tv, 0)], [(tv, 0)])
                act(ee, az, AF.Exp, [(tv, 0)], [(tv, 0)], scale=-1.0)
                act(ee, ee, AF.Ln, [(tv, 0), (cst, 0)], [(tv, 0)], bias=ONE)
                P.op("vector", lambda e: e.scalar_tensor_tensor(out=ee, in0=z, scalar=0.0, in1=ee, op0=ALU.max,
                                                                op1=ALU.add), reads=[(tv, 0)], writes=[(tv, 0)])
                P.op("vector", lambda e: e.tensor_scalar(out=V.t[:, :, 53:55], in0=ee, scalar1=-8.0, scalar2=None,
                                                         op0=ALU.mult), reads=[(tv, 0)], writes=[(V, 0)])
                P.op("vector", lambda e: e.tensor_scalar(out=V.t[:, :, 55:57], in0=ee, scalar1=-16.0, scalar2=None,
                                                         op0=ALU.mult), reads=[(tv, 0)], writes=[(V, 0)])
                tt("vector", V.t[:, :, 57], V.t[:, :, 51], V.t[:, :, 52], ALU.mult, [(V, 0)], [(V, 0)])
                P.op("vector", lambda e: e.tensor_scalar(out=V.t[:, :, 58:62], in0=V.t[:, :, 8:12], scalar1=0.5,
                                                         scalar2=None, op0=ALU.mult), reads=[(V, 0)], writes=[(V, 0)])
                P.op("vector", lambda e: e.tensor_scalar(out=V.t[:, :, 62:64], in0=ee, scalar1=-4.0, scalar2=None,
                                                         op0=ALU.mult), reads=[(tv, 0)], writes=[(V, 0)])
                P.dma("gpsimd", dw.t[:], dw_d[l].rearrange("g i j -> i g j"), writes=[(dw, 0)], sem_buf=dw)

            units = []
            for j in range(4):
                units.append(("A", j, [(S_VA, j), (S_ZA, j)]))
            for j in range(4):
                units.append(("B", j, [(S_BC, j), (S_BV, j), (S_BB, j), (S_ZB, j)]))
            units.append(("WO", 0, []))
            for j in range(4):
                units.append(("C1", j, [(S_CG, j), (S_CA, j)]))
            units.append(("C2", 0, [(S_ZC, 0), (S_ZC, 1), (S_ZC, 2), (S_ZC, 3)]))
            for g in range(4):
                units.append(("D", g, [(S_DV, g), (S_ZD, g)]))
            units.append(("WO", 1, []))
            loaded = {}

            def issue(ui):
                kind, j, ws = units[ui]
                loaded[ui] = [load_w(l, s, jj) for (s, jj) in ws]
                if kind == "A":
                    loaded[ui].append(load_gm(l, j))

            issue(0)

            with P.phase() as ph:
                tmp = [P.buf("n_tmp%d" % i, [128, 512], F32, stack=ph) for i in range(2)]
                rb = rms_alloc(ph, "n")
                for it_, tile in enumerate(tiles_all):
                    off, wd = TILES[tile]
                    col = 1 if tile == 0 else 0
                    rs = rms_stats(tile, rb, it_)
                    for k in range(8):
                        tb = tmp[k % 2]
                        tt("gpsimd", tb.t[:, :wd], res.t[:, k, off:off + wd], rs.t[:, :wd], ALU.mult,
                           [(res, hseg(k, tile)), (rs, 0)], [(tb, 0)])
                        act(hT.t[:, k, off:off + wd], tb.t[:, :wd], AF.Identity, [(tb, 0), (MT, l)],
                            [(hT, hseg(k, tile))], scale=MT.t[:, l, 1, k, col:col + 1],
                            bias=MT.t[:, l, 0, k, col:col + 1])

            for ui, (kind, j, ws) in enumerate(units):
                if DBG and l == 0 and ui == int(DBG):
                    with P.phase() as ph:
                        dbb = P.buf("dbgb", [128, TT], F32, dma=True, stack=ph)
                        for kk in range(min(int(DBG), 8)):
                            P.op("vector", lambda e: e.tensor_copy(out=dbb.t[:], in_=cat.t[:, kk, :]),
                                 reads=[(cat, None)], writes=[(dbb, 0)])
                            P.dma("sync", dbg_d[:, kk, :], dbb.t[:], reads=[(dbb, 0)], sem_buf=dbb)
                        P.wait_all("sync", [dbb])
                    return nc
                if l + 1 < NL:
                    mod_consume()
                    nb_ = 2 if ui < 5 else 1
                    mc0 = ui * 2 if ui < 5 else 10 + (ui - 5)
                    for q_ in range(nb_):
                        mod_issue(l + 1, mc0 + q_)
                if ui + 1 < len(units) and kind != "A":
                    issue(ui + 1)
                W = loaded.get(ui, [])

                if kind == "A":
                    Wva, Wza, gm = W
                    PA = 3
                    with P.phase() as ph:
                        va = P.buf("a_va", [128, TT + 3 * PA], BF16, nseg=5, stack=ph)
                        vapad = P.buf("a_vapad", [128, 1], BF16, stack=ph)
                        S = P.buf("a_S", [128, TT], F32, nseg=5, stack=ph)
                        dg = P.buf("a_dg", [128, 8, 128], BF16, stack=ph)
                        xcb = [P.buf("a_xcb%d" % i, [128, 512], BF16, stack=ph) for i in range(2)]
                        T1d = [[P.buf("a_T1%d%d" % (i, p_), [128, 512], F32, stack=ph) for p_ in range(2)]
                               for i in range(2)]
                        T2 = [P.buf("a_T2%d" % i, [128, 512], F32, stack=ph) for i in range(2)]
                        T3d = [[P.buf("a_T3%d%d" % (i, p_), [128, 512], F32, stack=ph) for p_ in range(2)]
                               for i in range(2)]
                        zsl = [P.buf("a_zs%d" % i, [128, 512], BF16, stack=ph) for i in range(2)]
                        stbuf = [[P.buf("a_st%d%d" % (n_, p_), [128, 1], F32, stack=ph) for p_ in range(2)]
                                 for n_ in range(2)]
                        colA = lambda tile: TILES[tile][0] + (PA if tile == 0 else 2 * PA)
                        for i in range(8):
                            P.op("gpsimd", lambda e: e.tensor_scalar(out=dg.t[:, i, :], in0=ident_b.t[:],
                                                                     scalar1=V.t[:, j, i:i + 1], scalar2=1.0,
                                                                     op0=ALU.mult, op1=ALU.mult),
                                 reads=[(ident_b, 0), (V, 0)], writes=[(dg, 0)])
                        for (p0, p1) in ((0, PA), (PA + CTX, 2 * PA + CTX), (2 * PA + TT, 3 * PA + TT)):
                            P.op("gpsimd", lambda e: e.memset(va.t[:, p0:p1], 0.0), writes=[(vapad, 0)])
                        if ui + 1 < len(units):
                            issue(ui + 1)
                        for it, tile in enumerate(tiles_all):
                            off, wd = TILES[tile]
                            ps = P.ps()
                            inproj(Wva, tile, ps)
                            c0 = colA(tile)
                            if it % 2 == 0:
                                act(va.t[:, c0:c0 + wd], ps.t[:, :wd], AF.Identity, [(ps, 0)], [(va, tile)])
                            else:
                                P.op("vector", lambda e: e.tensor_copy(out=va.t[:, c0:c0 + wd], in_=ps.t[:, :wd]),
                                     reads=[(ps, 0)], writes=[(va, tile)])

                        def nbr(tile):
                            if tile == 0:
                                return [0]
                            return [t for t in (tile - 1, tile, tile + 1) if 1 <= t <= 4]

                        FW = [0, 1, 2, 3, 4]
                        BK = [0, 4, 3, 2, 1]
                        prev = [0.0, 0.0]
                        prev_dep = [[], []]
                        ST = {}

                        def front(it):
                            tl = [FW[it], BK[it]]
                            psc = [None, None]
                            psr = [None, None]
                            psi_ = [None, None]
                            for n in range(2):
                                tile = tl[n]
                                off, wd = TILES[tile]
                                c0 = colA(tile)
                                psc[n] = P.ps()
                                for k in range(4):
                                    sh = (c0 - 3 + k) if n == 0 else (c0 + k)
                                    P.op("tensor", lambda e: e.matmul(psc[n].t[:, :wd], lhsT=dg.t[:, n * 4 + k, :],
                                                                      rhs=va.t[:, sh:sh + wd], start=(k == 0), stop=(k == 3)),
                                         reads=[(dg, 0), (vapad, 0), (va, nbr(tile))], writes=[(psc[n], 0)], inc=(k == 3))
                            for n in range(2):
                                wd = TILES[tl[n]][1]
                                P.op("vector", lambda e: e.tensor_copy(out=xcb[n].t[:, :wd], in_=psc[n].t[:, :wd]),
                                     reads=[(psc[n], 0)], writes=[(xcb[n], 0)])
                            for n in range(2):
                                wd = TILES[tl[n]][1]
                                psr[n] = P.ps()
                                P.op("tensor", lambda e: e.matmul(psr[n].t[:, :wd], lhsT=gm.t[:, n * 2, :],
                                                                  rhs=xcb[n].t[:, :wd], start=True, stop=True),
                                     reads=[(gm, 0), (xcb[n], 0)], writes=[(psr[n], 0)])
                                psi_[n] = P.ps()
                                P.op("tensor", lambda e: e.matmul(psi_[n].t[:, :wd], lhsT=gm.t[:, n * 2 + 1, :],
                                                                  rhs=xcb[n].t[:, :wd], start=True, stop=True),
                                     reads=[(gm, 0), (xcb[n], 0)], writes=[(psi_[n], 0)])
                            if it == 0:
                                comb = [(0, 1)] if 0 in tiles_out else []
                            elif it == 3:
                                comb = [(3, 0), (2, 1)]
                            elif it == 4:
                                comb = [(4, 0), (1, 1)]
                            else:
                                comb = []
                            pszl = []
                            for (ctile, cn) in comb:
                                psz = P.ps()
                                inproj(Wza, ctile, psz)
                                pszl.append(psz)
                            ST[it] = (tl, psc, psr, psi_, comb, pszl)

                        def mid(it):
                            tl, psc, psr, psi_, comb, pszl = ST[it]
                            T1 = [T1d[0][it % 2], T1d[1][it % 2]]
                            T3 = [T3d[0][it % 2], T3d[1][it % 2]]
                            for n in range(2):
                                wd = TILES[tl[n]][1]
                                act(T1[n].t[:, :wd], psr[n].t[:, :wd], AF.Tanh, [(psr[n], 0), (V, 0)], [(T1[n], 0)],
                                    scale=0.5, bias=V.t[:, j, 58 + n:59 + n])
                                act(T3[n].t[:, :wd], psi_[n].t[:, :wd], AF.Tanh, [(psi_[n], 0), (V, 0)], [(T3[n], 0)],
                                    scale=0.5, bias=V.t[:, j, 60 + n:61 + n])
                            for n in range(2):
                                wd = TILES[tl[n]][1]
                                act(T1[n].t[:, :wd], T1[n].t[:, :wd], AF.Exp, [(T1[n], 0), (V, 0)], [(T1[n], 0)],
                                    scale=V.t[:, j, 62 + n:63 + n], bias=V.t[:, j, 62 + n:63 + n])
                                tt("gpsimd", T2[n].t[:, :wd], T1[n].t[:, :wd], T1[n].t[:, :wd], ALU.mult, [(T1[n], 0)],
                                   [(T2[n], 0)])
                            for ci, (ctile, cn) in enumerate(comb):
                                wd = TILES[ctile][1]
                                act(zsl[ci].t[:, :wd], pszl[ci].t[:, :wd], AF.Tanh, [(pszl[ci], 0)], [(zsl[ci], 0)],
                                    scale=0.5)
                            for n in range(2):
                                wd = TILES[tl[n]][1]
                                P.op("vector", lambda e: e.scalar_tensor_tensor(out=T3[n].t[:, :wd], in0=T3[n].t[:, :wd],
                                                                                scalar=1.0, in1=psc[n].t[:, :wd],
                                                                                op0=ALU.add, op1=ALU.mult),
                                     reads=[(T3[n], 0), (psc[n], 0)], writes=[(T3[n], 0)])
                            for ci, (ctile, cn) in enumerate(comb):
                                wd = TILES[ctile][1]
                                P.op("vector", lambda e: e.scalar_tensor_tensor(out=zsl[ci].t[:, :wd],
                                                                                in0=zsl[ci].t[:, :wd],
                                                                                scalar=1.0, in1=pszl[ci].t[:, :wd],
                                                                                op0=ALU.add, op1=ALU.mult),
                                     reads=[(zsl[ci], 0), (pszl[ci], 0)], writes=[(zsl[ci], 0)])
                            for n in range(2):
                                wd = TILES[tl[n]][1]
                                act(T2[n].t[:, :wd], T2[n].t[:, :wd], AF.Sqrt, [(T2[n], 0), (cst, 0)], [(T2[n], 0)],
                                    scale=-0.25, bias=QUART)

                        def back(it):
                            tl, psc, psr, psi_, comb, pszl = ST[it]
                            T1 = [T1d[0][it % 2], T1d[1][it % 2]]
                            T3 = [T3d[0][it % 2], T3d[1][it % 2]]
                            for n in range(2):
                                wd = TILES[tl[n]][1]
                                tt("gpsimd", T3[n].t[:, :wd], T3[n].t[:, :wd], T2[n].t[:, :wd], ALU.mult,
                                   [(T3[n], 0), (T2[n], 0)], [(T3[n], 0)])
                            for n in range(2):
                                tile = tl[n]
                                off, wd = TILES[tile]
                                store_n = (n == 0) if it == 0 else (it in (1, 2))
                                if store_n:
                                    o_ap = S.t[:, off:off + wd]
                                    wr = [(S, tile)]
                                else:
                                    o_ap = T3[n].t[:, 0:wd]
                                    wr = [(T3[n], 0)]
                                d0 = T1[n].t[:, 0:wd]
                                d1 = T3[n].t[:, 0:wd]
                                if n == 1:
                                    o_ap, d0, d1 = o_ap[:, ::-1], d0[:, ::-1], d1[:, ::-1]
                                init = prev[n]
                                P.op("vector", lambda e: e.tensor_tensor_scan(out=o_ap, data0=d0, data1=d1, initial=init,
                                                                              op0=ALU.mult, op1=ALU.add),
                                     reads=[(T1[n], 0), (T3[n], 0)] + prev_dep[n], writes=wr)
                                stc = stbuf[n][it % 2]
                                col = (wd - 1) if n == 0 else 0
                                if store_n:
                                    src = S.t[:, off + col:off + col + 1]
                                else:
                                    src = T3[n].t[:, col:col + 1]
                                P.op("vector", lambda e: e.tensor_copy(out=stc.t[:, 0:1], in_=src), reads=wr,
                                     writes=[(stc, 0)])
                                prev[n] = stc.t[:, 0:1]
                                prev_dep[n] = [(stc, 0)]
                            for ci, (ctile, cn) in enumerate(comb):
                                off, wd = TILES[ctile]
                                tt("gpsimd", T3[cn].t[:, :wd], T3[cn].t[:, :wd], S.t[:, off:off + wd], ALU.add,
                                   [(T3[cn], 0), (S, ctile)], [(T3[cn], 0)])
                                P.op("vector", lambda e: e.scalar_tensor_tensor(out=cat.t[:, j, off:off + wd],
                                                                                in0=T3[cn].t[:, :wd], scalar=0.5,
                                                                                in1=zsl[ci].t[:, :wd], op0=ALU.mult,
                                                                                op1=ALU.mult),
                                     reads=[(T3[cn], 0), (zsl[ci], 0)], writes=[(cat, hseg(j, ctile))])

                        front(0)
                        for it in range(5):
                            mid(it)
                            if it + 1 < 5:
                                front(it + 1)
                            back(it)

                elif kind == "B":
                    Wbc, Wbv, Wbb, Wzb = W
                    with P.phase() as ph:
                        prod = P.buf("b_prod", [128, TT + 3], BF16, nseg=5, stack=ph)
                        dg = P.buf("b_dg", [128, 3, 128], BF16, stack=ph)
                        bcs = [P.buf("b_bcs%d" % i, [128, 512], F32, stack=ph) for i in range(2)]
                        zs = [P.buf("b_zs%d" % i, [128, 512], F32, stack=ph) for i in range(2)]
                        gb = [P.buf("b_g%d" % i, [128, 512], F32, stack=ph) for i in range(2)]
                        colB = lambda tile: TILES[tile][0] + (1 if tile == 0 else 2)
                        P.op("gpsimd", lambda e: e.memset(prod.t[:], 0.0), writes=[(prod, None)])
                        for i in range(3):
                            P.op("gpsimd", lambda e: e.tensor_scalar(out=dg.t[:, i, :], in0=ident_b.t[:],
                                                                     scalar1=V.t[:, j, 14 + i:15 + i], scalar2=1.0,
                                                                     op0=ALU.mult, op1=ALU.mult),
                                 reads=[(ident_b, 0), (V, 0)], writes=[(dg, 0)])
                        for it, tile in enumerate(tiles_out):
                            off, wd = TILES[tile]
                            c0 = colB(tile)
                            ps1 = P.ps()
                            inproj(Wbc, tile, ps1)
                            ps2 = P.ps()
                            inproj(Wbv, tile, ps2)
                            b_ = bcs[it % 2]
                            act(b_.t[:, :wd], ps1.t[:, :wd], AF.Identity, [(ps1, 0)], [(b_, 0)])
                            tt("vector", prod.t[:, c0:c0 + wd], ps2.t[:, :wd], b_.t[:, :wd], ALU.mult,
                               [(ps2, 0), (b_, 0)], [(prod, tile)])

                        def nbrB(tile):
                            if tile == 0:
                                return [0]
                            return [t for t in (tile - 1, tile, tile + 1) if 1 <= t <= 4]

                        for it, tile in enumerate(tiles_out):
                            off, wd = TILES[tile]
                            c0 = colB(tile)
                            ps3 = P.ps()
                            for k in range(3):
                                sh = c0 - 1 + k
                                P.op("tensor", lambda e: e.matmul(ps3.t[:, :wd], lhsT=dg.t[:, k, :],
                                                                  rhs=prod.t[:, sh:sh + wd], start=(k == 0), stop=(k == 2)),
                                     reads=[(dg, 0), (prod, nbrB(tile))], writes=[(ps3, 0)], inc=(k == 2))
                            ps4 = P.ps()
                            inproj(Wbb, tile, ps4)
                            ps5 = P.ps()
                            inproj(Wzb, tile, ps5)
                            z_ = zs[it % 2]
                            g_ = gb[it % 2]
                            act(z_.t[:, :wd], ps5.t[:, :wd], AF.Silu, [(ps5, 0)], [(z_, 0)])
                            tt("vector", g_.t[:, :wd], ps4.t[:, :wd], z_.t[:, :wd], ALU.mult, [(ps4, 0), (z_, 0)],
                               [(g_, 0)])
                            tt("vector", cat.t[:, 4 + j, off:off + wd], ps3.t[:, :wd], g_.t[:, :wd], ALU.mult,
                               [(ps3, 0), (g_, 0)], [(cat, hseg(4 + j, tile))])

                elif kind == "C1":
                    Wcg, Wca = W
                    PC = 15
                    with P.phase() as ph:
                        glu = P.buf("c_glu", [128, TT + 3 * PC], BF16, nseg=5, stack=ph)
                        dg = P.buf("c_dg", [128, 31, 128], BF16, stack=ph)
                        sg = [P.buf("c_sg%d" % i, [128, 512], F32, stack=ph) for i in range(2)]
                        colC = lambda tile: TILES[tile][0] + (PC if tile == 0 else 2 * PC)
                        P.op("gpsimd", lambda e: e.memset(glu.t[:], 0.0), writes=[(glu, None)])
                        for i in range(6, 31):
                            P.op("gpsimd", lambda e: e.tensor_scalar(out=dg.t[:, i, :], in0=ident_b.t[:],
                                                                     scalar1=V.t[:, j, 17 + i:18 + i], scalar2=1.0,
                                                                     op0=ALU.mult, op1=ALU.mult),
                                 reads=[(ident_b, 0), (V, 0)], writes=[(dg, 0)])
                        for it, tile in enumerate(tiles_out):
                            off, wd = TILES[tile]
                            c0 = colC(tile)
                            ps1 = P.ps()
                            inproj(Wcg, tile, ps1)
                            ps2 = P.ps()
                            inproj(Wca, tile, ps2)
                            s_ = sg[it % 2]
                            act(s_.t[:, :wd], ps1.t[:, :wd], AF.Sigmoid, [(ps1, 0)], [(s_, 0)])
                            tt("vector", glu.t[:, c0:c0 + wd], ps2.t[:, :wd], s_.t[:, :wd], ALU.mult,
                               [(ps2, 0), (s_, 0)], [(glu, tile)])

                        def nbrC(tile):
                            if tile == 0:
                                return [0]
                            return [t for t in (tile - 1, tile, tile + 1) if 1 <= t <= 4]

                        NDV = 6
                        acc = [P.buf("c_acc%d" % i, [128, 512], F32, stack=ph) for i in range(2)]
                        for it, tile in enumerate(tiles_out):
                            off, wd = TILES[tile]
                            c0 = colC(tile)
                            ps3 = P.ps()
                            for k in range(NDV, 31):
                                sh = c0 - 15 + k
                                P.op("tensor", lambda e: e.matmul(ps3.t[:, :wd], lhsT=dg.t[:, k, :],
                                                                  rhs=glu.t[:, sh:sh + wd], start=(k == NDV), stop=(k == 30)),
                                     reads=[(dg, 0), (glu, nbrC(tile))], writes=[(ps3, 0)], inc=(k == 30))
                            ac = acc[it % 2]
                            for k in range(NDV):
                                sh = c0 - 15 + k
                                if k == 0:
                                    P.op("vector", lambda e: e.tensor_scalar(out=ac.t[:, :wd], in0=glu.t[:, sh:sh + wd],
                                                                             scalar1=V.t[:, j, 17 + k:18 + k], scalar2=None,
                                                                             op0=ALU.mult),
                                         reads=[(glu, nbrC(tile)), (V, 0)], writes=[(ac, 0)])
                                else:
                                    P.op("vector", lambda e: e.scalar_tensor_tensor(out=ac.t[:, :wd],
                                                                                    in0=glu.t[:, sh:sh + wd],
                                                                                    scalar=V.t[:, j, 17 + k:18 + k],
                                                                                    in1=ac.t[:, :wd], op0=ALU.mult,
                                                                                    op1=ALU.add),
                                         reads=[(glu, nbrC(tile)), (V, 0), (ac, 0)], writes=[(ac, 0)])
                            tt("vector", cat.t[:, 4 + j, off:off + wd], ps3.t[:, :wd], ac.t[:, :wd], ALU.add,
                               [(ps3, 0), (ac, 0)], [(cat, hseg(4 + j, tile))])

                elif kind == "C2":
                    Wzc = W
                    with P.phase() as ph:
                        cpw = P.buf("c_pw", [128, 4, BW], BF16, dma=True, stack=ph)
                        P.dma("gpsimd", cpw.t[:], kp(cpw_d[l]), writes=[(cpw, 0)], sem_buf=cpw)
                        usq = [P.buf("c_usq%d" % i, [128, 512], BF16, stack=ph) for i in range(2)]
                        mean = P.buf("c_mean", [128, 512], F32, stack=ph)
                        m2 = P.buf("c_m2", [128, 512], F32, stack=ph)
                        rstd = P.buf("c_rstd", [128, 512], F32, stack=ph)
                        tb = [P.buf("c_t%d" % i, [128, 512], F32, stack=ph) for i in range(2)]
                        un = P.buf("c_un", [128, 4, 512], BF16, nseg=4, stack=ph)
                        zs = [P.buf("c_zs%d" % i, [128, 512], F32, stack=ph) for i in range(2)]
                        for tile in tiles_out:
                            off, wd = TILES[tile]
                            psm = P.ps()
                            psq = P.ps()
                            for jj in range(4):
                                u_ap = cat.t[:, 4 + jj, off:off + wd]
                                useg = (cat, hseg(4 + jj, tile))
                                P.op("tensor", lambda e: e.matmul(psm.t[:, :wd], lhsT=ones512.t[:], rhs=u_ap,
                                                                  start=(jj == 0), stop=(jj == 3)),
                                     reads=[(ones512, 0), useg], writes=[(psm, 0)], inc=(jj == 3))
                                q_ = usq[jj % 2]
                                act(q_.t[:, :wd], u_ap, AF.Square, [useg], [(q_, 0)])
                                P.op("tensor", lambda e: e.matmul(psq.t[:, :wd], lhsT=ones512.t[:], rhs=q_.t[:, :wd],
                                                                  start=(jj == 0), stop=(jj == 3)),
                                     reads=[(ones512, 0), (q_, 0)], writes=[(psq, 0)])
                            act(mean.t[:, :wd], psm.t[:, :wd], AF.Identity, [(psm, 0)], [(mean, 0)])
                            tt("gpsimd", m2.t[:, :wd], mean.t[:, :wd], mean.t[:, :wd], ALU.mult, [(mean, 0)], [(m2, 0)])
                            tt("vector", m2.t[:, :wd], psq.t[:, :wd], m2.t[:, :wd], ALU.subtract, [(psq, 0), (m2, 0)],
                               [(m2, 0)])
                            act(m2.t[:, :wd], m2.t[:, :wd], AF.Sqrt, [(m2, 0), (cst, 0)], [(m2, 0)], bias=EPS5)
                            P.op("vector", lambda e: e.reciprocal(out=rstd.t[:, :wd], in_=m2.t[:, :wd]),
                                 reads=[(m2, 0)], writes=[(rstd, 0)])
                            for jj in range(4):
                                t_ = tb[jj % 2]
                                tt("gpsimd", t_.t[:, :wd], cat.t[:, 4 + jj, off:off + wd], mean.t[:, :wd], ALU.subtract,
                                   [(cat, hseg(4 + jj, tile)), (mean, 0)], [(t_, 0)])
                                tt("vector", t_.t[:, :wd], t_.t[:, :wd], rstd.t[:, :wd], ALU.mult, [(t_, 0), (rstd, 0)],
                                   [(t_, 0)])
                                act(un.t[:, jj, :wd], t_.t[:, :wd], AF.Silu, [(t_, 0), (V, 0)], [(un, jj)],
                                    scale=V.t[:, jj, 48:49], bias=V.t[:, jj, 49:50])
                            for m in range(4):
                                psp = P.ps()
                                for kc in range(4):
                                    P.op("tensor", lambda e: e.matmul(psp.t[:, :wd], lhsT=cpw.t[:, kc, m * 128:(m + 1) * 128],
                                                                      rhs=un.t[:, kc, :wd], start=(kc == 0), stop=(kc == 3)),
                                         reads=[(cpw, 0), (un, kc)], writes=[(psp, 0)], inc=(kc == 3))
                                psz = P.ps()
                                inproj(Wzc[m], tile, psz)
                                z_ = zs[m % 2]
                                act(z_.t[:, :wd], psz.t[:, :wd], AF.Silu, [(psz, 0)], [(z_, 0)])
                                P.op("vector", lambda e: e.scalar_tensor_tensor(out=cat.t[:, m, off:off + wd],
                                                                                in0=psp.t[:, :wd],
                                                                                scalar=V.t[:, m, 50:51], in1=z_.t[:, :wd],
                                                                                op0=ALU.add, op1=ALU.mult),
                                     reads=[(psp, 0), (z_, 0), (V, 0)], writes=[(cat, hseg(m, tile))])

                elif kind == "D":
                    g = j
                    Wdv, Wzd = W
                    with P.phase() as ph:
                        dvtm = P.buf("d_dvtm", [128, 18, 128], BF16, nseg=18, stack=ph)
                        pmg = P.buf("d_pm", [128, NMG, 128], BF16, dma=True, stack=ph)
                        pT = [P.buf("d_pT%d" % i, [128, 512], BF16, stack=ph) for i in range(2)]
                        zs = [P.buf("d_zs%d" % i, [128, 512], F32, stack=ph) for i in range(2)]
                        tb = [P.buf("d_t%d" % i, [128, 512], F32, stack=ph) for i in range(2)]
                        ng = GCNT[g]
                        icc = P.buf("d_icc", [128, 64], F32, dma=True, stack=ph)
                        tabc = P.buf("d_tabc", [128, CTX], F32, dma=True, stack=ph)
                        P.dma("sync", icc.t[:], icc_d[:, g * 64:(g + 1) * 64], writes=[(icc, 0)], sem_buf=icc)
                        P.dma("sync", tabc.t[:], tabc_d[:, g * CTX:(g + 1) * CTX], writes=[(tabc, 0)], sem_buf=tabc)
                        P.dma("sync", pmg.t[:, 0:ng, :], pm_d[:, GOFF[g]:GOFF[g] + ng, :], writes=[(pmg, 0)], sem_buf=pmg)
                        t128s = list(range(0 if not last else 2, 18))
                        grp = []
                        cur = []
                        for t1 in t128s:
                            cur.append(t1)
                            if len(cur) == 4 or t1 == t128s[-1] or t1 == 1:
                                grp.append(cur)
                                cur = []
                        for gi, lst in enumerate(grp):
                            ps = P.ps()
                            for q, t1 in enumerate(lst):
                                tile = 0 if t1 < 2 else 1 + (t1 - 2) // 4
                                for k in range(8):
                                    P.op("tensor", lambda e: e.matmul(ps.t[:, q * 128:(q + 1) * 128],
                                                                      lhsT=hT.t[:, k, t1 * 128:(t1 + 1) * 128],
                                                                      rhs=Wdv.t[:, k, :], start=(k == 0), stop=(k == 7)),
                                         reads=[(Wdv, 0), (hT, hseg(k, tile))], writes=[(ps, 0)],
                                         inc=(k == 7 and q == len(lst) - 1))
                            n_ = len(lst)
                            dst = dvtm.t[:, lst[0]:lst[0] + n_, :]
                            srcp = ps.t[:, 0:n_ * 128].rearrange("p (q c) -> p q c", q=n_)
                            if gi % 2 == 0:
                                act(dst, srcp, AF.Identity, [(ps, 0)], [(dvtm, lst)])
                            else:
                                P.op("vector", lambda e: e.tensor_copy(out=dst, in_=srcp), reads=[(ps, 0)],
                                     writes=[(dvtm, lst)])
                        for it, tile in enumerate(tiles_out):
                            off, wd = TILES[tile]
                            n128 = wd // 128
                            t0 = off // 128
                            ps = P.ps()
                            nmm = sum(len(PLAN[g][t0 + q]) for q in range(n128))
                            cnt = 0
                            for q in range(n128):
                                lst = PLAN[g][t0 + q]
                                for idx, (i_, m_) in enumerate(lst):
                                    cnt += 1
                                    ml = m_ - GOFF[g]
                                    P.op("tensor", lambda e: e.matmul(ps.t[:, q * 128:(q + 1) * 128],
                                                                      lhsT=dvtm.t[:, i_, :], rhs=pmg.t[:, ml, :],
                                                                      start=(idx == 0), stop=(idx == len(lst) - 1)),
                                         reads=[(dvtm, i_), (pmg, 0)], writes=[(ps, 0)], inc=(cnt == nmm))
                            p_ = pT[it % 2]
                            if tile == 0:
                                tt("vector", p_.t[:, :wd], ps.t[:, :wd], tabc.t[:, :], ALU.mult,
                                   [(ps, 0), (tabc, 0)], [(p_, 0)])
                            else:
                                rbase = (off - CTX) // GRID_W
                                a = 0
                                while a < 8:
                                    b = a
                                    while b + 1 < 8 and IRC[g][rbase + b + 1] == IRC[g][rbase + a]:
                                        b += 1
                                    nr = b - a + 1
                                    sc_ = float(IRC[g][rbase + a])
                                    o_ap = p_.t[:, a * 64:(b + 1) * 64].rearrange("p (r c) -> p r c", c=64)
                                    i_ap = ps.t[:, a * 64:(b + 1) * 64].rearrange("p (r c) -> p r c", c=64)
                                    t_ap = icc.t[:, :].unsqueeze(1).broadcast_to([128, nr, 64])
                                    P.op("vector", lambda e: e.scalar_tensor_tensor(out=o_ap, in0=i_ap, scalar=sc_,
                                                                                    in1=t_ap, op0=ALU.mult, op1=ALU.mult),
                                         reads=[(ps, 0), (icc, 0)], writes=[(p_, 0)])
                                    a = b + 1
                            ps2 = P.ps()
                            P.op("tensor", lambda e: e.matmul(ps2.t[:, :wd], lhsT=dw.t[:, g, :], rhs=p_.t[:, :wd],
                                                              start=True, stop=True),
                                 reads=[(dw, 0), (p_, 0)], writes=[(ps2, 0)])
                            ps3 = P.ps()
                            inproj(Wzd, tile, ps3)
                            z_ = zs[it % 2]
                            t_ = tb[it % 2]
                            act(z_.t[:, :wd], ps3.t[:, :wd], AF.Silu, [(ps3, 0)], [(z_, 0)])
                            act(t_.t[:, :wd], ps2.t[:, :wd], AF.Identity, [(ps2, 0), (V, 0)], [(t_, 0)],
                                scale=V.t[:, g, 52:53], bias=V.t[:, g, 57:58])
                            tt("gpsimd", cat.t[:, 4 + g, off:off + wd], t_.t[:, :wd], z_.t[:, :wd], ALU.mult,
                               [(t_, 0), (z_, 0)], [(cat, hseg(4 + g, tile))])

                elif kind == "WO":
                    half = j

                    def load_wo(m):
                        b = woring[woi[0] % 2]
                        woi[0] += 1
                        P.dma("gpsimd", b.t[:], kp(wout_d[l, half * 1024:(half + 1) * 1024, m * 128:(m + 1) * 128]),
                              writes=[(b, 0)], sem_buf=b)
                        return b

                    nxt = load_wo(0)
                    for m in range(8):
                        wo = nxt
                        if m + 1 < 8:
                            nxt = load_wo(m + 1)
                        for tile in tiles_out:
                            off, wd = TILES[tile]
                            col = 1 if tile == 0 else 0
                            ps = P.ps()
                            for k in range(8):
                                P.op("tensor", lambda e: e.matmul(ps.t[:, :wd], lhsT=wo.t[:, k, :],
                                                                  rhs=cat.t[:, k, off:off + wd], start=(k == 0), stop=(k == 7)),
                                     reads=[(wo, 0), (cat, hseg(k, tile))], writes=[(ps, 0)], inc=(k == 7))
                            P.op("vector", lambda e: e.scalar_tensor_tensor(out=res.t[:, m, off:off + wd], in0=ps.t[:, :wd],
                                                                            scalar=MT.t[:, l, 2, m, col:col + 1],
                                                                            in1=res.t[:, m, off:off + wd],
                                                                            op0=ALU.mult, op1=ALU.add),
                                 reads=[(ps, 0), (MT, l), (res, hseg(m, tile))], writes=[(res, hseg(m, tile))])

            if l + 1 < NL:
                mod_consume()
                mod_finish(l + 1)

        with P.phase() as ph:
            tmp = [P.buf("f_tmp%d" % i, [128, 512], F32, stack=ph) for i in range(2)]
            ost = [P.buf("f_ost%d" % i, [128, D], F32, dma=True, stack=ph) for i in range(2)]
            oi = 0
            rb = rms_alloc(ph, "f")
            for it_, tile in enumerate([1, 2, 3, 4]):
                off, wd = TILES[tile]
                rs = rms_stats(tile, rb, it_)
                for k in range(8):
                    tb = tmp[k % 2]
                    tt("gpsimd", tb.t[:, :wd], res.t[:, k, off:off + wd], rs.t[:, :wd], ALU.mult,
                       [(res, hseg(k, tile)), (rs, 0)], [(tb, 0)])
                    act(res.t[:, k, off:off + wd], tb.t[:, :wd], AF.Identity, [(tb, 0), (VD, 0)],
                        [(res, hseg(k, tile))], scale=VD.t[:, k, 2:3])
                for q in range(4):
                    o_ = ost[oi % 2]
                    oi += 1
                    c0 = off + q * 128
                    for half in range(2):
                        ps = P.ps()
                        for kk in range(4):
                            k = half * 4 + kk
                            P.op("tensor", lambda e: e.transpose(out=ps.t[:, kk * 128:(kk + 1) * 128],
                                                                 in_=res.t[:, k, c0:c0 + 128], identity=ident_f.t[:]),
                                 reads=[(res, hseg(k, tile)), (ident_f, 0)], writes=[(ps, 0)], inc=(kk == 3))
                        if half == 0:
                            P.op("vector", lambda e: e.tensor_copy(out=o_.t[:, 0:512], in_=ps.t[:, 0:512]),
                                 reads=[(ps, 0)], writes=[(o_, 0)])
                        else:
                            act(o_.t[:, 512:1024], ps.t[:, 0:512], AF.Identity, [(ps, 0)], [(o_, 0)])
                    r0 = (tile - 1) * 512 + q * 128
                    P.dma("sync", out_d[r0:r0 + 128, :], o_.t[:], reads=[(o_, 0)], sem_buf=o_)
            P.wait_all("sync", ost)
    return nc


def prep_inputs(inp, NL=DEPTH):
    f = lambda a: np.ascontiguousarray(np.asarray(a, dtype=np.float32))
    x = f(inp["x"])
    B = x.shape[0]
    c = f(inp["c"])
    ctx = f(inp["ctx"])
    c_ctx = f(inp["c_ctx"])
    mod_b = f(inp["mod_b"])
    norm_g = f(inp["norm_g"])
    final_g = f(inp["final_g"])
    vec5 = np.zeros((DEPTH, NV, BW), np.float32)
    gm = np.zeros((DEPTH, 4, 128, 4, 128), np.float32)
    a_conv, a_br, a_bi, a_lam = f(inp["a_conv"]), f(inp["a_br"]), f(inp["a_bi"]), f(inp["a_lam"])
    b_conv, c_conv = f(inp["b_conv"]), f(inp["c_conv"])
    a_wr, a_wi = f(inp["a_wr"]), f(inp["a_wi"])
    nl = a_conv.shape[0]
    for l in range(nl):
        vec5[l, 0:4] = a_conv[l, 0]
        vec5[l, 4:8] = a_conv[l, 1]
        vec5[l, 8:10] = a_br[l]
        vec5[l, 10:12] = a_bi[l]
        vec5[l, 12:14] = a_lam[l]
        vec5[l, 14:17] = b_conv[l]
        vec5[l, 17:48] = c_conv[l]
        vec5[l, 48] = f(inp["c_ln_g"])[l]
        vec5[l, 49] = f(inp["c_ln_b"])[l]
        vec5[l, 50] = f(inp["c_pw_b"])[l]
        vec5[l, 51] = f(inp["d_b"])[l].reshape(-1)
        vec5[l, 52] = f(inp["d_scale"])[l]
        for j in range(4):
            for n in range(2):
                for gi, wsrc in enumerate((a_wr, a_wi)):
                    for hh in range(2):
                        gm[l, j, hh * 64:(hh + 1) * 64, n * 2 + gi, hh * 64:(hh + 1) * 64] = wsrc[l, n, 2 * j + hh]
    shared = {
        "vec5": vec5, "gm": gm,
        "mod_w": f(inp["mod_w"]), "w_in": f(inp["w_in"]), "w_out": f(inp["w_out"]),
        "c_pw": f(inp["c_pw"]), "d_w": f(inp["d_w"]),
        "pmats": PM,
        "icc": np.ascontiguousarray(np.broadcast_to(ICC.reshape(1, -1), (128, 256))).astype(np.float32),
        "tabc": np.ascontiguousarray(np.broadcast_to(TABC.reshape(1, -1), (128, 4 * CTX))).astype(np.float32),
    }
    for k_ in ("mod_w", "w_in", "w_out", "c_pw", "d_w"):
        a = shared[k_]
        if a.shape[0] < DEPTH:
            pad = np.zeros((DEPTH - a.shape[0],) + a.shape[1:], np.float32)
            shared[k_] = np.concatenate([a, pad], 0)
    maps = []
    for b in range(B):
        vecD = np.zeros((3 + 4 * DEPTH, D), np.float32)
        vecD[0] = c[b]
        vecD[1] = c_ctx
        vecD[2] = final_g
        for l in range(nl):
            vecD[3 + 4 * l] = norm_g[l]
            vecD[3 + 4 * l + 1:3 + 4 * l + 4] = mod_b[l].reshape(3, D)
        m = dict(shared)
        m["x"] = x[b]
        m["ctx"] = ctx[b]
        m["vecD"] = vecD
        maps.append(m)
    return maps


_NC_CACHE = {}


def kernel(**inputs):
    maps = prep_inputs(inputs)
    if "nc" not in _NC_CACHE:
        _NC_CACHE["nc"] = build(DEPTH)
    nc = _NC_CACHE["nc"]
    res = run_bass_kernel_spmd(nc, maps, core_ids=list(range(8)))
    out = np.stack([np.asarray(r["out"], dtype=np.float32) for r in res.results], 0)
    return out
```

```python
from contextlib import ExitStack, contextmanager
import numpy as np
import ml_dtypes
import concourse.bass as bass
import concourse.mybir as mybir
from concourse.bass_utils import run_bass_kernel_spmd

F32 = mybir.dt.float32
BF16 = mybir.dt.bfloat16
AF = mybir.ActivationFunctionType
ALU = mybir.AluOpType

DEPTH = 4
D = 1024
SEQ = 2048
CTX = 256
TT = CTX + SEQ
BW = 512
IN_W = 5632
GRID_W = 64
ROWS = SEQ // GRID_W
WINS = (2, 4, 8, 16)
NV = 53
NVX = 64
ENGS = ["tensor", "vector", "scalar", "gpsimd", "sync"]
TILES = [(0, 256), (256, 512), (768, 512), (1280, 512), (1792, 512)]
S_VA, S_ZA, S_BB, S_BC, S_BV, S_ZB, S_CA, S_CG, S_ZC, S_DV, S_ZD = range(11)


def _pool_plan():
    mats = []
    key2idx = {}
    plan = []
    goff = []
    gcnt = []
    icc = np.zeros((4, 64), np.float32)
    irc = np.zeros((4, ROWS), np.float64)
    tabc = np.zeros((4, CTX), np.float32)
    for g, w in enumerate(WINS):
        h = w // 2
        start = len(mats)
        key2idx = {}
        pl = {}
        pos = np.arange(CTX)
        lo = np.clip(pos - h, 0, CTX)
        hi = np.clip(pos + h, 0, CTX)
        cnt = (hi - lo)
        tabc[g] = 1.0 / cnt
        M = np.zeros((CTX, CTX), np.float32)
        for tp in range(CTX):
            M[lo[tp]:hi[tp], tp] = 1.0
            M[tp, tp] -= cnt[tp]
        for j in range(2):
            lst = []
            for i in range(2):
                blk = M[i * 128:(i + 1) * 128, j * 128:(j + 1) * 128]
                if not blk.any():
                    continue
                k = blk.tobytes()
                if k not in key2idx:
                    key2idx[k] = len(mats)
                    mats.append(blk.copy())
                lst.append((i, key2idx[k]))
            pl[j] = lst
        r = np.arange(ROWS)
        r0 = np.clip(r - h, 0, ROWS)
        r1 = np.clip(r + h, 0, ROWS)
        c = np.arange(GRID_W)
        c0 = np.clip(c - h, 0, GRID_W)
        c1 = np.clip(c + h, 0, GRID_W)
        icc[g] = 1.0 / (c1 - c0)
        irc[g] = 1.0 / (r1 - r0)
        colbox = np.zeros((GRID_W, GRID_W), np.float32)
        for cp in range(GRID_W):
            colbox[c0[cp]:c1[cp], cp] = 1.0
        for j in range(ROWS // 2):
            lst = []
            for i in range(ROWS // 2):
                blk = np.zeros((128, 128), np.float32)
                for a in range(2):
                    rr = 2 * i + a
                    for b in range(2):
                        rp = 2 * j + b
                        if r0[rp] <= rr < r1[rp]:
                            blk[a * 64:(a + 1) * 64, b * 64:(b + 1) * 64] = colbox
                if i == j:
                    for b in range(2):
                        rp = 2 * j + b
                        cn = (r1[rp] - r0[rp]) * (c1 - c0)
                        blk[b * 64 + np.arange(64), b * 64 + np.arange(64)] -= cn
                if not blk.any():
                    continue
                k = blk.tobytes()
                if k not in key2idx:
                    key2idx[k] = len(mats)
                    mats.append(blk.copy())
                lst.append((i + 2, key2idx[k]))
            pl[j + 2] = lst
        plan.append(pl)
        goff.append(start)
        gcnt.append(len(mats) - start)
    pm = np.stack(mats, 0)
    pm = np.ascontiguousarray(pm.transpose(1, 0, 2)).astype(ml_dtypes.bfloat16)
    return pm, plan, goff, gcnt, icc, irc, tabc


PM, PLAN, GOFF, GCNT, ICC, IRC, TABC = _pool_plan()
NM = PM.shape[1]
NMG = max(GCNT)


class Buf:
    def __init__(self, name, t, nseg, dma_sem, init_rd):
        self.name = name
        self.t = t
        self.nseg = nseg
        self.lw = [None] * nseg
        self.rd = [list(init_rd) for _ in range(nseg)]
        self.dma_sem = dma_sem


class Prog:
    def __init__(self, nc, stack):
        self.nc = nc
        self.stack = stack
        self.sems = {}
        self.count = {}
        self.seen = {e: {} for e in ENGS}
        self.pending = {}
        self.phase_bufs = None
        self.psb = []
        self.psi = 0
        for e in ENGS:
            self.new_sem("E_" + e)

    def new_sem(self, key):
        if key not in self.sems:
            self.sems[key] = self.stack.enter_context(self.nc.semaphore(key))
            self.count[key] = 0
        return key

    def buf(self, name, shape, dtype, nseg=1, dma=False, psum=False, stack=None):
        st = stack if stack is not None else self.stack
        self.uid = getattr(self, "uid", 0) + 1
        tname = "s%d_%s" % (self.uid, name)
        if psum:
            t = st.enter_context(self.nc.psum_tensor(tname, shape, dtype))
        else:
            t = st.enter_context(self.nc.sbuf_tensor(tname, shape, dtype))
        ds = self.new_sem("D_" + name) if dma else None
        init = list(self.pending.items()) if stack is not None else []
        b = Buf(name, t, nseg, ds, init)
        if stack is not None and self.phase_bufs is not None:
            self.phase_bufs.append(b)
        return b

    @contextmanager
    def phase(self):
        old = self.phase_bufs
        self.phase_bufs = []
        with ExitStack() as st:
            yield st
            for b in self.phase_bufs:
                for s in range(b.nseg):
                    for tok in [b.lw[s]] + b.rd[s]:
                        if tok is None:
                            continue
                        k, v = tok
                        if self.pending.get(k, 0) < v:
                            self.pending[k] = v
        self.phase_bufs = old

    def ps(self):
        b = self.psb[self.psi % len(self.psb)]
        self.psi += 1
        return b

    def _expand(self, lst):
        out = []
        for b, s in lst:
            if s is None:
                s = range(b.nseg)
            if isinstance(s, int):
                out.append((b, s))
            else:
                for x in s:
                    out.append((b, x))
        return out

    def _need(self, eng, toks):
        best = {}
        for tok in toks:
            if tok is None:
                continue
            k, v = tok
            if eng == "tensor" and k == "E_tensor":
                continue
            if best.get(k, 0) < v:
                best[k] = v
        e = getattr(self.nc, eng)
        for k, v in best.items():
            if self.seen[eng].get(k, 0) >= v:
                continue
            self.seen[eng][k] = v
            e.wait_ge(self.sems[k], v)

    def _deps(self, eng, reads, writes):
        toks = []
        rl = self._expand(reads)
        wl = self._expand(writes)
        for b, s in rl:
            toks.append(b.lw[s])
        for b, s in wl:
            toks.append(b.lw[s])
            toks.extend(b.rd[s])
        self._need(eng, toks)
        return rl, wl

    def _record(self, tok, rl, wl):
        for b, s in rl:
            b.rd[s].append(tok)
        for b, s in wl:
            b.lw[s] = tok
            b.rd[s] = []

    def op(self, eng, fn, reads=(), writes=(), inc=True):
        rl, wl = self._deps(eng, reads, writes)
        key = "E_" + eng
        ins = fn(getattr(self.nc, eng))
        if inc:
            self.count[key] += 1
            ins.then_inc(self.sems[key], 1)
            v = self.count[key]
        else:
            v = self.count[key] + 1
        self._record((key, v), rl, wl)

    def dma(self, eng, out_ap, in_ap, reads=(), writes=(), sem_buf=None, **kw):
        rl, wl = self._deps(eng, reads, writes)
        key = sem_buf.dma_sem
        self.count[key] += 16
        getattr(self.nc, eng).dma_start(out=out_ap, in_=in_ap, **kw).then_inc(self.sems[key], 16)
        self._record((key, self.count[key]), rl, wl)

    def wait_all(self, eng, bufs):
        toks = []
        for b in bufs:
            for s in range(b.nseg):
                toks.append(b.lw[s])
                toks.extend(b.rd[s])
        self._need(eng, toks)


def hseg(k, tile):
    return k * 5 + tile


def build(NL=DEPTH):
    nc = bass.Bass("TRN2", target_bir_lowering=False, dynamic_dma_scratch_size=8192)
    dt = lambda n, s, d=F32, k="ExternalInput": nc.dram_tensor(n, s, d, kind=k).ap()
    x_d = dt("x", [SEQ, D])
    ctx_d = dt("ctx", [CTX, D])
    vecD_d = dt("vecD", [3 + 4 * DEPTH, D])
    vec5_d = dt("vec5", [DEPTH, NV, BW])
    modw_d = dt("mod_w", [DEPTH, D, 3 * D])
    win_d = dt("w_in", [DEPTH, D, IN_W])
    wout_d = dt("w_out", [DEPTH, 2048, D])
    cpw_d = dt("c_pw", [DEPTH, BW, BW])
    dw_d = dt("d_w", [DEPTH, 4, 128, 128])
    gm_d = dt("gm", [DEPTH, 4, 128, 4, 128])
    pm_d = dt("pmats", [128, NM, 128], BF16)
    icc_d = dt("icc", [128, 4 * 64])
    tabc_d = dt("tabc", [128, 4 * CTX])
    out_d = dt("out", [SEQ, D], F32, "ExternalOutput")
    import os as _os
    DBG = _os.environ.get("KDBG")
    dbg_d = dt("dbg", [128, 8, TT], F32, "ExternalOutput") if DBG else None

    with ExitStack() as st:
        P = Prog(nc, st)
        for i in range(8):
            P.psb.append(P.buf("psb%d" % i, [128, 512], F32, psum=True))

        res = P.buf("res", [128, 8, TT], F32, nseg=40)
        hT = P.buf("hT", [128, 8, TT], BF16, nseg=40)
        cat = P.buf("cat", [128, 8, TT], BF16, nseg=40)
        ident_f = P.buf("ident_f", [128, 128], F32)
        ident_b = P.buf("ident_b", [128, 128], BF16)
        ones1024 = P.buf("ones1024", [128, 128], BF16)
        ones512 = P.buf("ones512", [128, 128], BF16)
        cst = P.buf("cst", [128, 4], F32)
        VD = P.buf("VD", [128, 8, 3 + 4 * DEPTH], F32)
        V = P.buf("V", [128, 4, NVX], F32)
        MT = P.buf("MT", [128, DEPTH, 3, 8, 2], F32, nseg=DEPTH)
        sc = P.buf("sc", [128, 8, 2], BF16)
        mring = [P.buf("mring%d" % i, [128, 8, 128], BF16, dma=True) for i in range(2)]
        modacc = P.buf("modacc", [128, 48], F32)
        mstate = {"pend": [], "ri": 0}
        NR = 8
        ring = [P.buf("ring%d" % i, [128, 8, 128], BF16, dma=True) for i in range(NR)]
        woring = [P.buf("wor%d" % i, [128, 8, 128], BF16, dma=True) for i in range(2)]
        gmring = [P.buf("gmr%d" % i, [128, 4, 128], BF16, dma=True) for i in range(2)]
        dw = P.buf("dw", [128, 4, 128], BF16, dma=True)
        ringi = [0]
        gmi = [0]
        woi = [0]

        EPS6 = cst.t[:, 0:1]
        EPS5 = cst.t[:, 1:2]
        ONE = cst.t[:, 2:3]
        QUART = cst.t[:, 3:4]

        P.op("gpsimd", lambda e: e.memset(ident_f.t[:], 1.0), writes=[(ident_f, 0)])
        P.op("gpsimd", lambda e: e.affine_select(out=ident_f.t[:], in_=ident_f.t[:], pattern=[[-1, 128]],
                                                  compare_op=ALU.is_equal, fill=0.0, base=0, channel_multiplier=1),
             reads=[(ident_f, 0)], writes=[(ident_f, 0)])
        P.op("gpsimd", lambda e: e.tensor_copy(out=ident_b.t[:], in_=ident_f.t[:]), reads=[(ident_f, 0)],
             writes=[(ident_b, 0)])
        P.op("gpsimd", lambda e: e.memset(ones1024.t[:], 1.0 / 1024.0), writes=[(ones1024, 0)])
        P.op("gpsimd", lambda e: e.memset(ones512.t[:], 1.0 / 512.0), writes=[(ones512, 0)])
        P.op("gpsimd", lambda e: e.memset(cst.t[:, 0:1], 1e-6), writes=[(cst, 0)])
        P.op("gpsimd", lambda e: e.memset(cst.t[:, 1:2], 1e-5), writes=[(cst, 0)])
        P.op("gpsimd", lambda e: e.memset(cst.t[:, 2:3], 1.0), writes=[(cst, 0)])
        P.op("gpsimd", lambda e: e.memset(cst.t[:, 3:4], 0.25 + 1.2e-7), writes=[(cst, 0)])

        def kp(ap):
            return ap.rearrange("(k p) c -> p k c", p=128)

        def load_w(l, slot, j):
            b = ring[ringi[0] % NR]
            ringi[0] += 1
            c0 = slot * BW + j * 128
            P.dma("gpsimd", b.t[:], kp(win_d[l, :, c0:c0 + 128]), writes=[(b, 0)], sem_buf=b)
            return b

        def load_gm(l, j):
            b = gmring[gmi[0] % 2]
            gmi[0] += 1
            P.dma("gpsimd", b.t[:], gm_d[l, j], writes=[(b, 0)], sem_buf=b)
            return b

        def mod_issue(l, mc):
            b = mring[mstate["ri"] % 2]
            mstate["ri"] += 1
            P.dma("gpsimd", b.t[:], kp(modw_d[l, :, mc * 128:(mc + 1) * 128]), writes=[(b, 0)], sem_buf=b)
            mstate["pend"].append((mc, b))

        def mod_consume():
            for (mc, b) in mstate["pend"]:
                psm = P.ps()
                for k in range(8):
                    P.op("tensor", lambda e: e.matmul(psm.t[:, 0:2], lhsT=b.t[:, k, :], rhs=sc.t[:, k, 0:2],
                                                      start=(k == 0), stop=(k == 7)),
                         reads=[(b, 0), (sc, 0)], writes=[(psm, 0)], inc=(k == 7))
                P.op("vector", lambda e: e.tensor_copy(out=modacc.t[:, 2 * mc:2 * mc + 2], in_=psm.t[:, 0:2]),
                     reads=[(psm, 0)], writes=[(modacc, 0)])
            mstate["pend"] = []

        def mod_finish(l):
            for j in range(3):
                for col in range(2):
                    tt("vector", MT.t[:, l, j, :, col], modacc.t[:, j * 16 + col:j * 16 + 16:2],
                       VD.t[:, :, 3 + 4 * l + 1 + j], ALU.add, [(modacc, 0), (VD, 0)], [(MT, l)])
            for col in range(2):
                P.op("vector", lambda e: e.scalar_tensor_tensor(out=MT.t[:, l, 1, :, col], in0=MT.t[:, l, 1, :, col],
                                                                scalar=1.0, in1=VD.t[:, :, 3 + 4 * l],
                                                                op0=ALU.add, op1=ALU.mult),
                     reads=[(MT, l), (VD, 0)], writes=[(MT, l)])

        def inproj(wb, tile, ps):
            off, wd = TILES[tile]
            for k in range(8):
                P.op("tensor", lambda e: e.matmul(ps.t[:, 0:wd], lhsT=wb.t[:, k, :], rhs=hT.t[:, k, off:off + wd],
                                                  start=(k == 0), stop=(k == 7)),
                     reads=[(wb, 0), (hT, hseg(k, tile))], writes=[(ps, 0)], inc=(k == 7))

        def act(out, in_, func, reads, writes, bias=None, scale=None):
            kw = {}
            if bias is not None:
                kw["bias"] = bias
            if scale is not None:
                kw["scale"] = scale
            P.op("scalar", lambda e: e.activation(out=out, in_=in_, func=func, **kw), reads=reads, writes=writes)

        def tt(eng, out, in0, in1, op, reads, writes):
            P.op(eng, lambda e: e.tensor_tensor(out=out, in0=in0, in1=in1, op=op), reads=reads, writes=writes)

        def rms_alloc(ph, tag):
            sq = [P.buf("%s_sq%d" % (tag, i), [128, 512], BF16, stack=ph) for i in range(2)]
            rt = P.buf(tag + "_rt", [128, 512], F32, stack=ph)
            rs = [P.buf(tag + "_rs%d" % i, [128, 512], F32, stack=ph) for i in range(2)]
            return sq, rt, rs

        def rms_stats(tile, bufs, it):
            off, wd = TILES[tile]
            sq, rt, rsl = bufs
            rs = rsl[it % 2]
            pss = P.ps()
            for k in range(8):
                s = sq[k % 2]
                act(s.t[:, :wd], res.t[:, k, off:off + wd], AF.Square, [(res, hseg(k, tile))], [(s, 0)])
                P.op("tensor", lambda e: e.matmul(pss.t[:, :wd], lhsT=ones1024.t[:], rhs=s.t[:, :wd],
                                                  start=(k == 0), stop=(k == 7)),
                     reads=[(ones1024, 0), (s, 0)], writes=[(pss, 0)])
            act(rt.t[:, :wd], pss.t[:, :wd], AF.Sqrt, [(pss, 0), (cst, 0)], [(rt, 0)], bias=EPS6)
            P.op("vector", lambda e: e.reciprocal(out=rs.t[:, :wd], in_=rt.t[:, :wd]), reads=[(rt, 0)],
                 writes=[(rs, 0)])
            return rs

        with P.phase() as ph:
            vds = P.buf("vds", [3 + 4 * DEPTH, D], F32, dma=True, stack=ph)
            nvd = 3 + 4 * DEPTH
            P.dma("sync", vds.t[:], vecD_d, writes=[(vds, 0)], sem_buf=vds)
            for k in range(8):
                ps = P.ps()
                P.op("tensor", lambda e: e.transpose(out=ps.t[:, 0:nvd], in_=vds.t[0:nvd, k * 128:(k + 1) * 128],
                                                     identity=ident_f.t[0:nvd, 0:nvd]),
                     reads=[(vds, 0), (ident_f, 0)], writes=[(ps, 0)])
                P.op("vector", lambda e: e.tensor_copy(out=VD.t[:, k, :], in_=ps.t[:, 0:nvd]), reads=[(ps, 0)],
                     writes=[(VD, 0)])
            act(sc.t[:], VD.t[:, :, 0:2], AF.Silu, [(VD, 0)], [(sc, 0)])
            mod_issue(0, 0)
            for mc in range(24):
                if mc + 1 < 24:
                    mod_issue(0, mc + 1)
                pend = mstate["pend"]
                mstate["pend"] = pend[:1]
                mod_consume()
                mstate["pend"] = pend[1:]
            mod_finish(0)
            xst = [P.buf("xst%d" % i, [128, D], F32, dma=True, stack=ph) for i in range(2)]
            for t128 in range(18):
                s_ = xst[t128 % 2]
                src = ctx_d[t128 * 128:(t128 + 1) * 128, :] if t128 < 2 else x_d[(t128 - 2) * 128:(t128 - 1) * 128, :]
                P.dma("sync", s_.t[:], src, writes=[(s_, 0)], sem_buf=s_)
                tile = 0 if t128 < 2 else 1 + (t128 - 2) // 4
                for half in range(2):
                    ps = P.ps()
                    for q in range(4):
                        k = half * 4 + q
                        P.op("tensor", lambda e: e.transpose(out=ps.t[:, q * 128:(q + 1) * 128],
                                                             in_=s_.t[:, k * 128:(k + 1) * 128], identity=ident_f.t[:]),
                             reads=[(s_, 0), (ident_f, 0)], writes=[(ps, 0)], inc=(q == 3))
                    dst = res.t[:, half * 4:(half + 1) * 4, t128 * 128:(t128 + 1) * 128]
                    srcp = ps.t[:, 0:512].rearrange("p (q c) -> p q c", q=4)
                    wr = [(res, hseg(half * 4 + q, tile)) for q in range(4)]
                    if half == 0:
                        P.op("vector", lambda e: e.tensor_copy(out=dst, in_=srcp), reads=[(ps, 0)], writes=wr)
                    else:
                        act(dst, srcp, AF.Identity, [(ps, 0)], wr)

        for l in range(NL):
            last = (l == NL - 1)
            tiles_all = [0, 1, 2, 3, 4]
            tiles_out = [1, 2, 3, 4] if last else tiles_all

            with P.phase() as ph:
                v5s = P.buf("v5s", [NV, BW], F32, dma=True, stack=ph)
                tv = P.buf("tv", [128, 4, 8], F32, stack=ph)
                P.dma("sync", v5s.t[:], vec5_d[l], writes=[(v5s, 0)], sem_buf=v5s)
                for j in range(4):
                    ps = P.ps()
                    P.op("tensor", lambda e: e.transpose(out=ps.t[:, 0:NV], in_=v5s.t[0:NV, j * 128:(j + 1) * 128],
                                                         identity=ident_f.t[0:NV, 0:NV]),
                         reads=[(v5s, 0), (ident_f, 0)], writes=[(ps, 0)])
                    P.op("vector", lambda e: e.tensor_copy(out=V.t[:, j, 0:NV], in_=ps.t[:, 0:NV]), reads=[(ps, 0)],
                         writes=[(V, 0)])
                z = tv.t[:, :, 0:2]
                az = tv.t[:, :, 2:4]
                ee = tv.t[:, :, 4:6]
                P.op("vector", lambda e: e.tensor_scalar(out=z, in0=V.t[:, :, 12:14], scalar1=-1.0, scalar2=None,
                                                         op0=ALU.mult), reads=[(V, 0)], writes=[(tv, 0)])
                act(az, z, AF.Abs, [(tv, 0)], [(tv, 0)])
                act(ee, az, AF.Exp, [(tv, 0)], [(tv, 0)], scale=-1.0)
                act(ee, ee, AF.Ln, [(tv, 0), (cst, 0)], [(tv, 0)], bias=ONE)
                P.op("vector", lambda e: e.scalar_tensor_tensor(out=ee, in0=z, scalar=0.0, in1=ee, op0=ALU.max,
                                                                op1=ALU.add), reads=[(tv, 0)], writes=[(tv, 0)])
                P.op("vector", lambda e: e.tensor_scalar(out=V.t[:, :, 53:55], in0=ee, scalar1=-8.0, scalar2=None,
                                                         op0=ALU.mult), reads=[(tv, 0)], writes=[(V, 0)])
                P.op("vector", lambda e: e.tensor_scalar(out=V.t[:, :, 55:57], in0=ee, scalar1=-16.0, scalar2=None,
                                                         op0=ALU.mult), reads=[(tv, 0)], writes=[(V, 0)])
                tt("vector", V.t[:, :, 57], V.t[:, :, 51], V.t[:, :, 52], ALU.mult, [(V, 0)], [(V, 0)])
                P.op("vector", lambda e: e.tensor_scalar(out=V.t[:, :, 58:62], in0=V.t[:, :, 8:12], scalar1=0.5,
                                                         scalar2=None, op0=ALU.mult), reads=[(V, 0)], writes=[(V, 0)])
                P.op("vector", lambda e: e.tensor_scalar(out=V.t[:, :, 62:64], in0=ee, scalar1=-4.0, scalar2=None,
                                                         op0=ALU.mult), reads=[(tv, 0)], writes=[(V, 0)])
                P.dma("gpsimd", dw.t[:], dw_d[l].rearrange("g i j -> i g j"), writes=[(dw, 0)], sem_buf=dw)

            units = []
            for j in range(4):
                units.append(("A", j, [(S_VA, j), (S_ZA, j)]))
            for j in range(4):
                units.append(("B", j, [(S_BC, j), (S_BV, j), (S_BB, j), (S_ZB, j)]))
            units.append(("WO", 0, []))
            for j in range(4):
                units.append(("C1", j, [(S_CG, j), (S_CA, j)]))
            units.append(("C2", 0, [(S_ZC, 0), (S_ZC, 1), (S_ZC, 2), (S_ZC, 3)]))
            for g in range(4):
                units.append(("D", g, [(S_DV, g), (S_ZD, g)]))
            units.append(("WO", 1, []))
            loaded = {}

            def issue(ui):
                kind, j, ws = units[ui]
                loaded[ui] = [load_w(l, s, jj) for (s, jj) in ws]
                if kind == "A":
                    loaded[ui].append(load_gm(l, j))

            issue(0)

            with P.phase() as ph:
                tmp = [P.buf("n_tmp%d" % i, [128, 512], F32, stack=ph) for i in range(2)]
                rb = rms_alloc(ph, "n")
                for it_, tile in enumerate(tiles_all):
                    off, wd = TILES[tile]
                    col = 1 if tile == 0 else 0
                    rs = rms_stats(tile, rb, it_)
                    for k in range(8):
                        tb = tmp[k % 2]
                        tt("gpsimd", tb.t[:, :wd], res.t[:, k, off:off + wd], rs.t[:, :wd], ALU.mult,
                           [(res, hseg(k, tile)), (rs, 0)], [(tb, 0)])
                        act(hT.t[:, k, off:off + wd], tb.t[:, :wd], AF.Identity, [(tb, 0), (MT, l)],
                            [(hT, hseg(k, tile))], scale=MT.t[:, l, 1, k, col:col + 1],
                            bias=MT.t[:, l, 0, k, col:col + 1])

            for ui, (kind, j, ws) in enumerate(units):
                if DBG and l == 0 and ui == int(DBG):
                    with P.phase() as ph:
                        dbb = P.buf("dbgb", [128, TT], F32, dma=True, stack=ph)
                        for kk in range(min(int(DBG), 8)):
                            P.op("vector", lambda e: e.tensor_copy(out=dbb.t[:], in_=cat.t[:, kk, :]),
                                 reads=[(cat, None)], writes=[(dbb, 0)])
                            P.dma("sync", dbg_d[:, kk, :], dbb.t[:], reads=[(dbb, 0)], sem_buf=dbb)
                        P.wait_all("sync", [dbb])
                    return nc
                if l + 1 < NL:
                    mod_consume()
                    nb_ = 2 if ui < 5 else 1
                    mc0 = ui * 2 if ui < 5 else 10 + (ui - 5)
                    for q_ in range(nb_):
                        mod_issue(l + 1, mc0 + q_)
                if ui + 1 < len(units) and kind != "A":
                    issue(ui + 1)
                W = loaded.get(ui, [])

                if kind == "A":
                    Wva, Wza, gm = W
                    PA = 3
                    with P.phase() as ph:
                        va = P.buf("a_va", [128, TT + 3 * PA], BF16, nseg=5, stack=ph)
                        vapad = P.buf("a_vapad", [128, 1], BF16, stack=ph)
                        S = P.buf("a_S", [128, TT], F32, nseg=5, stack=ph)
                        dg = P.buf("a_dg", [128, 8, 128], BF16, stack=ph)
                        xcb = [P.buf("a_xcb%d" % i, [128, 512], BF16, stack=ph) for i in range(2)]
                        T1d = [[P.buf("a_T1%d%d" % (i, p_), [128, 512], F32, stack=ph) for p_ in range(2)]
                               for i in range(2)]
                        T2 = [P.buf("a_T2%d" % i, [128, 512], F32, stack=ph) for i in range(2)]
                        T3d = [[P.buf("a_T3%d%d" % (i, p_), [128, 512], F32, stack=ph) for p_ in range(2)]
                               for i in range(2)]
                        zsl = [P.buf("a_zs%d" % i, [128, 512], BF16, stack=ph) for i in range(2)]
                        stbuf = [[P.buf("a_st%d%d" % (n_, p_), [128, 1], F32, stack=ph) for p_ in range(2)]
                                 for n_ in range(2)]
                        colA = lambda tile: TILES[tile][0] + (PA if tile == 0 else 2 * PA)
                        for i in range(8):
                            P.op("gpsimd", lambda e: e.tensor_scalar(out=dg.t[:, i, :], in0=ident_b.t[:],
                                                                     scalar1=V.t[:, j, i:i + 1], scalar2=1.0,
                                                                     op0=ALU.mult, op1=ALU.mult),
                                 reads=[(ident_b, 0), (V, 0)], writes=[(dg, 0)])
                        for (p0, p1) in ((0, PA), (PA + CTX, 2 * PA + CTX), (2 * PA + TT, 3 * PA + TT)):
                            P.op("gpsimd", lambda e: e.memset(va.t[:, p0:p1], 0.0), writes=[(vapad, 0)])
                        if ui + 1 < len(units):
                            issue(ui + 1)
                        for it, tile in enumerate(tiles_all):
                            off, wd = TILES[tile]
                            ps = P.ps()
                            inproj(Wva, tile, ps)
                            c0 = colA(tile)
                            if it % 2 == 0:
                                act(va.t[:, c0:c0 + wd], ps.t[:, :wd], AF.Identity, [(ps, 0)], [(va, tile)])
                            else:
                                P.op("vector", lambda e: e.tensor_copy(out=va.t[:, c0:c0 + wd], in_=ps.t[:, :wd]),
                                     reads=[(ps, 0)], writes=[(va, tile)])

                        def nbr(tile):
                            if tile == 0:
                                return [0]
                            return [t for t in (tile - 1, tile, tile + 1) if 1 <= t <= 4]

                        FW = [0, 1, 2, 3, 4]
                        BK = [0, 4, 3, 2, 1]
                        prev = [0.0, 0.0]
                        prev_dep = [[], []]
                        ST = {}

                        def front(it):
                            tl = [FW[it], BK[it]]
                            psc = [None, None]
                            psr = [None, None]
                            psi_ = [None, None]
                            for n in range(2):
                                tile = tl[n]
                                off, wd = TILES[tile]
                                c0 = colA(tile)
                                psc[n] = P.ps()
                                for k in range(4):
                                    sh = (c0 - 3 + k) if n == 0 else (c0 + k)
                                    P.op("tensor", lambda e: e.matmul(psc[n].t[:, :wd], lhsT=dg.t[:, n * 4 + k, :],
                                                                      rhs=va.t[:, sh:sh + wd], start=(k == 0), stop=(k == 3)),
                                         reads=[(dg, 0), (vapad, 0), (va, nbr(tile))], writes=[(psc[n], 0)], inc=(k == 3))
                            for n in range(2):
                                wd = TILES[tl[n]][1]
                                P.op("vector", lambda e: e.tensor_copy(out=xcb[n].t[:, :wd], in_=psc[n].t[:, :wd]),
                                     reads=[(psc[n], 0)], writes=[(xcb[n], 0)])
                            for n in range(2):
                                wd = TILES[tl[n]][1]
                                psr[n] = P.ps()
                                P.op("tensor", lambda e: e.matmul(psr[n].t[:, :wd], lhsT=gm.t[:, n * 2, :],
                                                                  rhs=xcb[n].t[:, :wd], start=True, stop=True),
                                     reads=[(gm, 0), (xcb[n], 0)], writes=[(psr[n], 0)])
                                psi_[n] = P.ps()
                                P.op("tensor", lambda e: e.matmul(psi_[n].t[:, :wd], lhsT=gm.t[:, n * 2 + 1, :],
                                                                  rhs=xcb[n].t[:, :wd], start=True, stop=True),
                                     reads=[(gm, 0), (xcb[n], 0)], writes=[(psi_[n], 0)])
                            if it == 0:
                                comb = [(0, 1)] if 0 in tiles_out else []
                            elif it == 3:
                                comb = [(3, 0), (2, 1)]
                            elif it == 4:
                                comb = [(4, 0), (1, 1)]
                            else:
                                comb = []
                            pszl = []
                            for (ctile, cn) in comb:
                                psz = P.ps()
                                inproj(Wza, ctile, psz)
                                pszl.append(psz)
                            ST[it] = (tl, psc, psr, psi_, comb, pszl)

                        def mid(it):
                            tl, psc, psr, psi_, comb, pszl = ST[it]
                            T1 = [T1d[0][it % 2], T1d[1][it % 2]]
                            T3 = [T3d[0][it % 2], T3d[1][it % 2]]
                            for n in range(2):
                                wd = TILES[tl[n]][1]
                                act(T1[n].t[:, :wd], psr[n].t[:, :wd], AF.Tanh, [(psr[n], 0), (V, 0)], [(T1[n], 0)],
                                    scale=0.5, bias=V.t[:, j, 58 + n:59 + n])
                                act(T3[n].t[:, :wd], psi_[n].t[:, :wd], AF.Tanh, [(psi_[n], 0), (V, 0)], [(T3[n], 0)],
                                    scale=0.5, bias=V.t[:, j, 60 + n:61 + n])
                            for n in range(2):
                                wd = TILES[tl[n]][1]
                                act(T1[n].t[:, :wd], T1[n].t[:, :wd], AF.Exp, [(T1[n], 0), (V, 0)], [(T1[n], 0)],
                                    scale=V.t[:, j, 62 + n:63 + n], bias=V.t[:, j, 62 + n:63 + n])
                                tt("gpsimd", T2[n].t[:, :wd], T1[n].t[:, :wd], T1[n].t[:, :wd], ALU.mult, [(T1[n], 0)],
                                   [(T2[n], 0)])
                            for ci, (ctile, cn) in enumerate(comb):
                                wd = TILES[ctile][1]
                                act(zsl[ci].t[:, :wd], pszl[ci].t[:, :wd], AF.Tanh, [(pszl[ci], 0)], [(zsl[ci], 0)],
                                    scale=0.5)
                            for n in range(2):
                                wd = TILES[tl[n]][1]
                                P.op("vector", lambda e: e.scalar_tensor_tensor(out=T3[n].t[:, :wd], in0=T3[n].t[:, :wd],
                                                                                scalar=1.0, in1=psc[n].t[:, :wd],
                                                                                op0=ALU.add, op1=ALU.mult),
                                     reads=[(T3[n], 0), (psc[n], 0)], writes=[(T3[n], 0)])
                            for ci, (ctile, cn) in enumerate(comb):
                                wd = TILES[ctile][1]
                                P.op("vector", lambda e: e.scalar_tensor_tensor(out=zsl[ci].t[:, :wd],
                                                                                in0=zsl[ci].t[:, :wd],
                                                                                scalar=1.0, in1=pszl[ci].t[:, :wd],
                                                                                op0=ALU.add, op1=ALU.mult),
                                     reads=[(zsl[ci], 0), (pszl[ci], 0)], writes=[(zsl[ci], 0)])
                            for n in range(2):
                                wd = TILES[tl[n]][1]
                                act(T2[n].t[:, :wd], T2[n].t[:, :wd], AF.Sqrt, [(T2[n], 0), (cst, 0)], [(T2[n], 0)],
                                    scale=-0.25, bias=QUART)

                        def back(it):
                            tl, psc, psr, psi_, comb, pszl = ST[it]
                            T1 = [T1d[0][it % 2], T1d[1][it % 2]]
                            T3 = [T3d[0][it % 2], T3d[1][it % 2]]
                            for n in range(2):
                                wd = TILES[tl[n]][1]
                                tt("gpsimd", T3[n].t[:, :wd], T3[n].t[:, :wd], T2[n].t[:, :wd], ALU.mult,
                                   [(T3[n], 0), (T2[n], 0)], [(T3[n], 0)])
                            for n in range(2):
                                tile = tl[n]
                                off, wd = TILES[tile]
                                store_n = (n == 0) if it == 0 else (it in (1, 2))
                                if store_n:
                                    o_ap = S.t[:, off:off + wd]
                                    wr = [(S, tile)]
                                else:
                                    o_ap = T3[n].t[:, 0:wd]
                                    wr = [(T3[n], 0)]
                                d0 = T1[n].t[:, 0:wd]
                                d1 = T3[n].t[:, 0:wd]
                                if n == 1:
                                    o_ap, d0, d1 = o_ap[:, ::-1], d0[:, ::-1], d1[:, ::-1]
                                init = prev[n]
                                P.op("vector", lambda e: e.tensor_tensor_scan(out=o_ap, data0=d0, data1=d1, initial=init,
                                                                              op0=ALU.mult, op1=ALU.add),
                                     reads=[(T1[n], 0), (T3[n], 0)] + prev_dep[n], writes=wr)
                                stc = stbuf[n][it % 2]
                                col = (wd - 1) if n == 0 else 0
                                if store_n:
                                    src = S.t[:, off + col:off + col + 1]
                                else:
                                    src = T3[n].t[:, col:col + 1]
                                P.op("vector", lambda e: e.tensor_copy(out=stc.t[:, 0:1], in_=src), reads=wr,
                                     writes=[(stc, 0)])
                                prev[n] = stc.t[:, 0:1]
                                prev_dep[n] = [(stc, 0)]
                            for ci, (ctile, cn) in enumerate(comb):
                                off, wd = TILES[ctile]
                                tt("gpsimd", T3[cn].t[:, :wd], T3[cn].t[:, :wd], S.t[:, off:off + wd], ALU.add,
                                   [(T3[cn], 0), (S, ctile)], [(T3[cn], 0)])
                                P.op("vector", lambda e: e.scalar_tensor_tensor(out=cat.t[:, j, off:off + wd],
                                                                                in0=T3[cn].t[:, :wd], scalar=0.5,
                                                                                in1=zsl[ci].t[:, :wd], op0=ALU.mult,
                                                                                op1=ALU.mult),
                                     reads=[(T3[cn], 0), (zsl[ci], 0)], writes=[(cat, hseg(j, ctile))])

                        front(0)
                        for it in range(5):
                            mid(it)
                            if it + 1 < 5:
                                front(it + 1)
                            back(it)

                elif kind == "B":
                    Wbc, Wbv, Wbb, Wzb = W
                    with P.phase() as ph:
                        prod = P.buf("b_prod", [128, TT + 3], BF16, nseg=5, stack=ph)
                        dg = P.buf("b_dg", [128, 3, 128], BF16, stack=ph)
                        bcs = [P.buf("b_bcs%d" % i, [128, 512], F32, stack=ph) for i in range(2)]
                        zs = [P.buf("b_zs%d" % i, [128, 512], F32, stack=ph) for i in range(2)]
                        gb = [P.buf("b_g%d" % i, [128, 512], F32, stack=ph) for i in range(2)]
                        colB = lambda tile: TILES[tile][0] + (1 if tile == 0 else 2)
                        P.op("gpsimd", lambda e: e.memset(prod.t[:], 0.0), writes=[(prod, None)])
                        for i in range(3):
                            P.op("gpsimd", lambda e: e.tensor_scalar(out=dg.t[:, i, :], in0=ident_b.t[:],
                                                                     scalar1=V.t[:, j, 14 + i:15 + i], scalar2=1.0,
                                                                     op0=ALU.mult, op1=ALU.mult),
                                 reads=[(ident_b, 0), (V, 0)], writes=[(dg, 0)])
                        for it, tile in enumerate(tiles_out):
                            off, wd = TILES[tile]
                            c0 = colB(tile)
                            ps1 = P.ps()
                            inproj(Wbc, tile, ps1)
                            ps2 = P.ps()
                            inproj(Wbv, tile, ps2)
                            b_ = bcs[it % 2]
                            act(b_.t[:, :wd], ps1.t[:, :wd], AF.Identity, [(ps1, 0)], [(b_, 0)])
                            tt("vector", prod.t[:, c0:c0 + wd], ps2.t[:, :wd], b_.t[:, :wd], ALU.mult,
                               [(ps2, 0), (b_, 0)], [(prod, tile)])

                        def nbrB(tile):
                            if tile == 0:
                                return [0]
                            return [t for t in (tile - 1, tile, tile + 1) if 1 <= t <= 4]

                        for it, tile in enumerate(tiles_out):
                            off, wd = TILES[tile]
                            c0 = colB(tile)
                            ps3 = P.ps()
                            for k in range(3):
                                sh = c0 - 1 + k
                                P.op("tensor", lambda e: e.matmul(ps3.t[:, :wd], lhsT=dg.t[:, k, :],
                                                                  rhs=prod.t[:, sh:sh + wd], start=(k == 0), stop=(k == 2)),
                                     reads=[(dg, 0), (prod, nbrB(tile))], writes=[(ps3, 0)], inc=(k == 2))
                            ps4 = P.ps()
                            inproj(Wbb, tile, ps4)
                            ps5 = P.ps()
                            inproj(Wzb, tile, ps5)
                            z_ = zs[it % 2]
                            g_ = gb[it % 2]
                            act(z_.t[:, :wd], ps5.t[:, :wd], AF.Silu, [(ps5, 0)], [(z_, 0)])
                            tt("vector", g_.t[:, :wd], ps4.t[:, :wd], z_.t[:, :wd], ALU.mult, [(ps4, 0), (z_, 0)],
                               [(g_, 0)])
                            tt("vector", cat.t[:, 4 + j, off:off + wd], ps3.t[:, :wd], g_.t[:, :wd], ALU.mult,
                               [(ps3, 0), (g_, 0)], [(cat, hseg(4 + j, tile))])

                elif kind == "C1":
                    Wcg, Wca = W
                    PC = 15
                    with P.phase() as ph:
                        glu = P.buf("c_glu", [128, TT + 3 * PC], BF16, nseg=5, stack=ph)
                        dg = P.buf("c_dg", [128, 31, 128], BF16, stack=ph)
                        sg = [P.buf("c_sg%d" % i, [128, 512], F32, stack=ph) for i in range(2)]
                        colC = lambda tile: TILES[tile][0] + (PC if tile == 0 else 2 * PC)
                        P.op("gpsimd", lambda e: e.memset(glu.t[:], 0.0), writes=[(glu, None)])
                        for i in range(6, 31):
                            P.op("gpsimd", lambda e: e.tensor_scalar(out=dg.t[:, i, :], in0=ident_b.t[:],
                                                                     scalar1=V.t[:, j, 17 + i:18 + i], scalar2=1.0,
                                                                     op0=ALU.mult, op1=ALU.mult),
                                 reads=[(ident_b, 0), (V, 0)], writes=[(dg, 0)])
                        for it, tile in enumerate(tiles_out):
                            off, wd = TILES[tile]
                            c0 = colC(tile)
                            ps1 = P.ps()
                            inproj(Wcg, tile, ps1)
                            ps2 = P.ps()
                            inproj(Wca, tile, ps2)
                            s_ = sg[it % 2]
                            act(s_.t[:, :wd], ps1.t[:, :wd], AF.Sigmoid, [(ps1, 0)], [(s_, 0)])
                            tt("vector", glu.t[:, c0:c0 + wd], ps2.t[:, :wd], s_.t[:, :wd], ALU.mult,
                               [(ps2, 0), (s_, 0)], [(glu, tile)])

                        def nbrC(tile):
                            if tile == 0:
                                return [0]
                            return [t for t in (tile - 1, tile, tile + 1) if 1 <= t <= 4]

                        NDV = 6
                        acc = [P.buf("c_acc%d" % i, [128, 512], F32, stack=ph) for i in range(2)]
                        for it, tile in enumerate(tiles_out):
                            off, wd = TILES[tile]
                            c0 = colC(tile)
                            ps3 = P.ps()
                            for k in range(NDV, 31):
                                sh = c0 - 15 + k
                                P.op("tensor", lambda e: e.matmul(ps3.t[:, :wd], lhsT=dg.t[:, k, :],
                                                                  rhs=glu.t[:, sh:sh + wd], start=(k == NDV), stop=(k == 30)),
                                     reads=[(dg, 0), (glu, nbrC(tile))], writes=[(ps3, 0)], inc=(k == 30))
                            ac = acc[it % 2]
                            for k in range(NDV):
                                sh = c0 - 15 + k
                                if k == 0:
                                    P.op("vector", lambda e: e.tensor_scalar(out=ac.t[:, :wd], in0=glu.t[:, sh:sh + wd],
                                                                             scalar1=V.t[:, j, 17 + k:18 + k], scalar2=None,
                                                                             op0=ALU.mult),
                                         reads=[(glu, nbrC(tile)), (V, 0)], writes=[(ac, 0)])
                                else:
                                    P.op("vector", lambda e: e.scalar_tensor_tensor(out=ac.t[:, :wd],
                                                                                    in0=glu.t[:, sh:sh + wd],
                                                                                    scalar=V.t[:, j, 17 + k:18 + k],
                                                                                    in1=ac.t[:, :wd], op0=ALU.mult,
                                                                                    op1=ALU.add),
                                         reads=[(glu, nbrC(tile)), (V, 0), (ac, 0)], writes=[(ac, 0)])
                            tt("vector", cat.t[:, 4 + j, off:off + wd], ps3.t[:, :wd], ac.t[:, :wd], ALU.add,
                               [(ps3, 0), (ac, 0)], [(cat, hseg(4 + j, tile))])

                elif kind == "C2":
                    Wzc = W
                    with P.phase() as ph:
                        cpw = P.buf("c_pw", [128, 4, BW], BF16, dma=True, stack=ph)
                        P.dma("gpsimd", cpw.t[:], kp(cpw_d[l]), writes=[(cpw, 0)], sem_buf=cpw)
                        usq = [P.buf("c_usq%d" % i, [128, 512], BF16, stack=ph) for i in range(2)]
                        meanl = [P.buf("c_mean%d" % i, [128, 512], F32, stack=ph) for i in range(2)]
                        m2l = [P.buf("c_m2%d" % i, [128, 512], F32, stack=ph) for i in range(2)]
                        rstdl = [P.buf("c_rstd%d" % i, [128, 512], F32, stack=ph) for i in range(2)]
                        tb = [P.buf("c_t%d" % i, [128, 512], F32, stack=ph) for i in range(2)]
                        unl = [P.buf("c_un%d" % i, [128, 4, 512], BF16, nseg=4, stack=ph) for i in range(2)]
                        zs = [P.buf("c_zs%d" % i, [128, 512], F32, stack=ph) for i in range(2)]

                        def partA(it, tile):
                            off, wd = TILES[tile]
                            mean, m2, rstd, un = meanl[it % 2], m2l[it % 2], rstdl[it % 2], unl[it % 2]
                            psm = P.ps()
                            psq = P.ps()
                            for jj in range(4):
                                u_ap = cat.t[:, 4 + jj, off:off + wd]
                                useg = (cat, hseg(4 + jj, tile))
                                P.op("tensor", lambda e: e.matmul(psm.t[:, :wd], lhsT=ones512.t[:], rhs=u_ap,
                                                                  start=(jj == 0), stop=(jj == 3)),
                                     reads=[(ones512, 0), useg], writes=[(psm, 0)], inc=(jj == 3))
                                q_ = usq[jj % 2]
                                act(q_.t[:, :wd], u_ap, AF.Square, [useg], [(q_, 0)])
                                P.op("tensor", lambda e: e.matmul(psq.t[:, :wd], lhsT=ones512.t[:], rhs=q_.t[:, :wd],
                                                                  start=(jj == 0), stop=(jj == 3)),
                                     reads=[(ones512, 0), (q_, 0)], writes=[(psq, 0)])
                            act(mean.t[:, :wd], psm.t[:, :wd], AF.Identity, [(psm, 0)], [(mean, 0)])
                            tt("gpsimd", m2.t[:, :wd], mean.t[:, :wd], mean.t[:, :wd], ALU.mult, [(mean, 0)], [(m2, 0)])
                            tt("vector", m2.t[:, :wd], psq.t[:, :wd], m2.t[:, :wd], ALU.subtract, [(psq, 0), (m2, 0)],
                               [(m2, 0)])
                            act(m2.t[:, :wd], m2.t[:, :wd], AF.Sqrt, [(m2, 0), (cst, 0)], [(m2, 0)], bias=EPS5)
                            P.op("vector", lambda e: e.reciprocal(out=rstd.t[:, :wd], in_=m2.t[:, :wd]),
                                 reads=[(m2, 0)], writes=[(rstd, 0)])
                            for jj in range(4):
                                t_ = tb[jj % 2]
                                tt("gpsimd", t_.t[:, :wd], cat.t[:, 4 + jj, off:off + wd], mean.t[:, :wd], ALU.subtract,
                                   [(cat, hseg(4 + jj, tile)), (mean, 0)], [(t_, 0)])
                                tt("vector", t_.t[:, :wd], t_.t[:, :wd], rstd.t[:, :wd], ALU.mult, [(t_, 0), (rstd, 0)],
                                   [(t_, 0)])
                                act(un.t[:, jj, :wd], t_.t[:, :wd], AF.Silu, [(t_, 0), (V, 0)], [(un, jj)],
                                    scale=V.t[:, jj, 48:49], bias=V.t[:, jj, 49:50])

                        def partB(it, tile):
                            off, wd = TILES[tile]
                            un = unl[it % 2]
                            for m in range(4):
                                psp = P.ps()
                                for kc in range(4):
                                    P.op("tensor", lambda e: e.matmul(psp.t[:, :wd], lhsT=cpw.t[:, kc, m * 128:(m + 1) * 128],
                                                                      rhs=un.t[:, kc, :wd], start=(kc == 0), stop=(kc == 3)),
                                         reads=[(cpw, 0), (un, kc)], writes=[(psp, 0)], inc=(kc == 3))
                                psz = P.ps()
                                inproj(Wzc[m], tile, psz)
                                z_ = zs[m % 2]
                                act(z_.t[:, :wd], psz.t[:, :wd], AF.Silu, [(psz, 0)], [(z_, 0)])
                                P.op("vector", lambda e: e.scalar_tensor_tensor(out=cat.t[:, m, off:off + wd],
                                                                                in0=psp.t[:, :wd],
                                                                                scalar=V.t[:, m, 50:51], in1=z_.t[:, :wd],
                                                                                op0=ALU.add, op1=ALU.mult),
                                     reads=[(psp, 0), (z_, 0), (V, 0)], writes=[(cat, hseg(m, tile))])

                        tl_ = list(tiles_out)
                        partA(0, tl_[0])
                        for it in range(len(tl_)):
                            if it + 1 < len(tl_):
                                partA(it + 1, tl_[it + 1])
                            partB(it, tl_[it])

                elif kind == "D":
                    g = j
                    Wdv, Wzd = W
                    with P.phase() as ph:
                        dvtm = P.buf("d_dvtm", [128, 18, 128], BF16, nseg=18, stack=ph)
                        pmg = P.buf("d_pm", [128, NMG, 128], BF16, dma=True, stack=ph)
                        pT = [P.buf("d_pT%d" % i, [128, 512], BF16, stack=ph) for i in range(2)]
                        zs = [P.buf("d_zs%d" % i, [128, 512], F32, stack=ph) for i in range(2)]
                        tb = [P.buf("d_t%d" % i, [128, 512], F32, stack=ph) for i in range(2)]
                        ng = GCNT[g]
                        icc = P.buf("d_icc", [128, 64], F32, dma=True, stack=ph)
                        tabc = P.buf("d_tabc", [128, CTX], F32, dma=True, stack=ph)
                        P.dma("sync", icc.t[:], icc_d[:, g * 64:(g + 1) * 64], writes=[(icc, 0)], sem_buf=icc)
                        P.dma("sync", tabc.t[:], tabc_d[:, g * CTX:(g + 1) * CTX], writes=[(tabc, 0)], sem_buf=tabc)
                        P.dma("sync", pmg.t[:, 0:ng, :], pm_d[:, GOFF[g]:GOFF[g] + ng, :], writes=[(pmg, 0)], sem_buf=pmg)
                        t128s = list(range(0 if not last else 2, 18))
                        grp = []
                        cur = []
                        for t1 in t128s:
                            cur.append(t1)
                            if len(cur) == 4 or t1 == t128s[-1] or t1 == 1:
                                grp.append(cur)
                                cur = []
                        for gi, lst in enumerate(grp):
                            ps = P.ps()
                            for q, t1 in enumerate(lst):
                                tile = 0 if t1 < 2 else 1 + (t1 - 2) // 4
                                for k in range(8):
                                    P.op("tensor", lambda e: e.matmul(ps.t[:, q * 128:(q + 1) * 128],
                                                                      lhsT=hT.t[:, k, t1 * 128:(t1 + 1) * 128],
                                                                      rhs=Wdv.t[:, k, :], start=(k == 0), stop=(k == 7)),
                                         reads=[(Wdv, 0), (hT, hseg(k, tile))], writes=[(ps, 0)],
                                         inc=(k == 7 and q == len(lst) - 1))
                            n_ = len(lst)
                            dst = dvtm.t[:, lst[0]:lst[0] + n_, :]
                            srcp = ps.t[:, 0:n_ * 128].rearrange("p (q c) -> p q c", q=n_)
                            if gi % 2 == 0:
                                act(dst, srcp, AF.Identity, [(ps, 0)], [(dvtm, lst)])
                            else:
                                P.op("vector", lambda e: e.tensor_copy(out=dst, in_=srcp), reads=[(ps, 0)],
                                     writes=[(dvtm, lst)])
                        for it, tile in enumerate(tiles_out):
                            off, wd = TILES[tile]
                            n128 = wd // 128
                            t0 = off // 128
                            ps = P.ps()
                            nmm = sum(len(PLAN[g][t0 + q]) for q in range(n128))
                            cnt = 0
                            for q in range(n128):
                                lst = PLAN[g][t0 + q]
                                for idx, (i_, m_) in enumerate(lst):
                                    cnt += 1
                                    ml = m_ - GOFF[g]
                                    P.op("tensor", lambda e: e.matmul(ps.t[:, q * 128:(q + 1) * 128],
                                                                      lhsT=dvtm.t[:, i_, :], rhs=pmg.t[:, ml, :],
                                                                      start=(idx == 0), stop=(idx == len(lst) - 1)),
                                         reads=[(dvtm, i_), (pmg, 0)], writes=[(ps, 0)], inc=(cnt == nmm))
                            p_ = pT[it % 2]
                            if tile == 0:
                                tt("vector", p_.t[:, :wd], ps.t[:, :wd], tabc.t[:, :], ALU.mult,
                                   [(ps, 0), (tabc, 0)], [(p_, 0)])
                            else:
                                rbase = (off - CTX) // GRID_W
                                a = 0
                                while a < 8:
                                    b = a
                                    while b + 1 < 8 and IRC[g][rbase + b + 1] == IRC[g][rbase + a]:
                                        b += 1
                                    nr = b - a + 1
                                    sc_ = float(IRC[g][rbase + a])
                                    o_ap = p_.t[:, a * 64:(b + 1) * 64].rearrange("p (r c) -> p r c", c=64)
                                    i_ap = ps.t[:, a * 64:(b + 1) * 64].rearrange("p (r c) -> p r c", c=64)
                                    t_ap = icc.t[:, :].unsqueeze(1).broadcast_to([128, nr, 64])
                                    P.op("vector", lambda e: e.scalar_tensor_tensor(out=o_ap, in0=i_ap, scalar=sc_,
                                                                                    in1=t_ap, op0=ALU.mult, op1=ALU.mult),
                                         reads=[(ps, 0), (icc, 0)], writes=[(p_, 0)])
                                    a = b + 1
                            ps2 = P.ps()
                            P.op("tensor", lambda e: e.matmul(ps2.t[:, :wd], lhsT=dw.t[:, g, :], rhs=p_.t[:, :wd],
                                                              start=True, stop=True),
                                 reads=[(dw, 0), (p_, 0)], writes=[(ps2, 0)])
                            ps3 = P.ps()
                            inproj(Wzd, tile, ps3)
                            z_ = zs[it % 2]
                            t_ = tb[it % 2]
                            act(z_.t[:, :wd], ps3.t[:, :wd], AF.Silu, [(ps3, 0)], [(z_, 0)])
                            act(t_.t[:, :wd], ps2.t[:, :wd], AF.Identity, [(ps2, 0), (V, 0)], [(t_, 0)],
                                scale=V.t[:, g, 52:53], bias=V.t[:, g, 57:58])
                            tt("gpsimd", cat.t[:, 4 + g, off:off + wd], t_.t[:, :wd], z_.t[:, :wd], ALU.mult,
                               [(t_, 0), (z_, 0)], [(cat, hseg(4 + g, tile))])

                elif kind == "WO":
                    half = j

                    def load_wo(m):
                        b = woring[woi[0] % 2]
                        woi[0] += 1
                        P.dma("gpsimd", b.t[:], kp(wout_d[l, half * 1024:(half + 1) * 1024, m * 128:(m + 1) * 128]),
                              writes=[(b, 0)], sem_buf=b)
                        return b

                    nxt = load_wo(0)
                    for m in range(8):
                        wo = nxt
                        if m + 1 < 8:
                            nxt = load_wo(m + 1)
                        for tile in tiles_out:
                            off, wd = TILES[tile]
                            col = 1 if tile == 0 else 0
                            ps = P.ps()
                            for k in range(8):
                                P.op("tensor", lambda e: e.matmul(ps.t[:, :wd], lhsT=wo.t[:, k, :],
                                                                  rhs=cat.t[:, k, off:off + wd], start=(k == 0), stop=(k == 7)),
                                     reads=[(wo, 0), (cat, hseg(k, tile))], writes=[(ps, 0)], inc=(k == 7))
                            P.op("vector", lambda e: e.scalar_tensor_tensor(out=res.t[:, m, off:off + wd], in0=ps.t[:, :wd],
                                                                            scalar=MT.t[:, l, 2, m, col:col + 1],
                                                                            in1=res.t[:, m, off:off + wd],
                                                                            op0=ALU.mult, op1=ALU.add),
                                 reads=[(ps, 0), (MT, l), (res, hseg(m, tile))], writes=[(res, hseg(m, tile))])

            if l + 1 < NL:
                mod_consume()
                mod_finish(l + 1)

        with P.phase() as ph:
            tmp = [P.buf("f_tmp%d" % i, [128, 512], F32, stack=ph) for i in range(2)]
            ost = [P.buf("f_ost%d" % i, [128, D], F32, dma=True, stack=ph) for i in range(2)]
            oi = 0
            rb = rms_alloc(ph, "f")
            for it_, tile in enumerate([1, 2, 3, 4]):
                off, wd = TILES[tile]
                rs = rms_stats(tile, rb, it_)
                for k in range(8):
                    tb = tmp[k % 2]
                    tt("gpsimd", tb.t[:, :wd], res.t[:, k, off:off + wd], rs.t[:, :wd], ALU.mult,
                       [(res, hseg(k, tile)), (rs, 0)], [(tb, 0)])
                    act(res.t[:, k, off:off + wd], tb.t[:, :wd], AF.Identity, [(tb, 0), (VD, 0)],
                        [(res, hseg(k, tile))], scale=VD.t[:, k, 2:3])
                for q in range(4):
                    o_ = ost[oi % 2]
                    oi += 1
                    c0 = off + q * 128
                    for half in range(2):
                        ps = P.ps()
                        for kk in range(4):
                            k = half * 4 + kk
                            P.op("tensor", lambda e: e.transpose(out=ps.t[:, kk * 128:(kk + 1) * 128],
                                                                 in_=res.t[:, k, c0:c0 + 128], identity=ident_f.t[:]),
                                 reads=[(res, hseg(k, tile)), (ident_f, 0)], writes=[(ps, 0)], inc=(kk == 3))
                        if half == 0:
                            P.op("vector", lambda e: e.tensor_copy(out=o_.t[:, 0:512], in_=ps.t[:, 0:512]),
                                 reads=[(ps, 0)], writes=[(o_, 0)])
                        else:
                            act(o_.t[:, 512:1024], ps.t[:, 0:512], AF.Identity, [(ps, 0)], [(o_, 0)])
                    r0 = (tile - 1) * 512 + q * 128
                    P.dma("sync", out_d[r0:r0 + 128, :], o_.t[:], reads=[(o_, 0)], sem_buf=o_)
            P.wait_all("sync", ost)
    return nc


def prep_inputs(inp, NL=DEPTH):
    f = lambda a: np.ascontiguousarray(np.asarray(a, dtype=np.float32))
    x = f(inp["x"])
    B = x.shape[0]
    c = f(inp["c"])
    ctx = f(inp["ctx"])
    c_ctx = f(inp["c_ctx"])
    mod_b = f(inp["mod_b"])
    norm_g = f(inp["norm_g"])
    final_g = f(inp["final_g"])
    vec5 = np.zeros((DEPTH, NV, BW), np.float32)
    gm = np.zeros((DEPTH, 4, 128, 4, 128), np.float32)
    a_conv, a_br, a_bi, a_lam = f(inp["a_conv"]), f(inp["a_br"]), f(inp["a_bi"]), f(inp["a_lam"])
    b_conv, c_conv = f(inp["b_conv"]), f(inp["c_conv"])
    a_wr, a_wi = f(inp["a_wr"]), f(inp["a_wi"])
    nl = a_conv.shape[0]
    for l in range(nl):
        vec5[l, 0:4] = a_conv[l, 0]
        vec5[l, 4:8] = a_conv[l, 1]
        vec5[l, 8:10] = a_br[l]
        vec5[l, 10:12] = a_bi[l]
        vec5[l, 12:14] = a_lam[l]
        vec5[l, 14:17] = b_conv[l]
        vec5[l, 17:48] = c_conv[l]
        vec5[l, 48] = f(inp["c_ln_g"])[l]
        vec5[l, 49] = f(inp["c_ln_b"])[l]
        vec5[l, 50] = f(inp["c_pw_b"])[l]
        vec5[l, 51] = f(inp["d_b"])[l].reshape(-1)
        vec5[l, 52] = f(inp["d_scale"])[l]
        for j in range(4):
            for n in range(2):
                for gi, wsrc in enumerate((a_wr, a_wi)):
                    for hh in range(2):
                        gm[l, j, hh * 64:(hh + 1) * 64, n * 2 + gi, hh * 64:(hh + 1) * 64] = wsrc[l, n, 2 * j + hh]
    shared = {
        "vec5": vec5, "gm": gm,
        "mod_w": f(inp["mod_w"]), "w_in": f(inp["w_in"]), "w_out": f(inp["w_out"]),
        "c_pw": f(inp["c_pw"]), "d_w": f(inp["d_w"]),
        "pmats": PM,
        "icc": np.ascontiguousarray(np.broadcast_to(ICC.reshape(1, -1), (128, 256))).astype(np.float32),
        "tabc": np.ascontiguousarray(np.broadcast_to(TABC.reshape(1, -1), (128, 4 * CTX))).astype(np.float32),
    }
    for k_ in ("mod_w", "w_in", "w_out", "c_pw", "d_w"):
        a = shared[k_]
        if a.shape[0] < DEPTH:
            pad = np.zeros((DEPTH - a.shape[0],) + a.shape[1:], np.float32)
            shared[k_] = np.concatenate([a, pad], 0)
    maps = []
    for b in range(B):
        vecD = np.zeros((3 + 4 * DEPTH, D), np.float32)
        vecD[0] = c[b]
        vecD[1] = c_ctx
        vecD[2] = final_g
        for l in range(nl):
            vecD[3 + 4 * l] = norm_g[l]
            vecD[3 + 4 * l + 1:3 + 4 * l + 4] = mod_b[l].reshape(3, D)
        m = dict(shared)
        m["x"] = x[b]
        m["ctx"] = ctx[b]
        m["vecD"] = vecD
        maps.append(m)
    return maps


_NC_CACHE = {}


def kernel(**inputs):
    maps = prep_inputs(inputs)
    if "nc" not in _NC_CACHE:
        _NC_CACHE["nc"] = build(DEPTH)
    nc = _NC_CACHE["nc"]
    res = run_bass_kernel_spmd(nc, maps, core_ids=list(range(8)))
    out = np.stack([np.asarray(r["out"], dtype=np.float32) for r in res.results], 0)
    return out
```

```python
from contextlib import ExitStack, contextmanager
import numpy as np
import ml_dtypes
import concourse.bass as bass
import concourse.mybir as mybir
from concourse.bass_utils import run_bass_kernel_spmd

F32 = mybir.dt.float32
BF16 = mybir.dt.bfloat16
AF = mybir.ActivationFunctionType
ALU = mybir.AluOpType

DEPTH = 4
D = 1024
SEQ = 2048
CTX = 256
TT = CTX + SEQ
BW = 512
IN_W = 5632
GRID_W = 64
ROWS = SEQ // GRID_W
WINS = (2, 4, 8, 16)
NV = 53
NVX = 64
ENGS = ["tensor", "vector", "scalar", "gpsimd", "sync"]
TILES = [(0, 256), (256, 512), (768, 512), (1280, 512), (1792, 512)]
S_VA, S_ZA, S_BB, S_BC, S_BV, S_ZB, S_CA, S_CG, S_ZC, S_DV, S_ZD = range(11)


def _pool_plan():
    mats = []
    key2idx = {}
    plan = []
    goff = []
    gcnt = []
    icc = np.zeros((4, 64), np.float32)
    irc = np.zeros((4, ROWS), np.float64)
    tabc = np.zeros((4, CTX), np.float32)
    for g, w in enumerate(WINS):
        h = w // 2
        start = len(mats)
        key2idx = {}
        pl = {}
        pos = np.arange(CTX)
        lo = np.clip(pos - h, 0, CTX)
        hi = np.clip(pos + h, 0, CTX)
        cnt = (hi - lo)
        tabc[g] = 1.0 / cnt
        M = np.zeros((CTX, CTX), np.float32)
        for tp in range(CTX):
            M[lo[tp]:hi[tp], tp] = 1.0
            M[tp, tp] -= cnt[tp]
        for j in range(2):
            lst = []
            for i in range(2):
                blk = M[i * 128:(i + 1) * 128, j * 128:(j + 1) * 128]
                if not blk.any():
                    continue
                k = blk.tobytes()
                if k not in key2idx:
                    key2idx[k] = len(mats)
                    mats.append(blk.copy())
                lst.append((i, key2idx[k]))
            pl[j] = lst
        r = np.arange(ROWS)
        r0 = np.clip(r - h, 0, ROWS)
        r1 = np.clip(r + h, 0, ROWS)
        c = np.arange(GRID_W)
        c0 = np.clip(c - h, 0, GRID_W)
        c1 = np.clip(c + h, 0, GRID_W)
        icc[g] = 1.0 / (c1 - c0)
        irc[g] = 1.0 / (r1 - r0)
        colbox = np.zeros((GRID_W, GRID_W), np.float32)
        for cp in range(GRID_W):
            colbox[c0[cp]:c1[cp], cp] = 1.0
        for j in range(ROWS // 2):
            lst = []
            for i in range(ROWS // 2):
                blk = np.zeros((128, 128), np.float32)
                for a in range(2):
                    rr = 2 * i + a
                    for b in range(2):
                        rp = 2 * j + b
                        if r0[rp] <= rr < r1[rp]:
                            blk[a * 64:(a + 1) * 64, b * 64:(b + 1) * 64] = colbox
                if i == j:
                    for b in range(2):
                        rp = 2 * j + b
                        cn = (r1[rp] - r0[rp]) * (c1 - c0)
                        blk[b * 64 + np.arange(64), b * 64 + np.arange(64)] -= cn
                if not blk.any():
                    continue
                k = blk.tobytes()
                if k not in key2idx:
                    key2idx[k] = len(mats)
                    mats.append(blk.copy())
                lst.append((i + 2, key2idx[k]))
            pl[j + 2] = lst
        plan.append(pl)
        goff.append(start)
        gcnt.append(len(mats) - start)
    pm = np.stack(mats, 0)
    pm = np.ascontiguousarray(pm.transpose(1, 0, 2)).astype(ml_dtypes.bfloat16)
    return pm, plan, goff, gcnt, icc, irc, tabc


PM, PLAN, GOFF, GCNT, ICC, IRC, TABC = _pool_plan()
NM = PM.shape[1]
NMG = max(GCNT)


class Buf:
    def __init__(self, name, t, nseg, dma_sem, init_rd):
        self.name = name
        self.t = t
        self.nseg = nseg
        self.lw = [None] * nseg
        self.rd = [list(init_rd) for _ in range(nseg)]
        self.dma_sem = dma_sem


class Prog:
    def __init__(self, nc, stack):
        self.nc = nc
        self.stack = stack
        self.sems = {}
        self.count = {}
        self.seen = {e: {} for e in ENGS}
        self.pending = {}
        self.phase_bufs = None
        self.psb = []
        self.psi = 0
        for e in ENGS:
            self.new_sem("E_" + e)

    def new_sem(self, key):
        if key not in self.sems:
            self.sems[key] = self.stack.enter_context(self.nc.semaphore(key))
            self.count[key] = 0
        return key

    def buf(self, name, shape, dtype, nseg=1, dma=False, psum=False, stack=None):
        st = stack if stack is not None else self.stack
        self.uid = getattr(self, "uid", 0) + 1
        tname = "s%d_%s" % (self.uid, name)
        if psum:
            t = st.enter_context(self.nc.psum_tensor(tname, shape, dtype))
        else:
            t = st.enter_context(self.nc.sbuf_tensor(tname, shape, dtype))
        ds = self.new_sem("D_" + name) if dma else None
        init = list(self.pending.items()) if stack is not None else []
        b = Buf(name, t, nseg, ds, init)
        if stack is not None and self.phase_bufs is not None:
            self.phase_bufs.append(b)
        return b

    @contextmanager
    def phase(self):
        old = self.phase_bufs
        self.phase_bufs = []
        with ExitStack() as st:
            yield st
            for b in self.phase_bufs:
                for s in range(b.nseg):
                    for tok in [b.lw[s]] + b.rd[s]:
                        if tok is None:
                            continue
                        k, v = tok
                        if self.pending.get(k, 0) < v:
                            self.pending[k] = v
        self.phase_bufs = old

    def ps(self):
        b = self.psb[self.psi % len(self.psb)]
        self.psi += 1
        return b

    def _expand(self, lst):
        out = []
        for b, s in lst:
            if s is None:
                s = range(b.nseg)
            if isinstance(s, int):
                out.append((b, s))
            else:
                for x in s:
                    out.append((b, x))
        return out

    def _need(self, eng, toks):
        best = {}
        for tok in toks:
            if tok is None:
                continue
            k, v = tok
            if eng == "tensor" and k == "E_tensor":
                continue
            if best.get(k, 0) < v:
                best[k] = v
        e = getattr(self.nc, eng)
        for k, v in best.items():
            if self.seen[eng].get(k, 0) >= v:
                continue
            self.seen[eng][k] = v
            e.wait_ge(self.sems[k], v)

    def _deps(self, eng, reads, writes):
        toks = []
        rl = self._expand(reads)
        wl = self._expand(writes)
        for b, s in rl:
            toks.append(b.lw[s])
        for b, s in wl:
            toks.append(b.lw[s])
            toks.extend(b.rd[s])
        self._need(eng, toks)
        return rl, wl

    def _record(self, tok, rl, wl):
        for b, s in rl:
            b.rd[s].append(tok)
        for b, s in wl:
            b.lw[s] = tok
            b.rd[s] = []

    def op(self, eng, fn, reads=(), writes=(), inc=True):
        rl, wl = self._deps(eng, reads, writes)
        key = "E_" + eng
        ins = fn(getattr(self.nc, eng))
        if inc:
            self.count[key] += 1
            ins.then_inc(self.sems[key], 1)
            v = self.count[key]
        else:
            v = self.count[key] + 1
        self._record((key, v), rl, wl)

    def dma(self, eng, out_ap, in_ap, reads=(), writes=(), sem_buf=None, **kw):
        rl, wl = self._deps(eng, reads, writes)
        key = sem_buf.dma_sem
        if eng == "gpsimd":
            hist = self.__dict__.setdefault("sw_hist", [])
            if len(hist) >= 5:
                self._need(eng, [hist[-5]])
        self.count[key] += 16
        getattr(self.nc, eng).dma_start(out=out_ap, in_=in_ap, **kw).then_inc(self.sems[key], 16)
        if eng == "gpsimd":
            self.sw_hist.append((key, self.count[key]))
        self._record((key, self.count[key]), rl, wl)

    def wait_all(self, eng, bufs):
        toks = []
        for b in bufs:
            for s in range(b.nseg):
                toks.append(b.lw[s])
                toks.extend(b.rd[s])
        self._need(eng, toks)


def hseg(k, tile):
    return k * 5 + tile


def build(NL=DEPTH):
    nc = bass.Bass("TRN2", target_bir_lowering=False, dynamic_dma_scratch_size=8192)
    dt = lambda n, s, d=F32, k="ExternalInput": nc.dram_tensor(n, s, d, kind=k).ap()
    x_d = dt("x", [SEQ, D])
    ctx_d = dt("ctx", [CTX, D])
    vecD_d = dt("vecD", [3 + 4 * DEPTH, D])
    vec5_d = dt("vec5", [DEPTH, NV, BW])
    modw_d = dt("mod_w", [DEPTH, D, 3 * D])
    win_d = dt("w_in", [DEPTH, D, IN_W])
    wout_d = dt("w_out", [DEPTH, 2048, D])
    cpw_d = dt("c_pw", [DEPTH, BW, BW])
    dw_d = dt("d_w", [DEPTH, 4, 128, 128])
    gm_d = dt("gm", [DEPTH, 4, 128, 4, 128])
    pm_d = dt("pmats", [128, NM, 128], BF16)
    icc_d = dt("icc", [128, 4 * 64])
    tabc_d = dt("tabc", [128, 4 * CTX])
    out_d = dt("out", [SEQ, D], F32, "ExternalOutput")
    import os as _os
    DBG = _os.environ.get("KDBG")
    dbg_d = dt("dbg", [128, 8, TT], F32, "ExternalOutput") if DBG else None

    with ExitStack() as st:
        P = Prog(nc, st)
        for i in range(8):
            P.psb.append(P.buf("psb%d" % i, [128, 512], F32, psum=True))

        res = P.buf("res", [128, 8, TT], F32, nseg=40)
        hT = P.buf("hT", [128, 8, TT], BF16, nseg=40)
        cat = P.buf("cat", [128, 8, TT], BF16, nseg=40)
        ident_f = P.buf("ident_f", [128, 128], F32)
        ident_b = P.buf("ident_b", [128, 128], BF16)
        ones1024 = P.buf("ones1024", [128, 128], BF16)
        ones512 = P.buf("ones512", [128, 128], BF16)
        cst = P.buf("cst", [128, 4], F32)
        VD = P.buf("VD", [128, 8, 3 + 4 * DEPTH], F32)
        V = P.buf("V", [128, 4, NVX], F32)
        MT = P.buf("MT", [128, DEPTH, 3, 8, 2], F32, nseg=DEPTH)
        sc = P.buf("sc", [128, 8, 2], BF16)
        mring = [P.buf("mring%d" % i, [128, 8, 128], BF16, dma=True) for i in range(2)]
        modacc = P.buf("modacc", [128, 48], F32)
        mstate = {"pend": [], "ri": 0}
        NR = 8
        ring = [P.buf("ring%d" % i, [128, 8, 128], BF16, dma=True) for i in range(NR)]
        woring = [P.buf("wor%d" % i, [128, 8, 128], BF16, dma=True) for i in range(2)]
        gmring = [P.buf("gmr%d" % i, [128, 4, 128], BF16, dma=True) for i in range(2)]
        dw = P.buf("dw", [128, 4, 128], BF16, dma=True)
        ringi = [0]
        gmi = [0]
        woi = [0]

        EPS6 = cst.t[:, 0:1]
        EPS5 = cst.t[:, 1:2]
        ONE = cst.t[:, 2:3]
        QUART = cst.t[:, 3:4]

        P.op("gpsimd", lambda e: e.memset(ident_f.t[:], 1.0), writes=[(ident_f, 0)])
        P.op("gpsimd", lambda e: e.affine_select(out=ident_f.t[:], in_=ident_f.t[:], pattern=[[-1, 128]],
                                                  compare_op=ALU.is_equal, fill=0.0, base=0, channel_multiplier=1),
             reads=[(ident_f, 0)], writes=[(ident_f, 0)])
        P.op("gpsimd", lambda e: e.tensor_copy(out=ident_b.t[:], in_=ident_f.t[:]), reads=[(ident_f, 0)],
             writes=[(ident_b, 0)])
        P.op("gpsimd", lambda e: e.memset(ones1024.t[:], 1.0 / 1024.0), writes=[(ones1024, 0)])
        P.op("gpsimd", lambda e: e.memset(ones512.t[:], 1.0 / 512.0), writes=[(ones512, 0)])
        P.op("gpsimd", lambda e: e.memset(cst.t[:, 0:1], 1e-6), writes=[(cst, 0)])
        P.op("gpsimd", lambda e: e.memset(cst.t[:, 1:2], 1e-5), writes=[(cst, 0)])
        P.op("gpsimd", lambda e: e.memset(cst.t[:, 2:3], 1.0), writes=[(cst, 0)])
        P.op("gpsimd", lambda e: e.memset(cst.t[:, 3:4], 0.25 + 1.2e-7), writes=[(cst, 0)])

        def kp(ap):
            return ap.rearrange("(k p) c -> p k c", p=128)

        def load_w(l, slot, j):
            b = ring[ringi[0] % NR]
            ringi[0] += 1
            c0 = slot * BW + j * 128
            P.dma("gpsimd", b.t[:], kp(win_d[l, :, c0:c0 + 128]), writes=[(b, 0)], sem_buf=b)
            return b

        def load_gm(l, j):
            b = gmring[gmi[0] % 2]
            gmi[0] += 1
            P.dma("gpsimd", b.t[:], gm_d[l, j], writes=[(b, 0)], sem_buf=b)
            return b

        def mod_issue(l, mc):
            b = mring[mstate["ri"] % 2]
            mstate["ri"] += 1
            P.dma("gpsimd", b.t[:], kp(modw_d[l, :, mc * 128:(mc + 1) * 128]), writes=[(b, 0)], sem_buf=b)
            mstate["pend"].append((mc, b))

        def mod_consume():
            for (mc, b) in mstate["pend"]:
                psm = P.ps()
                for k in range(8):
                    P.op("tensor", lambda e: e.matmul(psm.t[:, 0:2], lhsT=b.t[:, k, :], rhs=sc.t[:, k, 0:2],
                                                      start=(k == 0), stop=(k == 7)),
                         reads=[(b, 0), (sc, 0)], writes=[(psm, 0)], inc=(k == 7))
                P.op("vector", lambda e: e.tensor_copy(out=modacc.t[:, 2 * mc:2 * mc + 2], in_=psm.t[:, 0:2]),
                     reads=[(psm, 0)], writes=[(modacc, 0)])
            mstate["pend"] = []

        def mod_finish(l):
            for j in range(3):
                for col in range(2):
                    tt("vector", MT.t[:, l, j, :, col], modacc.t[:, j * 16 + col:j * 16 + 16:2],
                       VD.t[:, :, 3 + 4 * l + 1 + j], ALU.add, [(modacc, 0), (VD, 0)], [(MT, l)])
            for col in range(2):
                P.op("vector", lambda e: e.scalar_tensor_tensor(out=MT.t[:, l, 1, :, col], in0=MT.t[:, l, 1, :, col],
                                                                scalar=1.0, in1=VD.t[:, :, 3 + 4 * l],
                                                                op0=ALU.add, op1=ALU.mult),
                     reads=[(MT, l), (VD, 0)], writes=[(MT, l)])

        def inproj(wb, tile, ps):
            off, wd = TILES[tile]
            for k in range(8):
                P.op("tensor", lambda e: e.matmul(ps.t[:, 0:wd], lhsT=wb.t[:, k, :], rhs=hT.t[:, k, off:off + wd],
                                                  start=(k == 0), stop=(k == 7)),
                     reads=[(wb, 0), (hT, hseg(k, tile))], writes=[(ps, 0)], inc=(k == 7))

        def act(out, in_, func, reads, writes, bias=None, scale=None):
            kw = {}
            if bias is not None:
                kw["bias"] = bias
            if scale is not None:
                kw["scale"] = scale
            P.op("scalar", lambda e: e.activation(out=out, in_=in_, func=func, **kw), reads=reads, writes=writes)

        def tt(eng, out, in0, in1, op, reads, writes):
            P.op(eng, lambda e: e.tensor_tensor(out=out, in0=in0, in1=in1, op=op), reads=reads, writes=writes)

        def rms_alloc(ph, tag):
            sq = [P.buf("%s_sq%d" % (tag, i), [128, 512], BF16, stack=ph) for i in range(2)]
            rt = P.buf(tag + "_rt", [128, 512], F32, stack=ph)
            rs = [P.buf(tag + "_rs%d" % i, [128, 512], F32, stack=ph) for i in range(2)]
            return sq, rt, rs

        def rms_stats(tile, bufs, it):
            off, wd = TILES[tile]
            sq, rt, rsl = bufs
            rs = rsl[it % 2]
            pss = P.ps()
            for k in range(8):
                s = sq[k % 2]
                act(s.t[:, :wd], res.t[:, k, off:off + wd], AF.Square, [(res, hseg(k, tile))], [(s, 0)])
                P.op("tensor", lambda e: e.matmul(pss.t[:, :wd], lhsT=ones1024.t[:], rhs=s.t[:, :wd],
                                                  start=(k == 0), stop=(k == 7)),
                     reads=[(ones1024, 0), (s, 0)], writes=[(pss, 0)])
            act(rt.t[:, :wd], pss.t[:, :wd], AF.Sqrt, [(pss, 0), (cst, 0)], [(rt, 0)], bias=EPS6)
            P.op("vector", lambda e: e.reciprocal(out=rs.t[:, :wd], in_=rt.t[:, :wd]), reads=[(rt, 0)],
                 writes=[(rs, 0)])
            return rs

        with P.phase() as ph:
            vds = P.buf("vds", [3 + 4 * DEPTH, D], F32, dma=True, stack=ph)
            nvd = 3 + 4 * DEPTH
            P.dma("sync", vds.t[:], vecD_d, writes=[(vds, 0)], sem_buf=vds)
            for k in range(8):
                ps = P.ps()
                P.op("tensor", lambda e: e.transpose(out=ps.t[:, 0:nvd], in_=vds.t[0:nvd, k * 128:(k + 1) * 128],
                                                     identity=ident_f.t[0:nvd, 0:nvd]),
                     reads=[(vds, 0), (ident_f, 0)], writes=[(ps, 0)])
                P.op("vector", lambda e: e.tensor_copy(out=VD.t[:, k, :], in_=ps.t[:, 0:nvd]), reads=[(ps, 0)],
                     writes=[(VD, 0)])
            act(sc.t[:], VD.t[:, :, 0:2], AF.Silu, [(VD, 0)], [(sc, 0)])
            mod_issue(0, 0)
            for mc in range(24):
                if mc + 1 < 24:
                    mod_issue(0, mc + 1)
                pend = mstate["pend"]
                mstate["pend"] = pend[:1]
                mod_consume()
                mstate["pend"] = pend[1:]
            mod_finish(0)
            xst = [P.buf("xst%d" % i, [128, D], F32, dma=True, stack=ph) for i in range(2)]
            for t128 in range(18):
                s_ = xst[t128 % 2]
                src = ctx_d[t128 * 128:(t128 + 1) * 128, :] if t128 < 2 else x_d[(t128 - 2) * 128:(t128 - 1) * 128, :]
                P.dma("sync", s_.t[:], src, writes=[(s_, 0)], sem_buf=s_)
                tile = 0 if t128 < 2 else 1 + (t128 - 2) // 4
                for half in range(2):
                    ps = P.ps()
                    for q in range(4):
                        k = half * 4 + q
                        P.op("tensor", lambda e: e.transpose(out=ps.t[:, q * 128:(q + 1) * 128],
                                                             in_=s_.t[:, k * 128:(k + 1) * 128], identity=ident_f.t[:]),
                             reads=[(s_, 0), (ident_f, 0)], writes=[(ps, 0)], inc=(q == 3))
                    dst = res.t[:, half * 4:(half + 1) * 4, t128 * 128:(t128 + 1) * 128]
                    srcp = ps.t[:, 0:512].rearrange("p (q c) -> p q c", q=4)
                    wr = [(res, hseg(half * 4 + q, tile)) for q in range(4)]
                    if half == 0:
                        P.op("vector", lambda e: e.tensor_copy(out=dst, in_=srcp), reads=[(ps, 0)], writes=wr)
                    else:
                        act(dst, srcp, AF.Identity, [(ps, 0)], wr)

        for l in range(NL):
            last = (l == NL - 1)
            tiles_all = [0, 1, 2, 3, 4]
            tiles_out = [1, 2, 3, 4] if last else tiles_all

            with P.phase() as ph:
                v5s = P.buf("v5s", [NV, BW], F32, dma=True, stack=ph)
                tv = P.buf("tv", [128, 4, 8], F32, stack=ph)
                P.dma("sync", v5s.t[:], vec5_d[l], writes=[(v5s, 0)], sem_buf=v5s)
                for j in range(4):
                    ps = P.ps()
                    P.op("tensor", lambda e: e.transpose(out=ps.t[:, 0:NV], in_=v5s.t[0:NV, j * 128:(j + 1) * 128],
                                                         identity=ident_f.t[0:NV, 0:NV]),
                         reads=[(v5s, 0), (ident_f, 0)], writes=[(ps, 0)])
                    P.op("vector", lambda e: e.tensor_copy(out=V.t[:, j, 0:NV], in_=ps.t[:, 0:NV]), reads=[(ps, 0)],
                         writes=[(V, 0)])
                z = tv.t[:, :, 0:2]
                az = tv.t[:, :, 2:4]
                ee = tv.t[:, :, 4:6]
                P.op("vector", lambda e: e.tensor_scalar(out=z, in0=V.t[:, :, 12:14], scalar1=-1.0, scalar2=None,
                                                         op0=ALU.mult), reads=[(V, 0)], writes=[(tv, 0)])
                act(az, z, AF.Abs, [(tv, 0)], [(tv, 0)])
                act(ee, az, AF.Exp, [(tv, 0)], [(tv, 0)], scale=-1.0)
                act(ee, ee, AF.Ln, [(tv, 0), (cst, 0)], [(tv, 0)], bias=ONE)
                P.op("vector", lambda e: e.scalar_tensor_tensor(out=ee, in0=z, scalar=0.0, in1=ee, op0=ALU.max,
                                                                op1=ALU.add), reads=[(tv, 0)], writes=[(tv, 0)])
                P.op("vector", lambda e: e.tensor_scalar(out=V.t[:, :, 53:55], in0=ee, scalar1=-8.0, scalar2=None,
                                                         op0=ALU.mult), reads=[(tv, 0)], writes=[(V, 0)])
                P.op("vector", lambda e: e.tensor_scalar(out=V.t[:, :, 55:57], in0=ee, scalar1=-16.0, scalar2=None,
                                                         op0=ALU.mult), reads=[(tv, 0)], writes=[(V, 0)])
                tt("vector", V.t[:, :, 57], V.t[:, :, 51], V.t[:, :, 52], ALU.mult, [(V, 0)], [(V, 0)])
                P.op("vector", lambda e: e.tensor_scalar(out=V.t[:, :, 58:62], in0=V.t[:, :, 8:12], scalar1=0.5,
                                                         scalar2=None, op0=ALU.mult), reads=[(V, 0)], writes=[(V, 0)])
                P.op("vector", lambda e: e.tensor_scalar(out=V.t[:, :, 62:64], in0=ee, scalar1=-4.0, scalar2=None,
                                                         op0=ALU.mult), reads=[(tv, 0)], writes=[(V, 0)])
                P.dma("gpsimd", dw.t[:], dw_d[l].rearrange("g i j -> i g j"), writes=[(dw, 0)], sem_buf=dw)

            units = []
            for j in range(4):
                units.append(("A", j, [(S_VA, j), (S_ZA, j)]))
            for j in range(4):
                units.append(("B", j, [(S_BC, j), (S_BV, j), (S_BB, j), (S_ZB, j)]))
            units.append(("WO", 0, []))
            for j in range(4):
                units.append(("C1", j, [(S_CG, j), (S_CA, j)]))
            units.append(("C2", 0, [(S_ZC, 0), (S_ZC, 1), (S_ZC, 2), (S_ZC, 3)]))
            for g in range(4):
                units.append(("D", g, [(S_DV, g), (S_ZD, g)]))
            units.append(("WO", 1, []))
            loaded = {}

            def issue(ui):
                kind, j, ws = units[ui]
                loaded[ui] = [load_w(l, s, jj) for (s, jj) in ws]
                if kind == "A":
                    loaded[ui].append(load_gm(l, j))

            issue(0)

            with P.phase() as ph:
                tmp = [P.buf("n_tmp%d" % i, [128, 512], F32, stack=ph) for i in range(2)]
                rb = rms_alloc(ph, "n")
                for it_, tile in enumerate(tiles_all):
                    off, wd = TILES[tile]
                    col = 1 if tile == 0 else 0
                    rs = rms_stats(tile, rb, it_)
                    for k in range(8):
                        tb = tmp[k % 2]
                        tt("gpsimd" if k % 2 == 0 else "vector", tb.t[:, :wd], res.t[:, k, off:off + wd], rs.t[:, :wd],
                           ALU.mult, [(res, hseg(k, tile)), (rs, 0)], [(tb, 0)])
                        if k in (0, 3, 6):
                            act(hT.t[:, k, off:off + wd], tb.t[:, :wd], AF.Identity, [(tb, 0), (MT, l)],
                                [(hT, hseg(k, tile))], scale=MT.t[:, l, 1, k, col:col + 1],
                                bias=MT.t[:, l, 0, k, col:col + 1])
                        else:
                            P.op("vector", lambda e: e.tensor_scalar(out=hT.t[:, k, off:off + wd], in0=tb.t[:, :wd],
                                                                     scalar1=MT.t[:, l, 1, k, col:col + 1],
                                                                     scalar2=MT.t[:, l, 0, k, col:col + 1],
                                                                     op0=ALU.mult, op1=ALU.add),
                                 reads=[(tb, 0), (MT, l)], writes=[(hT, hseg(k, tile))])

            for ui, (kind, j, ws) in enumerate(units):
                if DBG and l == 0 and ui == int(DBG):
                    with P.phase() as ph:
                        dbb = P.buf("dbgb", [128, TT], F32, dma=True, stack=ph)
                        for kk in range(min(int(DBG), 8)):
                            P.op("vector", lambda e: e.tensor_copy(out=dbb.t[:], in_=cat.t[:, kk, :]),
                                 reads=[(cat, None)], writes=[(dbb, 0)])
                            P.dma("sync", dbg_d[:, kk, :], dbb.t[:], reads=[(dbb, 0)], sem_buf=dbb)
                        P.wait_all("sync", [dbb])
                    return nc
                if l + 1 < NL:
                    mod_consume()
                    nb_ = 2 if ui < 5 else 1
                    mc0 = ui * 2 if ui < 5 else 10 + (ui - 5)
                    for q_ in range(nb_):
                        mod_issue(l + 1, mc0 + q_)
                if ui + 1 < len(units) and kind != "A":
                    issue(ui + 1)
                W = loaded.get(ui, [])

                if kind == "A":
                    Wva, Wza, gm = W
                    PA = 3
                    with P.phase() as ph:
                        va = P.buf("a_va", [128, TT + 3 * PA], BF16, nseg=5, stack=ph)
                        vapad = P.buf("a_vapad", [128, 1], BF16, stack=ph)
                        S = P.buf("a_S", [128, TT], F32, nseg=5, stack=ph)
                        dg = P.buf("a_dg", [128, 8, 128], BF16, stack=ph)
                        xcb = [P.buf("a_xcb%d" % i, [128, 512], BF16, stack=ph) for i in range(2)]
                        T1d = [[P.buf("a_T1%d%d" % (i, p_), [128, 512], F32, stack=ph) for p_ in range(2)]
                               for i in range(2)]
                        T2 = [P.buf("a_T2%d" % i, [128, 512], F32, stack=ph) for i in range(2)]
                        T3d = [[P.buf("a_T3%d%d" % (i, p_), [128, 512], F32, stack=ph) for p_ in range(2)]
                               for i in range(2)]
                        zsl = [P.buf("a_zs%d" % i, [128, 512], BF16, stack=ph) for i in range(2)]
                        stbuf = [[P.buf("a_st%d%d" % (n_, p_), [128, 1], F32, stack=ph) for p_ in range(2)]
                                 for n_ in range(2)]
                        colA = lambda tile: TILES[tile][0] + (PA if tile == 0 else 2 * PA)
                        for i in range(8):
                            P.op("gpsimd", lambda e: e.tensor_scalar(out=dg.t[:, i, :], in0=ident_b.t[:],
                                                                     scalar1=V.t[:, j, i:i + 1], scalar2=1.0,
                                                                     op0=ALU.mult, op1=ALU.mult),
                                 reads=[(ident_b, 0), (V, 0)], writes=[(dg, 0)])
                        for (p0, p1) in ((0, PA), (PA + CTX, 2 * PA + CTX), (2 * PA + TT, 3 * PA + TT)):
                            P.op("gpsimd", lambda e: e.memset(va.t[:, p0:p1], 0.0), writes=[(vapad, 0)])
                        if ui + 1 < len(units):
                            issue(ui + 1)
                        for it, tile in enumerate(tiles_all):
                            off, wd = TILES[tile]
                            ps = P.ps()
                            inproj(Wva, tile, ps)
                            c0 = colA(tile)
                            if it % 2 == 0:
                                act(va.t[:, c0:c0 + wd], ps.t[:, :wd], AF.Identity, [(ps, 0)], [(va, tile)])
                            else:
                                P.op("vector", lambda e: e.tensor_copy(out=va.t[:, c0:c0 + wd], in_=ps.t[:, :wd]),
                                     reads=[(ps, 0)], writes=[(va, tile)])

                        def nbr(tile):
                            if tile == 0:
                                return [0]
                            return [t for t in (tile - 1, tile, tile + 1) if 1 <= t <= 4]

                        FW = [0, 1, 2, 3, 4]
                        BK = [0, 4, 3, 2, 1]
                        prev = [0.0, 0.0]
                        prev_dep = [[], []]
                        ST = {}

                        def front(it):
                            tl = [FW[it], BK[it]]
                            psc = [None, None]
                            psr = [None, None]
                            psi_ = [None, None]
                            for n in range(2):
                                tile = tl[n]
                                off, wd = TILES[tile]
                                c0 = colA(tile)
                                psc[n] = P.ps()
                                for k in range(4):
                                    sh = (c0 - 3 + k) if n == 0 else (c0 + k)
                                    P.op("tensor", lambda e: e.matmul(psc[n].t[:, :wd], lhsT=dg.t[:, n * 4 + k, :],
                                                                      rhs=va.t[:, sh:sh + wd], start=(k == 0), stop=(k == 3)),
                                         reads=[(dg, 0), (vapad, 0), (va, nbr(tile))], writes=[(psc[n], 0)], inc=(k == 3))
                            for n in range(2):
                                wd = TILES[tl[n]][1]
                                P.op("vector", lambda e: e.tensor_copy(out=xcb[n].t[:, :wd], in_=psc[n].t[:, :wd]),
                                     reads=[(psc[n], 0)], writes=[(xcb[n], 0)])
                            for n in range(2):
                                wd = TILES[tl[n]][1]
                                psr[n] = P.ps()
                                P.op("tensor", lambda e: e.matmul(psr[n].t[:, :wd], lhsT=gm.t[:, n * 2, :],
                                                                  rhs=xcb[n].t[:, :wd], start=True, stop=True),
                                     reads=[(gm, 0), (xcb[n], 0)], writes=[(psr[n], 0)])
                                psi_[n] = P.ps()
                                P.op("tensor", lambda e: e.matmul(psi_[n].t[:, :wd], lhsT=gm.t[:, n * 2 + 1, :],
                                                                  rhs=xcb[n].t[:, :wd], start=True, stop=True),
                                     reads=[(gm, 0), (xcb[n], 0)], writes=[(psi_[n], 0)])
                            if it == 0:
                                comb = [(0, 1)] if 0 in tiles_out else []
                            elif it == 3:
                                comb = [(3, 0), (2, 1)]
                            elif it == 4:
                                comb = [(4, 0), (1, 1)]
                            else:
                                comb = []
                            pszl = []
                            for (ctile, cn) in comb:
                                psz = P.ps()
                                inproj(Wza, ctile, psz)
                                pszl.append(psz)
                            ST[it] = (tl, psc, psr, psi_, comb, pszl)

                        def mid(it):
                            tl, psc, psr, psi_, comb, pszl = ST[it]
                            T1 = [T1d[0][it % 2], T1d[1][it % 2]]
                            T3 = [T3d[0][it % 2], T3d[1][it % 2]]
                            for n in range(2):
                                wd = TILES[tl[n]][1]
                                act(T1[n].t[:, :wd], psr[n].t[:, :wd], AF.Tanh, [(psr[n], 0), (V, 0)], [(T1[n], 0)],
                                    scale=0.5, bias=V.t[:, j, 58 + n:59 + n])
                                act(T3[n].t[:, :wd], psi_[n].t[:, :wd], AF.Tanh, [(psi_[n], 0), (V, 0)], [(T3[n], 0)],
                                    scale=0.5, bias=V.t[:, j, 60 + n:61 + n])
                            for n in range(2):
                                wd = TILES[tl[n]][1]
                                act(T1[n].t[:, :wd], T1[n].t[:, :wd], AF.Exp, [(T1[n], 0), (V, 0)], [(T1[n], 0)],
                                    scale=V.t[:, j, 62 + n:63 + n], bias=V.t[:, j, 62 + n:63 + n])
                                tt("gpsimd", T2[n].t[:, :wd], T1[n].t[:, :wd], T1[n].t[:, :wd], ALU.mult, [(T1[n], 0)],
                                   [(T2[n], 0)])
                            for ci, (ctile, cn) in enumerate(comb):
                                wd = TILES[ctile][1]
                                act(zsl[ci].t[:, :wd], pszl[ci].t[:, :wd], AF.Tanh, [(pszl[ci], 0)], [(zsl[ci], 0)],
                                    scale=0.5)
                            for n in range(2):
                                wd = TILES[tl[n]][1]
                                P.op("vector", lambda e: e.scalar_tensor_tensor(out=T3[n].t[:, :wd], in0=T3[n].t[:, :wd],
                                                                                scalar=1.0, in1=psc[n].t[:, :wd],
                                                                                op0=ALU.add, op1=ALU.mult),
                                     reads=[(T3[n], 0), (psc[n], 0)], writes=[(T3[n], 0)])
                            for ci, (ctile, cn) in enumerate(comb):
                                wd = TILES[ctile][1]
                                P.op("vector", lambda e: e.scalar_tensor_tensor(out=zsl[ci].t[:, :wd],
                                                                                in0=zsl[ci].t[:, :wd],
                                                                                scalar=1.0, in1=pszl[ci].t[:, :wd],
                                                                                op0=ALU.add, op1=ALU.mult),
                                     reads=[(zsl[ci], 0), (pszl[ci], 0)], writes=[(zsl[ci], 0)])
                            for n in range(2):
                                wd = TILES[tl[n]][1]
                                act(T2[n].t[:, :wd], T2[n].t[:, :wd], AF.Sqrt, [(T2[n], 0), (cst, 0)], [(T2[n], 0)],
                                    scale=-0.25, bias=QUART)

                        def back(it):
                            tl, psc, psr, psi_, comb, pszl = ST[it]
                            T1 = [T1d[0][it % 2], T1d[1][it % 2]]
                            T3 = [T3d[0][it % 2], T3d[1][it % 2]]
                            for n in range(2):
                                wd = TILES[tl[n]][1]
                                tt("gpsimd", T3[n].t[:, :wd], T3[n].t[:, :wd], T2[n].t[:, :wd], ALU.mult,
                                   [(T3[n], 0), (T2[n], 0)], [(T3[n], 0)])
                            for n in range(2):
                                tile = tl[n]
                                off, wd = TILES[tile]
                                store_n = (n == 0) if it == 0 else (it in (1, 2))
                                if store_n:
                                    o_ap = S.t[:, off:off + wd]
                                    wr = [(S, tile)]
                                else:
                                    o_ap = T3[n].t[:, 0:wd]
                                    wr = [(T3[n], 0)]
                                d0 = T1[n].t[:, 0:wd]
                                d1 = T3[n].t[:, 0:wd]
                                if n == 1:
                                    o_ap, d0, d1 = o_ap[:, ::-1], d0[:, ::-1], d1[:, ::-1]
                                init = prev[n]
                                P.op("vector", lambda e: e.tensor_tensor_scan(out=o_ap, data0=d0, data1=d1, initial=init,
                                                                              op0=ALU.mult, op1=ALU.add),
                                     reads=[(T1[n], 0), (T3[n], 0)] + prev_dep[n], writes=wr)
                                stc = stbuf[n][it % 2]
                                col = (wd - 1) if n == 0 else 0
                                if store_n:
                                    src = S.t[:, off + col:off + col + 1]
                                else:
                                    src = T3[n].t[:, col:col + 1]
                                P.op("vector", lambda e: e.tensor_copy(out=stc.t[:, 0:1], in_=src), reads=wr,
                                     writes=[(stc, 0)])
                                prev[n] = stc.t[:, 0:1]
                                prev_dep[n] = [(stc, 0)]
                            for ci, (ctile, cn) in enumerate(comb):
                                off, wd = TILES[ctile]
                                tt("gpsimd", T3[cn].t[:, :wd], T3[cn].t[:, :wd], S.t[:, off:off + wd], ALU.add,
                                   [(T3[cn], 0), (S, ctile)], [(T3[cn], 0)])
                                P.op("vector", lambda e: e.scalar_tensor_tensor(out=cat.t[:, j, off:off + wd],
                                                                                in0=T3[cn].t[:, :wd], scalar=0.5,
                                                                                in1=zsl[ci].t[:, :wd], op0=ALU.mult,
                                                                                op1=ALU.mult),
                                     reads=[(T3[cn], 0), (zsl[ci], 0)], writes=[(cat, hseg(j, ctile))])

                        front(0)
                        for it in range(5):
                            mid(it)
                            if it + 1 < 5:
                                front(it + 1)
                            back(it)

                elif kind == "B":
                    Wbc, Wbv, Wbb, Wzb = W
                    with P.phase() as ph:
                        prod = P.buf("b_prod", [128, TT + 3], BF16, nseg=5, stack=ph)
                        dg = P.buf("b_dg", [128, 3, 128], BF16, stack=ph)
                        bcs = [P.buf("b_bcs%d" % i, [128, 512], F32, stack=ph) for i in range(2)]
                        zs = [P.buf("b_zs%d" % i, [128, 512], F32, stack=ph) for i in range(2)]
                        gb = [P.buf("b_g%d" % i, [128, 512], F32, stack=ph) for i in range(2)]
                        colB = lambda tile: TILES[tile][0] + (1 if tile == 0 else 2)
                        P.op("gpsimd", lambda e: e.memset(prod.t[:], 0.0), writes=[(prod, None)])
                        for i in range(3):
                            P.op("gpsimd", lambda e: e.tensor_scalar(out=dg.t[:, i, :], in0=ident_b.t[:],
                                                                     scalar1=V.t[:, j, 14 + i:15 + i], scalar2=1.0,
                                                                     op0=ALU.mult, op1=ALU.mult),
                                 reads=[(ident_b, 0), (V, 0)], writes=[(dg, 0)])
                        for it, tile in enumerate(tiles_out):
                            off, wd = TILES[tile]
                            c0 = colB(tile)
                            ps1 = P.ps()
                            inproj(Wbc, tile, ps1)
                            ps2 = P.ps()
                            inproj(Wbv, tile, ps2)
                            b_ = bcs[it % 2]
                            act(b_.t[:, :wd], ps1.t[:, :wd], AF.Identity, [(ps1, 0)], [(b_, 0)])
                            tt("vector", prod.t[:, c0:c0 + wd], ps2.t[:, :wd], b_.t[:, :wd], ALU.mult,
                               [(ps2, 0), (b_, 0)], [(prod, tile)])

                        def nbrB(tile):
                            if tile == 0:
                                return [0]
                            return [t for t in (tile - 1, tile, tile + 1) if 1 <= t <= 4]

                        for it, tile in enumerate(tiles_out):
                            off, wd = TILES[tile]
                            c0 = colB(tile)
                            ps3 = P.ps()
                            for k in range(3):
                                sh = c0 - 1 + k
                                P.op("tensor", lambda e: e.matmul(ps3.t[:, :wd], lhsT=dg.t[:, k, :],
                                                                  rhs=prod.t[:, sh:sh + wd], start=(k == 0), stop=(k == 2)),
                                     reads=[(dg, 0), (prod, nbrB(tile))], writes=[(ps3, 0)], inc=(k == 2))
                            ps4 = P.ps()
                            inproj(Wbb, tile, ps4)
                            ps5 = P.ps()
                            inproj(Wzb, tile, ps5)
                            z_ = zs[it % 2]
                            g_ = gb[it % 2]
                            act(z_.t[:, :wd], ps5.t[:, :wd], AF.Silu, [(ps5, 0)], [(z_, 0)])
                            tt("vector", g_.t[:, :wd], ps4.t[:, :wd], z_.t[:, :wd], ALU.mult, [(ps4, 0), (z_, 0)],
                               [(g_, 0)])
                            tt("vector", cat.t[:, 4 + j, off:off + wd], ps3.t[:, :wd], g_.t[:, :wd], ALU.mult,
                               [(ps3, 0), (g_, 0)], [(cat, hseg(4 + j, tile))])

                elif kind == "C1":
                    Wcg, Wca = W
                    PC = 15
                    with P.phase() as ph:
                        glu = P.buf("c_glu", [128, TT + 3 * PC], BF16, nseg=5, stack=ph)
                        dg = P.buf("c_dg", [128, 31, 128], BF16, stack=ph)
                        sg = [P.buf("c_sg%d" % i, [128, 512], F32, stack=ph) for i in range(2)]
                        colC = lambda tile: TILES[tile][0] + (PC if tile == 0 else 2 * PC)
                        P.op("gpsimd", lambda e: e.memset(glu.t[:], 0.0), writes=[(glu, None)])
                        for i in range(6, 31):
                            P.op("gpsimd", lambda e: e.tensor_scalar(out=dg.t[:, i, :], in0=ident_b.t[:],
                                                                     scalar1=V.t[:, j, 17 + i:18 + i], scalar2=1.0,
                                                                     op0=ALU.mult, op1=ALU.mult),
                                 reads=[(ident_b, 0), (V, 0)], writes=[(dg, 0)])
                        for it, tile in enumerate(tiles_out):
                            off, wd = TILES[tile]
                            c0 = colC(tile)
                            ps1 = P.ps()
                            inproj(Wcg, tile, ps1)
                            ps2 = P.ps()
                            inproj(Wca, tile, ps2)
                            s_ = sg[it % 2]
                            act(s_.t[:, :wd], ps1.t[:, :wd], AF.Sigmoid, [(ps1, 0)], [(s_, 0)])
                            tt("vector", glu.t[:, c0:c0 + wd], ps2.t[:, :wd], s_.t[:, :wd], ALU.mult,
                               [(ps2, 0), (s_, 0)], [(glu, tile)])

                        def nbrC(tile):
                            if tile == 0:
                                return [0]
                            return [t for t in (tile - 1, tile, tile + 1) if 1 <= t <= 4]

                        NDV = 6
                        acc = [P.buf("c_acc%d" % i, [128, 512], F32, stack=ph) for i in range(2)]
                        for it, tile in enumerate(tiles_out):
                            off, wd = TILES[tile]
                            c0 = colC(tile)
                            ps3 = P.ps()
                            for k in range(NDV, 31):
                                sh = c0 - 15 + k
                                P.op("tensor", lambda e: e.matmul(ps3.t[:, :wd], lhsT=dg.t[:, k, :],
                                                                  rhs=glu.t[:, sh:sh + wd], start=(k == NDV), stop=(k == 30)),
                                     reads=[(dg, 0), (glu, nbrC(tile))], writes=[(ps3, 0)], inc=(k == 30))
                            ac = acc[it % 2]
                            for k in range(NDV):
                                sh = c0 - 15 + k
                                if k == 0:
                                    P.op("vector", lambda e: e.tensor_scalar(out=ac.t[:, :wd], in0=glu.t[:, sh:sh + wd],
                                                                             scalar1=V.t[:, j, 17 + k:18 + k], scalar2=None,
                                                                             op0=ALU.mult),
                                         reads=[(glu, nbrC(tile)), (V, 0)], writes=[(ac, 0)])
                                else:
                                    P.op("vector", lambda e: e.scalar_tensor_tensor(out=ac.t[:, :wd],
                                                                                    in0=glu.t[:, sh:sh + wd],
                                                                                    scalar=V.t[:, j, 17 + k:18 + k],
                                                                                    in1=ac.t[:, :wd], op0=ALU.mult,
                                                                                    op1=ALU.add),
                                         reads=[(glu, nbrC(tile)), (V, 0), (ac, 0)], writes=[(ac, 0)])
                            tt("vector", cat.t[:, 4 + j, off:off + wd], ps3.t[:, :wd], ac.t[:, :wd], ALU.add,
                               [(ps3, 0), (ac, 0)], [(cat, hseg(4 + j, tile))])

                elif kind == "C2":
                    Wzc = W
                    with P.phase() as ph:
                        cpw = P.buf("c_pw", [128, 4, BW], BF16, dma=True, stack=ph)
                        P.dma("gpsimd", cpw.t[:], kp(cpw_d[l]), writes=[(cpw, 0)], sem_buf=cpw)
                        usq = [P.buf("c_usq%d" % i, [128, 512], BF16, stack=ph) for i in range(2)]
                        meanl = [P.buf("c_mean%d" % i, [128, 512], F32, stack=ph) for i in range(2)]
                        m2l = [P.buf("c_m2%d" % i, [128, 512], F32, stack=ph) for i in range(2)]
                        rstdl = [P.buf("c_rstd%d" % i, [128, 512], F32, stack=ph) for i in range(2)]
                        tb = [P.buf("c_t%d" % i, [128, 512], F32, stack=ph) for i in range(2)]
                        unl = [P.buf("c_un%d" % i, [128, 4, 512], BF16, nseg=4, stack=ph) for i in range(2)]
                        zs = [P.buf("c_zs%d" % i, [128, 512], F32, stack=ph) for i in range(2)]

                        def partA(it, tile):
                            off, wd = TILES[tile]
                            mean, m2, rstd, un = meanl[it % 2], m2l[it % 2], rstdl[it % 2], unl[it % 2]
                            psm = P.ps()
                            psq = P.ps()
                            for jj in range(4):
                                u_ap = cat.t[:, 4 + jj, off:off + wd]
                                useg = (cat, hseg(4 + jj, tile))
                                P.op("tensor", lambda e: e.matmul(psm.t[:, :wd], lhsT=ones512.t[:], rhs=u_ap,
                                                                  start=(jj == 0), stop=(jj == 3)),
                                     reads=[(ones512, 0), useg], writes=[(psm, 0)], inc=(jj == 3))
                                q_ = usq[jj % 2]
                                act(q_.t[:, :wd], u_ap, AF.Square, [useg], [(q_, 0)])
                                P.op("tensor", lambda e: e.matmul(psq.t[:, :wd], lhsT=ones512.t[:], rhs=q_.t[:, :wd],
                                                                  start=(jj == 0), stop=(jj == 3)),
                                     reads=[(ones512, 0), (q_, 0)], writes=[(psq, 0)])
                            act(mean.t[:, :wd], psm.t[:, :wd], AF.Identity, [(psm, 0)], [(mean, 0)])
                            tt("gpsimd", m2.t[:, :wd], mean.t[:, :wd], mean.t[:, :wd], ALU.mult, [(mean, 0)], [(m2, 0)])
                            tt("vector", m2.t[:, :wd], psq.t[:, :wd], m2.t[:, :wd], ALU.subtract, [(psq, 0), (m2, 0)],
                               [(m2, 0)])
                            act(m2.t[:, :wd], m2.t[:, :wd], AF.Sqrt, [(m2, 0), (cst, 0)], [(m2, 0)], bias=EPS5)
                            P.op("vector", lambda e: e.reciprocal(out=rstd.t[:, :wd], in_=m2.t[:, :wd]),
                                 reads=[(m2, 0)], writes=[(rstd, 0)])
                            for jj in range(4):
                                t_ = tb[jj % 2]
                                tt("gpsimd", t_.t[:, :wd], cat.t[:, 4 + jj, off:off + wd], mean.t[:, :wd], ALU.subtract,
                                   [(cat, hseg(4 + jj, tile)), (mean, 0)], [(t_, 0)])
                                tt("vector", t_.t[:, :wd], t_.t[:, :wd], rstd.t[:, :wd], ALU.mult, [(t_, 0), (rstd, 0)],
                                   [(t_, 0)])
                                act(un.t[:, jj, :wd], t_.t[:, :wd], AF.Silu, [(t_, 0), (V, 0)], [(un, jj)],
                                    scale=V.t[:, jj, 48:49], bias=V.t[:, jj, 49:50])

                        def partB(it, tile):
                            off, wd = TILES[tile]
                            un = unl[it % 2]
                            for m in range(4):
                                psp = P.ps()
                                for kc in range(4):
                                    P.op("tensor", lambda e: e.matmul(psp.t[:, :wd], lhsT=cpw.t[:, kc, m * 128:(m + 1) * 128],
                                                                      rhs=un.t[:, kc, :wd], start=(kc == 0), stop=(kc == 3)),
                                         reads=[(cpw, 0), (un, kc)], writes=[(psp, 0)], inc=(kc == 3))
                                psz = P.ps()
                                inproj(Wzc[m], tile, psz)
                                z_ = zs[m % 2]
                                act(z_.t[:, :wd], psz.t[:, :wd], AF.Silu, [(psz, 0)], [(z_, 0)])
                                P.op("vector", lambda e: e.scalar_tensor_tensor(out=cat.t[:, m, off:off + wd],
                                                                                in0=psp.t[:, :wd],
                                                                                scalar=V.t[:, m, 50:51], in1=z_.t[:, :wd],
                                                                                op0=ALU.add, op1=ALU.mult),
                                     reads=[(psp, 0), (z_, 0), (V, 0)], writes=[(cat, hseg(m, tile))])

                        tl_ = list(tiles_out)
                        partA(0, tl_[0])
                        for it in range(len(tl_)):
                            if it + 1 < len(tl_):
                                partA(it + 1, tl_[it + 1])
                            partB(it, tl_[it])

                elif kind == "D":
                    g = j
                    Wdv, Wzd = W
                    with P.phase() as ph:
                        dvtm = P.buf("d_dvtm", [128, 18, 128], BF16, nseg=18, stack=ph)
                        pmg = P.buf("d_pm", [128, NMG, 128], BF16, dma=True, stack=ph)
                        pT = [P.buf("d_pT%d" % i, [128, 512], BF16, stack=ph) for i in range(2)]
                        zs = [P.buf("d_zs%d" % i, [128, 512], F32, stack=ph) for i in range(2)]
                        tb = [P.buf("d_t%d" % i, [128, 512], F32, stack=ph) for i in range(2)]
                        ng = GCNT[g]
                        icc = P.buf("d_icc", [128, 64], F32, dma=True, stack=ph)
                        tabc = P.buf("d_tabc", [128, CTX], F32, dma=True, stack=ph)
                        P.dma("sync", icc.t[:], icc_d[:, g * 64:(g + 1) * 64], writes=[(icc, 0)], sem_buf=icc)
                        P.dma("sync", tabc.t[:], tabc_d[:, g * CTX:(g + 1) * CTX], writes=[(tabc, 0)], sem_buf=tabc)
                        P.dma("sync", pmg.t[:, 0:ng, :], pm_d[:, GOFF[g]:GOFF[g] + ng, :], writes=[(pmg, 0)], sem_buf=pmg)
                        t128s = list(range(0 if not last else 2, 18))
                        grp = []
                        cur = []
                        for t1 in t128s:
                            cur.append(t1)
                            if len(cur) == 4 or t1 == t128s[-1] or t1 == 1:
                                grp.append(cur)
                                cur = []
                        for gi, lst in enumerate(grp):
                            ps = P.ps()
                            for q, t1 in enumerate(lst):
                                tile = 0 if t1 < 2 else 1 + (t1 - 2) // 4
                                for k in range(8):
                                    P.op("tensor", lambda e: e.matmul(ps.t[:, q * 128:(q + 1) * 128],
                                                                      lhsT=hT.t[:, k, t1 * 128:(t1 + 1) * 128],
                                                                      rhs=Wdv.t[:, k, :], start=(k == 0), stop=(k == 7)),
                                         reads=[(Wdv, 0), (hT, hseg(k, tile))], writes=[(ps, 0)],
                                         inc=(k == 7 and q == len(lst) - 1))
                            n_ = len(lst)
                            dst = dvtm.t[:, lst[0]:lst[0] + n_, :]
                            srcp = ps.t[:, 0:n_ * 128].rearrange("p (q c) -> p q c", q=n_)
                            if gi % 2 == 0:
                                act(dst, srcp, AF.Identity, [(ps, 0)], [(dvtm, lst)])
                            else:
                                P.op("vector", lambda e: e.tensor_copy(out=dst, in_=srcp), reads=[(ps, 0)],
                                     writes=[(dvtm, lst)])
                        for it, tile in enumerate(tiles_out):
                            off, wd = TILES[tile]
                            n128 = wd // 128
                            t0 = off // 128
                            ps = P.ps()
                            nmm = sum(len(PLAN[g][t0 + q]) for q in range(n128))
                            cnt = 0
                            for q in range(n128):
                                lst = PLAN[g][t0 + q]
                                for idx, (i_, m_) in enumerate(lst):
                                    cnt += 1
                                    ml = m_ - GOFF[g]
                                    P.op("tensor", lambda e: e.matmul(ps.t[:, q * 128:(q + 1) * 128],
                                                                      lhsT=dvtm.t[:, i_, :], rhs=pmg.t[:, ml, :],
                                                                      start=(idx == 0), stop=(idx == len(lst) - 1)),
                                         reads=[(dvtm, i_), (pmg, 0)], writes=[(ps, 0)], inc=(cnt == nmm))
                            p_ = pT[it % 2]
                            ps3 = P.ps()
                            inproj(Wzd, tile, ps3)
                            if tile == 0:
                                tt("vector", p_.t[:, :wd], ps.t[:, :wd], tabc.t[:, :], ALU.mult,
                                   [(ps, 0), (tabc, 0)], [(p_, 0)])
                            else:
                                rbase = (off - CTX) // GRID_W
                                a = 0
                                while a < 8:
                                    b = a
                                    while b + 1 < 8 and IRC[g][rbase + b + 1] == IRC[g][rbase + a]:
                                        b += 1
                                    nr = b - a + 1
                                    sc_ = float(IRC[g][rbase + a])
                                    o_ap = p_.t[:, a * 64:(b + 1) * 64].rearrange("p (r c) -> p r c", c=64)
                                    i_ap = ps.t[:, a * 64:(b + 1) * 64].rearrange("p (r c) -> p r c", c=64)
                                    t_ap = icc.t[:, :].unsqueeze(1).broadcast_to([128, nr, 64])
                                    P.op("vector", lambda e: e.scalar_tensor_tensor(out=o_ap, in0=i_ap, scalar=sc_,
                                                                                    in1=t_ap, op0=ALU.mult, op1=ALU.mult),
                                         reads=[(ps, 0), (icc, 0)], writes=[(p_, 0)])
                                    a = b + 1
                            ps2 = P.ps()
                            P.op("tensor", lambda e: e.matmul(ps2.t[:, :wd], lhsT=dw.t[:, g, :], rhs=p_.t[:, :wd],
                                                              start=True, stop=True),
                                 reads=[(dw, 0), (p_, 0)], writes=[(ps2, 0)])
                            z_ = zs[it % 2]
                            t_ = tb[it % 2]
                            act(z_.t[:, :wd], ps3.t[:, :wd], AF.Silu, [(ps3, 0)], [(z_, 0)])
                            act(t_.t[:, :wd], ps2.t[:, :wd], AF.Identity, [(ps2, 0), (V, 0)], [(t_, 0)],
                                scale=V.t[:, g, 52:53], bias=V.t[:, g, 57:58])
                            tt("gpsimd", cat.t[:, 4 + g, off:off + wd], t_.t[:, :wd], z_.t[:, :wd], ALU.mult,
                               [(t_, 0), (z_, 0)], [(cat, hseg(4 + g, tile))])

                elif kind == "WO":
                    half = j

                    def load_wo(m):
                        b = woring[woi[0] % 2]
                        woi[0] += 1
                        P.dma("gpsimd", b.t[:], kp(wout_d[l, half * 1024:(half + 1) * 1024, m * 128:(m + 1) * 128]),
                              writes=[(b, 0)], sem_buf=b)
                        return b

                    nxt = load_wo(0)
                    for m in range(8):
                        wo = nxt
                        if m + 1 < 8:
                            nxt = load_wo(m + 1)
                        for tile in tiles_out:
                            off, wd = TILES[tile]
                            col = 1 if tile == 0 else 0
                            ps = P.ps()
                            for k in range(8):
                                P.op("tensor", lambda e: e.matmul(ps.t[:, :wd], lhsT=wo.t[:, k, :],
                                                                  rhs=cat.t[:, k, off:off + wd], start=(k == 0), stop=(k == 7)),
                                     reads=[(wo, 0), (cat, hseg(k, tile))], writes=[(ps, 0)], inc=(k == 7))
                            P.op("vector", lambda e: e.scalar_tensor_tensor(out=res.t[:, m, off:off + wd], in0=ps.t[:, :wd],
                                                                            scalar=MT.t[:, l, 2, m, col:col + 1],
                                                                            in1=res.t[:, m, off:off + wd],
                                                                            op0=ALU.mult, op1=ALU.add),
                                 reads=[(ps, 0), (MT, l), (res, hseg(m, tile))], writes=[(res, hseg(m, tile))])

            if l + 1 < NL:
                mod_consume()
                mod_finish(l + 1)

        with P.phase() as ph:
            tmp = [P.buf("f_tmp%d" % i, [128, 512], F32, stack=ph) for i in range(2)]
            ost = [P.buf("f_ost%d" % i, [128, D], F32, dma=True, stack=ph) for i in range(2)]
            oi = 0
            rb = rms_alloc(ph, "f")
            for it_, tile in enumerate([1, 2, 3, 4]):
                off, wd = TILES[tile]
                rs = rms_stats(tile, rb, it_)
                for k in range(8):
                    tb = tmp[k % 2]
                    tt("gpsimd", tb.t[:, :wd], res.t[:, k, off:off + wd], rs.t[:, :wd], ALU.mult,
                       [(res, hseg(k, tile)), (rs, 0)], [(tb, 0)])
                    act(res.t[:, k, off:off + wd], tb.t[:, :wd], AF.Identity, [(tb, 0), (VD, 0)],
                        [(res, hseg(k, tile))], scale=VD.t[:, k, 2:3])
                for q in range(4):
                    o_ = ost[oi % 2]
                    oi += 1
                    c0 = off + q * 128
                    for half in range(2):
                        ps = P.ps()
                        for kk in range(4):
                            k = half * 4 + kk
                            P.op("tensor", lambda e: e.transpose(out=ps.t[:, kk * 128:(kk + 1) * 128],
                                                                 in_=res.t[:, k, c0:c0 + 128], identity=ident_f.t[:]),
                                 reads=[(res, hseg(k, tile)), (ident_f, 0)], writes=[(ps, 0)], inc=(kk == 3))
                        if half == 0:
                            P.op("vector", lambda e: e.tensor_copy(out=o_.t[:, 0:512], in_=ps.t[:, 0:512]),
                                 reads=[(ps, 0)], writes=[(o_, 0)])
                        else:
                            act(o_.t[:, 512:1024], ps.t[:, 0:512], AF.Identity, [(ps, 0)], [(o_, 0)])
                    r0 = (tile - 1) * 512 + q * 128
                    P.dma("sync", out_d[r0:r0 + 128, :], o_.t[:], reads=[(o_, 0)], sem_buf=o_)
            P.wait_all("sync", ost)
    return nc


def prep_inputs(inp, NL=DEPTH):
    f = lambda a: np.ascontiguousarray(np.asarray(a, dtype=np.float32))
    x = f(inp["x"])
    B = x.shape[0]
    c = f(inp["c"])
    ctx = f(inp["ctx"])
    c_ctx = f(inp["c_ctx"])
    mod_b = f(inp["mod_b"])
    norm_g = f(inp["norm_g"])
    final_g = f(inp["final_g"])
    vec5 = np.zeros((DEPTH, NV, BW), np.float32)
    gm = np.zeros((DEPTH, 4, 128, 4, 128), np.float32)
    a_conv, a_br, a_bi, a_lam = f(inp["a_conv"]), f(inp["a_br"]), f(inp["a_bi"]), f(inp["a_lam"])
    b_conv, c_conv = f(inp["b_conv"]), f(inp["c_conv"])
    a_wr, a_wi = f(inp["a_wr"]), f(inp["a_wi"])
    nl = a_conv.shape[0]
    for l in range(nl):
        vec5[l, 0:4] = a_conv[l, 0]
        vec5[l, 4:8] = a_conv[l, 1]
        vec5[l, 8:10] = a_br[l]
        vec5[l, 10:12] = a_bi[l]
        vec5[l, 12:14] = a_lam[l]
        vec5[l, 14:17] = b_conv[l]
        vec5[l, 17:48] = c_conv[l]
        vec5[l, 48] = f(inp["c_ln_g"])[l]
        vec5[l, 49] = f(inp["c_ln_b"])[l]
        vec5[l, 50] = f(inp["c_pw_b"])[l]
        vec5[l, 51] = f(inp["d_b"])[l].reshape(-1)
        vec5[l, 52] = f(inp["d_scale"])[l]
        for j in range(4):
            for n in range(2):
                for gi, wsrc in enumerate((a_wr, a_wi)):
                    for hh in range(2):
                        gm[l, j, hh * 64:(hh + 1) * 64, n * 2 + gi, hh * 64:(hh + 1) * 64] = wsrc[l, n, 2 * j + hh]
    shared = {
        "vec5": vec5, "gm": gm,
        "mod_w": f(inp["mod_w"]), "w_in": f(inp["w_in"]), "w_out": f(inp["w_out"]),
        "c_pw": f(inp["c_pw"]), "d_w": f(inp["d_w"]),
        "pmats": PM,
        "icc": np.ascontiguousarray(np.broadcast_to(ICC.reshape(1, -1), (128, 256))).astype(np.float32),
        "tabc": np.ascontiguousarray(np.broadcast_to(TABC.reshape(1, -1), (128, 4 * CTX))).astype(np.float32),
    }
    for k_ in ("mod_w", "w_in", "w_out", "c_pw", "d_w"):
        a = shared[k_]
        if a.shape[0] < DEPTH:
            pad = np.zeros((DEPTH - a.shape[0],) + a.shape[1:], np.float32)
            shared[k_] = np.concatenate([a, pad], 0)
    maps = []
    for b in range(B):
        vecD = np.zeros((3 + 4 * DEPTH, D), np.float32)
        vecD[0] = c[b]
        vecD[1] = c_ctx
        vecD[2] = final_g
        for l in range(nl):
            vecD[3 + 4 * l] = norm_g[l]
            vecD[3 + 4 * l + 1:3 + 4 * l + 4] = mod_b[l].reshape(3, D)
        m = dict(shared)
        m["x"] = x[b]
        m["ctx"] = ctx[b]
        m["vecD"] = vecD
        maps.append(m)
    return maps


_NC_CACHE = {}


def kernel(**inputs):
    maps = prep_inputs(inputs)
    if "nc" not in _NC_CACHE:
        _NC_CACHE["nc"] = build(DEPTH)
    nc = _NC_CACHE["nc"]
    res = run_bass_kernel_spmd(nc, maps, core_ids=list(range(8)))
    out = np.stack([np.asarray(r["out"], dtype=np.float32) for r in res.results], 0)
    return out
```

```python
from contextlib import ExitStack, contextmanager
import numpy as np
import ml_dtypes
import concourse.bass as bass
import concourse.mybir as mybir
from concourse.bass_utils import run_bass_kernel_spmd

F32 = mybir.dt.float32
BF16 = mybir.dt.bfloat16
AF = mybir.ActivationFunctionType
ALU = mybir.AluOpType

DEPTH = 4
D = 1024
SEQ = 2048
CTX = 256
TT = CTX + SEQ
BW = 512
IN_W = 5632
GRID_W = 64
ROWS = SEQ // GRID_W
WINS = (2, 4, 8, 16)
NV = 53
NVX = 64
ENGS = ["tensor", "vector", "scalar", "gpsimd", "sync"]
TILES = [(0, 256), (256, 512), (768, 512), (1280, 512), (1792, 512)]
S_VA, S_ZA, S_BB, S_BC, S_BV, S_ZB, S_CA, S_CG, S_ZC, S_DV, S_ZD = range(11)


def _pool_plan():
    mats = []
    key2idx = {}
    plan = []
    goff = []
    gcnt = []
    icc = np.zeros((4, 64), np.float32)
    irc = np.zeros((4, ROWS), np.float64)
    tabc = np.zeros((4, CTX), np.float32)
    for g, w in enumerate(WINS):
        h = w // 2
        start = len(mats)
        key2idx = {}
        pl = {}
        pos = np.arange(CTX)
        lo = np.clip(pos - h, 0, CTX)
        hi = np.clip(pos + h, 0, CTX)
        cnt = (hi - lo)
        tabc[g] = 1.0 / cnt
        M = np.zeros((CTX, CTX), np.float32)
        for tp in range(CTX):
            M[lo[tp]:hi[tp], tp] = 1.0
            M[tp, tp] -= cnt[tp]
        for j in range(2):
            lst = []
            for i in range(2):
                blk = M[i * 128:(i + 1) * 128, j * 128:(j + 1) * 128]
                if not blk.any():
                    continue
                k = blk.tobytes()
                if k not in key2idx:
                    key2idx[k] = len(mats)
                    mats.append(blk.copy())
                lst.append((i, key2idx[k]))
            pl[j] = lst
        r = np.arange(ROWS)
        r0 = np.clip(r - h, 0, ROWS)
        r1 = np.clip(r + h, 0, ROWS)
        c = np.arange(GRID_W)
        c0 = np.clip(c - h, 0, GRID_W)
        c1 = np.clip(c + h, 0, GRID_W)
        icc[g] = 1.0 / (c1 - c0)
        irc[g] = 1.0 / (r1 - r0)
        colbox = np.zeros((GRID_W, GRID_W), np.float32)
        for cp in range(GRID_W):
            colbox[c0[cp]:c1[cp], cp] = 1.0
        for j in range(ROWS // 2):
            lst = []
            for i in range(ROWS // 2):
                blk = np.zeros((128, 128), np.float32)
                for a in range(2):
                    rr = 2 * i + a
                    for b in range(2):
                        rp = 2 * j + b
                        if r0[rp] <= rr < r1[rp]:
                            blk[a * 64:(a + 1) * 64, b * 64:(b + 1) * 64] = colbox
                if i == j:
                    for b in range(2):
                        rp = 2 * j + b
                        cn = (r1[rp] - r0[rp]) * (c1 - c0)
                        blk[b * 64 + np.arange(64), b * 64 + np.arange(64)] -= cn
                if not blk.any():
                    continue
                k = blk.tobytes()
                if k not in key2idx:
                    key2idx[k] = len(mats)
                    mats.append(blk.copy())
                lst.append((i + 2, key2idx[k]))
            pl[j + 2] = lst
        plan.append(pl)
        goff.append(start)
        gcnt.append(len(mats) - start)
    pm = np.stack(mats, 0)
    pm = np.ascontiguousarray(pm.transpose(1, 0, 2)).astype(ml_dtypes.bfloat16)
    return pm, plan, goff, gcnt, icc, irc, tabc


PM, PLAN, GOFF, GCNT, ICC, IRC, TABC = _pool_plan()
NM = PM.shape[1]
NMG = max(GCNT)


class Buf:
    def __init__(self, name, t, nseg, dma_sem, init_rd):
        self.name = name
        self.t = t
        self.nseg = nseg
        self.lw = [None] * nseg
        self.rd = [list(init_rd) for _ in range(nseg)]
        self.dma_sem = dma_sem


class Prog:
    def __init__(self, nc, stack):
        self.nc = nc
        self.stack = stack
        self.sems = {}
        self.count = {}
        self.seen = {e: {} for e in ENGS}
        self.pending = {}
        self.phase_bufs = None
        self.psb = []
        self.psi = 0
        for e in ENGS:
            self.new_sem("E_" + e)

    def new_sem(self, key):
        if key not in self.sems:
            self.sems[key] = self.stack.enter_context(self.nc.semaphore(key))
            self.count[key] = 0
        return key

    def buf(self, name, shape, dtype, nseg=1, dma=False, psum=False, stack=None):
        st = stack if stack is not None else self.stack
        self.uid = getattr(self, "uid", 0) + 1
        tname = "s%d_%s" % (self.uid, name)
        if psum:
            t = st.enter_context(self.nc.psum_tensor(tname, shape, dtype))
        else:
            t = st.enter_context(self.nc.sbuf_tensor(tname, shape, dtype))
        ds = self.new_sem("D_" + name) if dma else None
        init = list(self.pending.items()) if stack is not None else []
        b = Buf(name, t, nseg, ds, init)
        if stack is not None and self.phase_bufs is not None:
            self.phase_bufs.append(b)
        return b

    @contextmanager
    def phase(self):
        old = self.phase_bufs
        self.phase_bufs = []
        with ExitStack() as st:
            yield st
            for b in self.phase_bufs:
                for s in range(b.nseg):
                    for tok in [b.lw[s]] + b.rd[s]:
                        if tok is None:
                            continue
                        k, v = tok
                        if self.pending.get(k, 0) < v:
                            self.pending[k] = v
        self.phase_bufs = old

    def ps(self):
        b = self.psb[self.psi % len(self.psb)]
        self.psi += 1
        return b

    def _expand(self, lst):
        out = []
        for b, s in lst:
            if s is None:
                s = range(b.nseg)
            if isinstance(s, int):
                out.append((b, s))
            else:
                for x in s:
                    out.append((b, x))
        return out

    def _need(self, eng, toks):
        best = {}
        for tok in toks:
            if tok is None:
                continue
            k, v = tok
            if eng == "tensor" and k == "E_tensor":
                continue
            if best.get(k, 0) < v:
                best[k] = v
        e = getattr(self.nc, eng)
        for k, v in best.items():
            if self.seen[eng].get(k, 0) >= v:
                continue
            self.seen[eng][k] = v
            e.wait_ge(self.sems[k], v)

    def _deps(self, eng, reads, writes):
        toks = []
        rl = self._expand(reads)
        wl = self._expand(writes)
        for b, s in rl:
            toks.append(b.lw[s])
        for b, s in wl:
            toks.append(b.lw[s])
            toks.extend(b.rd[s])
        self._need(eng, toks)
        return rl, wl

    def _record(self, tok, rl, wl):
        for b, s in rl:
            b.rd[s].append(tok)
        for b, s in wl:
            b.lw[s] = tok
            b.rd[s] = []

    def op(self, eng, fn, reads=(), writes=(), inc=True):
        rl, wl = self._deps(eng, reads, writes)
        key = "E_" + eng
        ins = fn(getattr(self.nc, eng))
        if inc:
            self.count[key] += 1
            ins.then_inc(self.sems[key], 1)
            v = self.count[key]
        else:
            v = self.count[key] + 1
        self._record((key, v), rl, wl)

    def dma(self, eng, out_ap, in_ap, reads=(), writes=(), sem_buf=None, **kw):
        rl, wl = self._deps(eng, reads, writes)
        key = sem_buf.dma_sem
        if eng == "gpsimd":
            hist = self.__dict__.setdefault("sw_hist", [])
            if len(hist) >= 5:
                self._need(eng, [hist[-5]])
        self.count[key] += 16
        getattr(self.nc, eng).dma_start(out=out_ap, in_=in_ap, **kw).then_inc(self.sems[key], 16)
        if eng == "gpsimd":
            self.sw_hist.append((key, self.count[key]))
        self._record((key, self.count[key]), rl, wl)

    def wait_all(self, eng, bufs):
        toks = []
        for b in bufs:
            for s in range(b.nseg):
                toks.append(b.lw[s])
                toks.extend(b.rd[s])
        self._need(eng, toks)


def hseg(k, tile):
    return k * 5 + tile


def build(NL=DEPTH):
    nc = bass.Bass("TRN2", target_bir_lowering=False, dynamic_dma_scratch_size=8192)
    dt = lambda n, s, d=F32, k="ExternalInput": nc.dram_tensor(n, s, d, kind=k).ap()
    x_d = dt("x", [SEQ, D])
    ctx_d = dt("ctx", [CTX, D])
    vecD_d = dt("vecD", [3 + 4 * DEPTH, D])
    vec5_d = dt("vec5", [DEPTH, NV, BW])
    modw_d = dt("mod_w", [DEPTH, D, 3 * D])
    win_d = dt("w_in", [DEPTH, D, IN_W])
    wout_d = dt("w_out", [DEPTH, 2048, D])
    cpw_d = dt("c_pw", [DEPTH, BW, BW])
    dw_d = dt("d_w", [DEPTH, 4, 128, 128])
    gm_d = dt("gm", [DEPTH, 4, 128, 4, 128])
    pm_d = dt("pmats", [128, NM, 128], BF16)
    icc_d = dt("icc", [128, 4 * 64])
    tabc_d = dt("tabc", [128, 4 * CTX])
    out_d = dt("out", [SEQ, D], F32, "ExternalOutput")
    import os as _os
    DBG = _os.environ.get("KDBG")
    dbg_d = dt("dbg", [128, 8, TT], F32, "ExternalOutput") if DBG else None

    with ExitStack() as st:
        P = Prog(nc, st)
        for i in range(8):
            P.psb.append(P.buf("psb%d" % i, [128, 512], F32, psum=True))

        res = P.buf("res", [128, 8, TT], F32, nseg=40)
        hT = P.buf("hT", [128, 8, TT], BF16, nseg=40)
        cat = P.buf("cat", [128, 8, TT], BF16, nseg=40)
        ident_f = P.buf("ident_f", [128, 128], F32)
        ident_b = P.buf("ident_b", [128, 128], BF16)
        ones1024 = P.buf("ones1024", [128, 128], BF16)
        ones512 = P.buf("ones512", [128, 128], BF16)
        cst = P.buf("cst", [128, 4], F32)
        VD = P.buf("VD", [128, 8, 3 + 4 * DEPTH], F32)
        V = P.buf("V", [128, 4, NVX], F32)
        MT = P.buf("MT", [128, DEPTH, 3, 8, 2], F32, nseg=DEPTH)
        sc = P.buf("sc", [128, 8, 2], BF16)
        mring = [P.buf("mring%d" % i, [128, 8, 128], BF16, dma=True) for i in range(2)]
        modacc = P.buf("modacc", [128, 48], F32)
        mstate = {"pend": [], "ri": 0}
        NR = 8
        ring = [P.buf("ring%d" % i, [128, 8, 128], BF16, dma=True) for i in range(NR)]
        woring = [P.buf("wor%d" % i, [128, 8, 128], BF16, dma=True) for i in range(2)]
        gmring = [P.buf("gmr%d" % i, [128, 4, 128], BF16, dma=True) for i in range(2)]
        dw = P.buf("dw", [128, 4, 128], BF16, dma=True)
        ringi = [0]
        gmi = [0]
        woi = [0]

        EPS6 = cst.t[:, 0:1]
        EPS5 = cst.t[:, 1:2]
        ONE = cst.t[:, 2:3]
        QUART = cst.t[:, 3:4]

        P.op("gpsimd", lambda e: e.memset(ident_f.t[:], 1.0), writes=[(ident_f, 0)])
        P.op("gpsimd", lambda e: e.affine_select(out=ident_f.t[:], in_=ident_f.t[:], pattern=[[-1, 128]],
                                                  compare_op=ALU.is_equal, fill=0.0, base=0, channel_multiplier=1),
             reads=[(ident_f, 0)], writes=[(ident_f, 0)])
        P.op("gpsimd", lambda e: e.tensor_copy(out=ident_b.t[:], in_=ident_f.t[:]), reads=[(ident_f, 0)],
             writes=[(ident_b, 0)])
        P.op("gpsimd", lambda e: e.memset(ones1024.t[:], 1.0 / 1024.0), writes=[(ones1024, 0)])
        P.op("gpsimd", lambda e: e.memset(ones512.t[:], 1.0 / 512.0), writes=[(ones512, 0)])
        P.op("gpsimd", lambda e: e.memset(cst.t[:, 0:1], 1e-6), writes=[(cst, 0)])
        P.op("gpsimd", lambda e: e.memset(cst.t[:, 1:2], 1e-5), writes=[(cst, 0)])
        P.op("gpsimd", lambda e: e.memset(cst.t[:, 2:3], 1.0), writes=[(cst, 0)])
        P.op("gpsimd", lambda e: e.memset(cst.t[:, 3:4], 0.25 + 1.2e-7), writes=[(cst, 0)])

        def kp(ap):
            return ap.rearrange("(k p) c -> p k c", p=128)

        def load_w(l, slot, j):
            b = ring[ringi[0] % NR]
            ringi[0] += 1
            c0 = slot * BW + j * 128
            P.dma("gpsimd", b.t[:], kp(win_d[l, :, c0:c0 + 128]), writes=[(b, 0)], sem_buf=b)
            return b

        def load_gm(l, j):
            b = gmring[gmi[0] % 2]
            gmi[0] += 1
            P.dma("gpsimd", b.t[:], gm_d[l, j], writes=[(b, 0)], sem_buf=b)
            return b

        def mod_issue(l, mc):
            b = mring[mstate["ri"] % 2]
            mstate["ri"] += 1
            P.dma("gpsimd", b.t[:], kp(modw_d[l, :, mc * 128:(mc + 1) * 128]), writes=[(b, 0)], sem_buf=b)
            mstate["pend"].append((mc, b))

        def mod_consume():
            for (mc, b) in mstate["pend"]:
                psm = P.ps()
                for k in range(8):
                    P.op("tensor", lambda e: e.matmul(psm.t[:, 0:2], lhsT=b.t[:, k, :], rhs=sc.t[:, k, 0:2],
                                                      start=(k == 0), stop=(k == 7)),
                         reads=[(b, 0), (sc, 0)], writes=[(psm, 0)], inc=(k == 7))
                P.op("vector", lambda e: e.tensor_copy(out=modacc.t[:, 2 * mc:2 * mc + 2], in_=psm.t[:, 0:2]),
                     reads=[(psm, 0)], writes=[(modacc, 0)])
            mstate["pend"] = []

        def mod_finish(l):
            for j in range(3):
                for col in range(2):
                    tt("vector", MT.t[:, l, j, :, col], modacc.t[:, j * 16 + col:j * 16 + 16:2],
                       VD.t[:, :, 3 + 4 * l + 1 + j], ALU.add, [(modacc, 0), (VD, 0)], [(MT, l)])
            for col in range(2):
                P.op("vector", lambda e: e.scalar_tensor_tensor(out=MT.t[:, l, 1, :, col], in0=MT.t[:, l, 1, :, col],
                                                                scalar=1.0, in1=VD.t[:, :, 3 + 4 * l],
                                                                op0=ALU.add, op1=ALU.mult),
                     reads=[(MT, l), (VD, 0)], writes=[(MT, l)])

        def inproj(wb, tile, ps):
            off, wd = TILES[tile]
            for k in range(8):
                P.op("tensor", lambda e: e.matmul(ps.t[:, 0:wd], lhsT=wb.t[:, k, :], rhs=hT.t[:, k, off:off + wd],
                                                  start=(k == 0), stop=(k == 7)),
                     reads=[(wb, 0), (hT, hseg(k, tile))], writes=[(ps, 0)], inc=(k == 7))

        def act(out, in_, func, reads, writes, bias=None, scale=None):
            kw = {}
            if bias is not None:
                kw["bias"] = bias
            if scale is not None:
                kw["scale"] = scale
            P.op("scalar", lambda e: e.activation(out=out, in_=in_, func=func, **kw), reads=reads, writes=writes)

        def tt(eng, out, in0, in1, op, reads, writes):
            P.op(eng, lambda e: e.tensor_tensor(out=out, in0=in0, in1=in1, op=op), reads=reads, writes=writes)

        def rms_alloc(ph, tag):
            sq = [P.buf("%s_sq%d" % (tag, i), [128, 512], BF16, stack=ph) for i in range(2)]
            rt = P.buf(tag + "_rt", [128, 512], F32, stack=ph)
            rs = [P.buf(tag + "_rs%d" % i, [128, 512], F32, stack=ph) for i in range(2)]
            return sq, rt, rs

        def rms_stats(tile, bufs, it):
            off, wd = TILES[tile]
            sq, rt, rsl = bufs
            rs = rsl[it % 2]
            pss = P.ps()
            for k in range(8):
                s = sq[k % 2]
                act(s.t[:, :wd], res.t[:, k, off:off + wd], AF.Square, [(res, hseg(k, tile))], [(s, 0)])
                P.op("tensor", lambda e: e.matmul(pss.t[:, :wd], lhsT=ones1024.t[:], rhs=s.t[:, :wd],
                                                  start=(k == 0), stop=(k == 7)),
                     reads=[(ones1024, 0), (s, 0)], writes=[(pss, 0)])
            act(rt.t[:, :wd], pss.t[:, :wd], AF.Sqrt, [(pss, 0), (cst, 0)], [(rt, 0)], bias=EPS6)
            P.op("vector", lambda e: e.reciprocal(out=rs.t[:, :wd], in_=rt.t[:, :wd]), reads=[(rt, 0)],
                 writes=[(rs, 0)])
            return rs

        with P.phase() as ph:
            vds = P.buf("vds", [3 + 4 * DEPTH, D], F32, dma=True, stack=ph)
            nvd = 3 + 4 * DEPTH
            P.dma("sync", vds.t[:], vecD_d, writes=[(vds, 0)], sem_buf=vds)
            for k in range(8):
                ps = P.ps()
                P.op("tensor", lambda e: e.transpose(out=ps.t[:, 0:nvd], in_=vds.t[0:nvd, k * 128:(k + 1) * 128],
                                                     identity=ident_f.t[0:nvd, 0:nvd]),
                     reads=[(vds, 0), (ident_f, 0)], writes=[(ps, 0)])
                P.op("vector", lambda e: e.tensor_copy(out=VD.t[:, k, :], in_=ps.t[:, 0:nvd]), reads=[(ps, 0)],
                     writes=[(VD, 0)])
            act(sc.t[:], VD.t[:, :, 0:2], AF.Silu, [(VD, 0)], [(sc, 0)])
            xst = [P.buf("xst%d" % i, [128, D], F32, dma=True, stack=ph) for i in range(2)]
            for t128 in range(18):
                s_ = xst[t128 % 2]
                src = ctx_d[t128 * 128:(t128 + 1) * 128, :] if t128 < 2 else x_d[(t128 - 2) * 128:(t128 - 1) * 128, :]
                P.dma("sync", s_.t[:], src, writes=[(s_, 0)], sem_buf=s_)
                tile = 0 if t128 < 2 else 1 + (t128 - 2) // 4
                for half in range(2):
                    ps = P.ps()
                    for q in range(4):
                        k = half * 4 + q
                        P.op("tensor", lambda e: e.transpose(out=ps.t[:, q * 128:(q + 1) * 128],
                                                             in_=s_.t[:, k * 128:(k + 1) * 128], identity=ident_f.t[:]),
                             reads=[(s_, 0), (ident_f, 0)], writes=[(ps, 0)], inc=(q == 3))
                    dst = res.t[:, half * 4:(half + 1) * 4, t128 * 128:(t128 + 1) * 128]
                    srcp = ps.t[:, 0:512].rearrange("p (q c) -> p q c", q=4)
                    wr = [(res, hseg(half * 4 + q, tile)) for q in range(4)]
                    if half == 0:
                        P.op("vector", lambda e: e.tensor_copy(out=dst, in_=srcp), reads=[(ps, 0)], writes=wr)
                    else:
                        act(dst, srcp, AF.Identity, [(ps, 0)], wr)

            mod_issue(0, 0)
            for mc in range(24):
                if mc + 1 < 24:
                    mod_issue(0, mc + 1)
                pend = mstate["pend"]
                mstate["pend"] = pend[:1]
                mod_consume()
                mstate["pend"] = pend[1:]
            mod_finish(0)
        for l in range(NL):
            last = (l == NL - 1)
            tiles_all = [0, 1, 2, 3, 4]
            tiles_out = [1, 2, 3, 4] if last else tiles_all

            with P.phase() as ph:
                v5s = P.buf("v5s", [NV, BW], F32, dma=True, stack=ph)
                tv = P.buf("tv", [128, 4, 8], F32, stack=ph)
                P.dma("sync", v5s.t[:], vec5_d[l], writes=[(v5s, 0)], sem_buf=v5s)
                for j in range(4):
                    ps = P.ps()
                    P.op("tensor", lambda e: e.transpose(out=ps.t[:, 0:NV], in_=v5s.t[0:NV, j * 128:(j + 1) * 128],
                                                         identity=ident_f.t[0:NV, 0:NV]),
                         reads=[(v5s, 0), (ident_f, 0)], writes=[(ps, 0)])
                    P.op("vector", lambda e: e.tensor_copy(out=V.t[:, j, 0:NV], in_=ps.t[:, 0:NV]), reads=[(ps, 0)],
                         writes=[(V, 0)])
                z = tv.t[:, :, 0:2]
                az = tv.t[:, :, 2:4]
                ee = tv.t[:, :, 4:6]
                P.op("vector", lambda e: e.tensor_scalar(out=z, in0=V.t[:, :, 12:14], scalar1=-1.0, scalar2=None,
                                                         op0=ALU.mult), reads=[(V, 0)], writes=[(tv, 0)])
                act(az, z, AF.Abs, [(tv, 0)], [(tv, 0)])
                act(ee, az, AF.Exp, [(tv, 0)], [(tv, 0)], scale=-1.0)
                act(ee, ee, AF.Ln, [(tv, 0), (cst, 0)], [(tv, 0)], bias=ONE)
                P.op("vector", lambda e: e.scalar_tensor_tensor(out=ee, in0=z, scalar=0.0, in1=ee, op0=ALU.max,
                                                                op1=ALU.add), reads=[(tv, 0)], writes=[(tv, 0)])
                P.op("vector", lambda e: e.tensor_scalar(out=V.t[:, :, 53:55], in0=ee, scalar1=-8.0, scalar2=None,
                                                         op0=ALU.mult), reads=[(tv, 0)], writes=[(V, 0)])
                P.op("vector", lambda e: e.tensor_scalar(out=V.t[:, :, 55:57], in0=ee, scalar1=-16.0, scalar2=None,
                                                         op0=ALU.mult), reads=[(tv, 0)], writes=[(V, 0)])
                tt("vector", V.t[:, :, 57], V.t[:, :, 51], V.t[:, :, 52], ALU.mult, [(V, 0)], [(V, 0)])
                P.op("vector", lambda e: e.tensor_scalar(out=V.t[:, :, 58:62], in0=V.t[:, :, 8:12], scalar1=0.5,
                                                         scalar2=None, op0=ALU.mult), reads=[(V, 0)], writes=[(V, 0)])
                P.op("vector", lambda e: e.tensor_scalar(out=V.t[:, :, 62:64], in0=ee, scalar1=-4.0, scalar2=None,
                                                         op0=ALU.mult), reads=[(tv, 0)], writes=[(V, 0)])
                P.dma("gpsimd", dw.t[:], dw_d[l].rearrange("g i j -> i g j"), writes=[(dw, 0)], sem_buf=dw)

            units = []
            for j in range(4):
                units.append(("A", j, [(S_VA, j), (S_ZA, j)]))
            for j in range(4):
                units.append(("B", j, [(S_BC, j), (S_BV, j), (S_BB, j), (S_ZB, j)]))
            units.append(("WO", 0, []))
            for j in range(4):
                units.append(("C1", j, [(S_CG, j), (S_CA, j)]))
            units.append(("C2", 0, [(S_ZC, 0), (S_ZC, 1), (S_ZC, 2), (S_ZC, 3)]))
            for g in range(4):
                units.append(("D", g, [(S_DV, g), (S_ZD, g)]))
            units.append(("WO", 1, []))
            loaded = {}

            def issue(ui):
                kind, j, ws = units[ui]
                loaded[ui] = [load_w(l, s, jj) for (s, jj) in ws]
                if kind == "A":
                    loaded[ui].append(load_gm(l, j))

            issue(0)

            with P.phase() as ph:
                tmp = [P.buf("n_tmp%d" % i, [128, 512], F32, stack=ph) for i in range(2)]
                rb = rms_alloc(ph, "n")
                for it_, tile in enumerate(tiles_all):
                    off, wd = TILES[tile]
                    col = 1 if tile == 0 else 0
                    rs = rms_stats(tile, rb, it_)
                    for k in range(8):
                        tb = tmp[k % 2]
                        tt("gpsimd" if k % 2 == 0 else "vector", tb.t[:, :wd], res.t[:, k, off:off + wd], rs.t[:, :wd],
                           ALU.mult, [(res, hseg(k, tile)), (rs, 0)], [(tb, 0)])
                        if k in (0, 3, 6):
                            act(hT.t[:, k, off:off + wd], tb.t[:, :wd], AF.Identity, [(tb, 0), (MT, l)],
                                [(hT, hseg(k, tile))], scale=MT.t[:, l, 1, k, col:col + 1],
                                bias=MT.t[:, l, 0, k, col:col + 1])
                        else:
                            P.op("vector", lambda e: e.tensor_scalar(out=hT.t[:, k, off:off + wd], in0=tb.t[:, :wd],
                                                                     scalar1=MT.t[:, l, 1, k, col:col + 1],
                                                                     scalar2=MT.t[:, l, 0, k, col:col + 1],
                                                                     op0=ALU.mult, op1=ALU.add),
                                 reads=[(tb, 0), (MT, l)], writes=[(hT, hseg(k, tile))])

            for ui, (kind, j, ws) in enumerate(units):
                if DBG and l == 0 and ui == int(DBG):
                    with P.phase() as ph:
                        dbb = P.buf("dbgb", [128, TT], F32, dma=True, stack=ph)
                        for kk in range(min(int(DBG), 8)):
                            P.op("vector", lambda e: e.tensor_copy(out=dbb.t[:], in_=cat.t[:, kk, :]),
                                 reads=[(cat, None)], writes=[(dbb, 0)])
                            P.dma("sync", dbg_d[:, kk, :], dbb.t[:], reads=[(dbb, 0)], sem_buf=dbb)
                        P.wait_all("sync", [dbb])
                    return nc
                if l + 1 < NL:
                    mod_consume()
                    nb_ = 2 if ui < 5 else 1
                    mc0 = ui * 2 if ui < 5 else 10 + (ui - 5)
                    for q_ in range(nb_):
                        mod_issue(l + 1, mc0 + q_)
                if ui + 1 < len(units) and kind != "A":
                    issue(ui + 1)
                W = loaded.get(ui, [])

                if kind == "A":
                    Wva, Wza, gm = W
                    PA = 3
                    with P.phase() as ph:
                        va = P.buf("a_va", [128, TT + 3 * PA], BF16, nseg=5, stack=ph)
                        vapad = P.buf("a_vapad", [128, 1], BF16, stack=ph)
                        S = P.buf("a_S", [128, TT], F32, nseg=5, stack=ph)
                        dg = P.buf("a_dg", [128, 8, 128], BF16, stack=ph)
                        xcb = [P.buf("a_xcb%d" % i, [128, 512], BF16, stack=ph) for i in range(2)]
                        T1d = [[P.buf("a_T1%d%d" % (i, p_), [128, 512], F32, stack=ph) for p_ in range(2)]
                               for i in range(2)]
                        T2 = [P.buf("a_T2%d" % i, [128, 512], F32, stack=ph) for i in range(2)]
                        T3d = [[P.buf("a_T3%d%d" % (i, p_), [128, 512], F32, stack=ph) for p_ in range(2)]
                               for i in range(2)]
                        zsl = [P.buf("a_zs%d" % i, [128, 512], BF16, stack=ph) for i in range(2)]
                        stbuf = [[P.buf("a_st%d%d" % (n_, p_), [128, 1], F32, stack=ph) for p_ in range(2)]
                                 for n_ in range(2)]
                        colA = lambda tile: TILES[tile][0] + (PA if tile == 0 else 2 * PA)
                        for i in range(8):
                            P.op("gpsimd", lambda e: e.tensor_scalar(out=dg.t[:, i, :], in0=ident_b.t[:],
                                                                     scalar1=V.t[:, j, i:i + 1], scalar2=1.0,
                                                                     op0=ALU.mult, op1=ALU.mult),
                                 reads=[(ident_b, 0), (V, 0)], writes=[(dg, 0)])
                        for (p0, p1) in ((0, PA), (PA + CTX, 2 * PA + CTX), (2 * PA + TT, 3 * PA + TT)):
                            P.op("gpsimd", lambda e: e.memset(va.t[:, p0:p1], 0.0), writes=[(vapad, 0)])
                        if ui + 1 < len(units):
                            issue(ui + 1)
                        for it, tile in enumerate(tiles_all):
                            off, wd = TILES[tile]
                            ps = P.ps()
                            inproj(Wva, tile, ps)
                            c0 = colA(tile)
                            if it % 2 == 0:
                                act(va.t[:, c0:c0 + wd], ps.t[:, :wd], AF.Identity, [(ps, 0)], [(va, tile)])
                            else:
                                P.op("vector", lambda e: e.tensor_copy(out=va.t[:, c0:c0 + wd], in_=ps.t[:, :wd]),
                                     reads=[(ps, 0)], writes=[(va, tile)])

                        def nbr(tile):
                            if tile == 0:
                                return [0]
                            return [t for t in (tile - 1, tile, tile + 1) if 1 <= t <= 4]

                        FW = [0, 1, 2, 3, 4]
                        BK = [0, 4, 3, 2, 1]
                        prev = [0.0, 0.0]
                        prev_dep = [[], []]
                        ST = {}

                        def front(it):
                            tl = [FW[it], BK[it]]
                            psc = [None, None]
                            psr = [None, None]
                            psi_ = [None, None]
                            for n in range(2):
                                tile = tl[n]
                                off, wd = TILES[tile]
                                c0 = colA(tile)
                                psc[n] = P.ps()
                                for k in range(4):
                                    sh = (c0 - 3 + k) if n == 0 else (c0 + k)
                                    P.op("tensor", lambda e: e.matmul(psc[n].t[:, :wd], lhsT=dg.t[:, n * 4 + k, :],
                                                                      rhs=va.t[:, sh:sh + wd], start=(k == 0), stop=(k == 3)),
                                         reads=[(dg, 0), (vapad, 0), (va, nbr(tile))], writes=[(psc[n], 0)], inc=(k == 3))
                            for n in range(2):
                                wd = TILES[tl[n]][1]
                                P.op("vector", lambda e: e.tensor_copy(out=xcb[n].t[:, :wd], in_=psc[n].t[:, :wd]),
                                     reads=[(psc[n], 0)], writes=[(xcb[n], 0)])
                            for n in range(2):
                                wd = TILES[tl[n]][1]
                                psr[n] = P.ps()
                                P.op("tensor", lambda e: e.matmul(psr[n].t[:, :wd], lhsT=gm.t[:, n * 2, :],
                                                                  rhs=xcb[n].t[:, :wd], start=True, stop=True),
                                     reads=[(gm, 0), (xcb[n], 0)], writes=[(psr[n], 0)])
                                psi_[n] = P.ps()
                                P.op("tensor", lambda e: e.matmul(psi_[n].t[:, :wd], lhsT=gm.t[:, n * 2 + 1, :],
                                                                  rhs=xcb[n].t[:, :wd], start=True, stop=True),
                                     reads=[(gm, 0), (xcb[n], 0)], writes=[(psi_[n], 0)])
                            if it == 0:
                                comb = [(0, 1)] if 0 in tiles_out else []
                            elif it == 3:
                                comb = [(3, 0), (2, 1)]
                            elif it == 4:
                                comb = [(4, 0), (1, 1)]
                            else:
                                comb = []
                            pszl = []
                            for (ctile, cn) in comb:
                                psz = P.ps()
                                inproj(Wza, ctile, psz)
                                pszl.append(psz)
                            ST[it] = (tl, psc, psr, psi_, comb, pszl)

                        def mid(it):
                            tl, psc, psr, psi_, comb, pszl = ST[it]
                            T1 = [T1d[0][it % 2], T1d[1][it % 2]]
                            T3 = [T3d[0][it % 2], T3d[1][it % 2]]
                            for n in range(2):
                                wd = TILES[tl[n]][1]
                                act(T1[n].t[:, :wd], psr[n].t[:, :wd], AF.Tanh, [(psr[n], 0), (V, 0)], [(T1[n], 0)],
                                    scale=0.5, bias=V.t[:, j, 58 + n:59 + n])
                                act(T3[n].t[:, :wd], psi_[n].t[:, :wd], AF.Tanh, [(psi_[n], 0), (V, 0)], [(T3[n], 0)],
                                    scale=0.5, bias=V.t[:, j, 60 + n:61 + n])
                            for n in range(2):
                                wd = TILES[tl[n]][1]
                                act(T1[n].t[:, :wd], T1[n].t[:, :wd], AF.Exp, [(T1[n], 0), (V, 0)], [(T1[n], 0)],
                                    scale=V.t[:, j, 62 + n:63 + n], bias=V.t[:, j, 62 + n:63 + n])
                                tt("gpsimd", T2[n].t[:, :wd], T1[n].t[:, :wd], T1[n].t[:, :wd], ALU.mult, [(T1[n], 0)],
                                   [(T2[n], 0)])
                            for ci, (ctile, cn) in enumerate(comb):
                                wd = TILES[ctile][1]
                                act(zsl[ci].t[:, :wd], pszl[ci].t[:, :wd], AF.Tanh, [(pszl[ci], 0)], [(zsl[ci], 0)],
                                    scale=0.5)
                            for n in range(2):
                                wd = TILES[tl[n]][1]
                                P.op("vector", lambda e: e.scalar_tensor_tensor(out=T3[n].t[:, :wd], in0=T3[n].t[:, :wd],
                                                                                scalar=1.0, in1=psc[n].t[:, :wd],
                                                                                op0=ALU.add, op1=ALU.mult),
                                     reads=[(T3[n], 0), (psc[n], 0)], writes=[(T3[n], 0)])
                            for ci, (ctile, cn) in enumerate(comb):
                                wd = TILES[ctile][1]
                                P.op("vector", lambda e: e.scalar_tensor_tensor(out=zsl[ci].t[:, :wd],
                                                                                in0=zsl[ci].t[:, :wd],
                                                                                scalar=1.0, in1=pszl[ci].t[:, :wd],
                                                                                op0=ALU.add, op1=ALU.mult),
                                     reads=[(zsl[ci], 0), (pszl[ci], 0)], writes=[(zsl[ci], 0)])
                            for n in range(2):
                                wd = TILES[tl[n]][1]
                                act(T2[n].t[:, :wd], T2[n].t[:, :wd], AF.Sqrt, [(T2[n], 0), (cst, 0)], [(T2[n], 0)],
                                    scale=-0.25, bias=QUART)

                        def back(it):
                            tl, psc, psr, psi_, comb, pszl = ST[it]
                            T1 = [T1d[0][it % 2], T1d[1][it % 2]]
                            T3 = [T3d[0][it % 2], T3d[1][it % 2]]
                            for n in range(2):
                                wd = TILES[tl[n]][1]
                                tt("gpsimd", T3[n].t[:, :wd], T3[n].t[:, :wd], T2[n].t[:, :wd], ALU.mult,
                                   [(T3[n], 0), (T2[n], 0)], [(T3[n], 0)])
                            for n in range(2):
                                tile = tl[n]
                                off, wd = TILES[tile]
                                store_n = (n == 0) if it == 0 else (it in (1, 2))
                                if store_n:
                                    o_ap = S.t[:, off:off + wd]
                                    wr = [(S, tile)]
                                else:
                                    o_ap = T3[n].t[:, 0:wd]
                                    wr = [(T3[n], 0)]
                                d0 = T1[n].t[:, 0:wd]
                                d1 = T3[n].t[:, 0:wd]
                                if n == 1:
                                    o_ap, d0, d1 = o_ap[:, ::-1], d0[:, ::-1], d1[:, ::-1]
                                init = prev[n]
                                P.op("vector", lambda e: e.tensor_tensor_scan(out=o_ap, data0=d0, data1=d1, initial=init,
                                                                              op0=ALU.mult, op1=ALU.add),
                                     reads=[(T1[n], 0), (T3[n], 0)] + prev_dep[n], writes=wr)
                                stc = stbuf[n][it % 2]
                                col = (wd - 1) if n == 0 else 0
                                if store_n:
                                    src = S.t[:, off + col:off + col + 1]
                                else:
                                    src = T3[n].t[:, col:col + 1]
                                P.op("vector", lambda e: e.tensor_copy(out=stc.t[:, 0:1], in_=src), reads=wr,
                                     writes=[(stc, 0)])
                                prev[n] = stc.t[:, 0:1]
                                prev_dep[n] = [(stc, 0)]
                            for ci, (ctile, cn) in enumerate(comb):
                                off, wd = TILES[ctile]
                                tt("gpsimd", T3[cn].t[:, :wd], T3[cn].t[:, :wd], S.t[:, off:off + wd], ALU.add,
                                   [(T3[cn], 0), (S, ctile)], [(T3[cn], 0)])
                                P.op("vector", lambda e: e.scalar_tensor_tensor(out=cat.t[:, j, off:off + wd],
                                                                                in0=T3[cn].t[:, :wd], scalar=0.5,
                                                                                in1=zsl[ci].t[:, :wd], op0=ALU.mult,
                                                                                op1=ALU.mult),
                                     reads=[(T3[cn], 0), (zsl[ci], 0)], writes=[(cat, hseg(j, ctile))])

                        front(0)
                        for it in range(5):
                            mid(it)
                            if it + 1 < 5:
                                front(it + 1)
                            back(it)

                elif kind == "B":
                    Wbc, Wbv, Wbb, Wzb = W
                    with P.phase() as ph:
                        prod = P.buf("b_prod", [128, TT + 3], BF16, nseg=5, stack=ph)
                        dg = P.buf("b_dg", [128, 3, 128], BF16, stack=ph)
                        bcs = [P.buf("b_bcs%d" % i, [128, 512], F32, stack=ph) for i in range(2)]
                        zs = [P.buf("b_zs%d" % i, [128, 512], F32, stack=ph) for i in range(2)]
                        gb = [P.buf("b_g%d" % i, [128, 512], F32, stack=ph) for i in range(2)]
                        colB = lambda tile: TILES[tile][0] + (1 if tile == 0 else 2)
                        bpad = P.buf("b_pad", [128, 1], BF16, stack=ph)
                        for p0 in (0, 1 + CTX, 2 + TT):
                            P.op("gpsimd", lambda e: e.memset(prod.t[:, p0:p0 + 1], 0.0), writes=[(bpad, 0)])
                        for i in range(3):
                            P.op("gpsimd", lambda e: e.tensor_scalar(out=dg.t[:, i, :], in0=ident_b.t[:],
                                                                     scalar1=V.t[:, j, 14 + i:15 + i], scalar2=1.0,
                                                                     op0=ALU.mult, op1=ALU.mult),
                                 reads=[(ident_b, 0), (V, 0)], writes=[(dg, 0)])
                        for it, tile in enumerate(tiles_out):
                            off, wd = TILES[tile]
                            c0 = colB(tile)
                            ps1 = P.ps()
                            inproj(Wbc, tile, ps1)
                            ps2 = P.ps()
                            inproj(Wbv, tile, ps2)
                            b_ = bcs[it % 2]
                            act(b_.t[:, :wd], ps1.t[:, :wd], AF.Identity, [(ps1, 0)], [(b_, 0)])
                            tt("vector", prod.t[:, c0:c0 + wd], ps2.t[:, :wd], b_.t[:, :wd], ALU.mult,
                               [(ps2, 0), (b_, 0)], [(prod, tile)])

                        def nbrB(tile):
                            if tile == 0:
                                return [0]
                            return [t for t in (tile - 1, tile, tile + 1) if 1 <= t <= 4]

                        for it, tile in enumerate(tiles_out):
                            off, wd = TILES[tile]
                            c0 = colB(tile)
                            ps3 = P.ps()
                            for k in range(3):
                                sh = c0 - 1 + k
                                P.op("tensor", lambda e: e.matmul(ps3.t[:, :wd], lhsT=dg.t[:, k, :],
                                                                  rhs=prod.t[:, sh:sh + wd], start=(k == 0), stop=(k == 2)),
                                     reads=[(dg, 0), (bpad, 0), (prod, nbrB(tile))], writes=[(ps3, 0)], inc=(k == 2))
                            ps4 = P.ps()
                            inproj(Wbb, tile, ps4)
                            ps5 = P.ps()
                            inproj(Wzb, tile, ps5)
                            z_ = zs[it % 2]
                            g_ = gb[it % 2]
                            act(z_.t[:, :wd], ps5.t[:, :wd], AF.Silu, [(ps5, 0)], [(z_, 0)])
                            tt("vector", g_.t[:, :wd], ps4.t[:, :wd], z_.t[:, :wd], ALU.mult, [(ps4, 0), (z_, 0)],
                               [(g_, 0)])
                            tt("vector", cat.t[:, 4 + j, off:off + wd], ps3.t[:, :wd], g_.t[:, :wd], ALU.mult,
                               [(ps3, 0), (g_, 0)], [(cat, hseg(4 + j, tile))])

                elif kind == "C1":
                    Wcg, Wca = W
                    PC = 15
                    with P.phase() as ph:
                        glu = P.buf("c_glu", [128, TT + 3 * PC], BF16, nseg=5, stack=ph)
                        dg = P.buf("c_dg", [128, 31, 128], BF16, stack=ph)
                        sg = [P.buf("c_sg%d" % i, [128, 512], F32, stack=ph) for i in range(2)]
                        colC = lambda tile: TILES[tile][0] + (PC if tile == 0 else 2 * PC)
                        cpad = P.buf("c_pad", [128, 1], BF16, stack=ph)
                        for p0 in (0, PC + CTX, 2 * PC + TT):
                            P.op("gpsimd", lambda e: e.memset(glu.t[:, p0:p0 + PC], 0.0), writes=[(cpad, 0)])
                        for i in range(6, 31):
                            P.op("gpsimd", lambda e: e.tensor_scalar(out=dg.t[:, i, :], in0=ident_b.t[:],
                                                                     scalar1=V.t[:, j, 17 + i:18 + i], scalar2=1.0,
                                                                     op0=ALU.mult, op1=ALU.mult),
                                 reads=[(ident_b, 0), (V, 0)], writes=[(dg, 0)])
                        for it, tile in enumerate(tiles_out):
                            off, wd = TILES[tile]
                            c0 = colC(tile)
                            ps1 = P.ps()
                            inproj(Wcg, tile, ps1)
                            ps2 = P.ps()
                            inproj(Wca, tile, ps2)
                            s_ = sg[it % 2]
                            act(s_.t[:, :wd], ps1.t[:, :wd], AF.Sigmoid, [(ps1, 0)], [(s_, 0)])
                            tt("vector", glu.t[:, c0:c0 + wd], ps2.t[:, :wd], s_.t[:, :wd], ALU.mult,
                               [(ps2, 0), (s_, 0)], [(glu, tile)])

                        def nbrC(tile):
                            if tile == 0:
                                return [0]
                            return [t for t in (tile - 1, tile, tile + 1) if 1 <= t <= 4]

                        NDV = 6
                        acc = [P.buf("c_acc%d" % i, [128, 512], F32, stack=ph) for i in range(2)]
                        for it, tile in enumerate(tiles_out):
                            off, wd = TILES[tile]
                            c0 = colC(tile)
                            ps3 = P.ps()
                            for k in range(NDV, 31):
                                sh = c0 - 15 + k
                                P.op("tensor", lambda e: e.matmul(ps3.t[:, :wd], lhsT=dg.t[:, k, :],
                                                                  rhs=glu.t[:, sh:sh + wd], start=(k == NDV), stop=(k == 30)),
                                     reads=[(dg, 0), (cpad, 0), (glu, nbrC(tile))], writes=[(ps3, 0)], inc=(k == 30))
                            ac = acc[it % 2]
                            for k in range(NDV):
                                sh = c0 - 15 + k
                                if k == 0:
                                    P.op("vector", lambda e: e.tensor_scalar(out=ac.t[:, :wd], in0=glu.t[:, sh:sh + wd],
                                                                             scalar1=V.t[:, j, 17 + k:18 + k], scalar2=None,
                                                                             op0=ALU.mult),
                                         reads=[(glu, nbrC(tile)), (cpad, 0), (V, 0)], writes=[(ac, 0)])
                                else:
                                    P.op("vector", lambda e: e.scalar_tensor_tensor(out=ac.t[:, :wd],
                                                                                    in0=glu.t[:, sh:sh + wd],
                                                                                    scalar=V.t[:, j, 17 + k:18 + k],
                                                                                    in1=ac.t[:, :wd], op0=ALU.mult,
                                                                                    op1=ALU.add),
                                         reads=[(glu, nbrC(tile)), (cpad, 0), (V, 0), (ac, 0)], writes=[(ac, 0)])
                            tt("vector", cat.t[:, 4 + j, off:off + wd], ps3.t[:, :wd], ac.t[:, :wd], ALU.add,
                               [(ps3, 0), (ac, 0)], [(cat, hseg(4 + j, tile))])

                elif kind == "C2":
                    Wzc = W
                    with P.phase() as ph:
                        cpw = P.buf("c_pw", [128, 4, BW], BF16, dma=True, stack=ph)
                        P.dma("gpsimd", cpw.t[:], kp(cpw_d[l]), writes=[(cpw, 0)], sem_buf=cpw)
                        usq = [P.buf("c_usq%d" % i, [128, 512], BF16, stack=ph) for i in range(2)]
                        meanl = [P.buf("c_mean%d" % i, [128, 512], F32, stack=ph) for i in range(2)]
                        m2l = [P.buf("c_m2%d" % i, [128, 512], F32, stack=ph) for i in range(2)]
                        rstdl = [P.buf("c_rstd%d" % i, [128, 512], F32, stack=ph) for i in range(2)]
                        tb = [P.buf("c_t%d" % i, [128, 512], F32, stack=ph) for i in range(2)]
                        unl = [P.buf("c_un%d" % i, [128, 4, 512], BF16, nseg=4, stack=ph) for i in range(2)]
                        zs = [P.buf("c_zs%d" % i, [128, 512], F32, stack=ph) for i in range(2)]

                        def partA(it, tile):
                            off, wd = TILES[tile]
                            mean, m2, rstd, un = meanl[it % 2], m2l[it % 2], rstdl[it % 2], unl[it % 2]
                            psm = P.ps()
                            psq = P.ps()
                            for jj in range(4):
                                u_ap = cat.t[:, 4 + jj, off:off + wd]
                                useg = (cat, hseg(4 + jj, tile))
                                P.op("tensor", lambda e: e.matmul(psm.t[:, :wd], lhsT=ones512.t[:], rhs=u_ap,
                                                                  start=(jj == 0), stop=(jj == 3)),
                                     reads=[(ones512, 0), useg], writes=[(psm, 0)], inc=(jj == 3))
                                q_ = usq[jj % 2]
                                act(q_.t[:, :wd], u_ap, AF.Square, [useg], [(q_, 0)])
                                P.op("tensor", lambda e: e.matmul(psq.t[:, :wd], lhsT=ones512.t[:], rhs=q_.t[:, :wd],
                                                                  start=(jj == 0), stop=(jj == 3)),
                                     reads=[(ones512, 0), (q_, 0)], writes=[(psq, 0)])
                            act(mean.t[:, :wd], psm.t[:, :wd], AF.Identity, [(psm, 0)], [(mean, 0)])
                            tt("gpsimd", m2.t[:, :wd], mean.t[:, :wd], mean.t[:, :wd], ALU.mult, [(mean, 0)], [(m2, 0)])
                            tt("vector", m2.t[:, :wd], psq.t[:, :wd], m2.t[:, :wd], ALU.subtract, [(psq, 0), (m2, 0)],
                               [(m2, 0)])
                            act(m2.t[:, :wd], m2.t[:, :wd], AF.Sqrt, [(m2, 0), (cst, 0)], [(m2, 0)], bias=EPS5)
                            P.op("vector", lambda e: e.reciprocal(out=rstd.t[:, :wd], in_=m2.t[:, :wd]),
                                 reads=[(m2, 0)], writes=[(rstd, 0)])
                            for jj in range(4):
                                t_ = tb[jj % 2]
                                tt("gpsimd", t_.t[:, :wd], cat.t[:, 4 + jj, off:off + wd], mean.t[:, :wd], ALU.subtract,
                                   [(cat, hseg(4 + jj, tile)), (mean, 0)], [(t_, 0)])
                                tt("vector", t_.t[:, :wd], t_.t[:, :wd], rstd.t[:, :wd], ALU.mult, [(t_, 0), (rstd, 0)],
                                   [(t_, 0)])
                                act(un.t[:, jj, :wd], t_.t[:, :wd], AF.Silu, [(t_, 0), (V, 0)], [(un, jj)],
                                    scale=V.t[:, jj, 48:49], bias=V.t[:, jj, 49:50])

                        def partB(it, tile):
                            off, wd = TILES[tile]
                            un = unl[it % 2]
                            for m in range(4):
                                psp = P.ps()
                                for kc in range(4):
                                    P.op("tensor", lambda e: e.matmul(psp.t[:, :wd], lhsT=cpw.t[:, kc, m * 128:(m + 1) * 128],
                                                                      rhs=un.t[:, kc, :wd], start=(kc == 0), stop=(kc == 3)),
                                         reads=[(cpw, 0), (un, kc)], writes=[(psp, 0)], inc=(kc == 3))
                                psz = P.ps()
                                inproj(Wzc[m], tile, psz)
                                z_ = zs[m % 2]
                                act(z_.t[:, :wd], psz.t[:, :wd], AF.Silu, [(psz, 0)], [(z_, 0)])
                                P.op("vector", lambda e: e.scalar_tensor_tensor(out=cat.t[:, m, off:off + wd],
                                                                                in0=psp.t[:, :wd],
                                                                                scalar=V.t[:, m, 50:51], in1=z_.t[:, :wd],
                                                                                op0=ALU.add, op1=ALU.mult),
                                     reads=[(psp, 0), (z_, 0), (V, 0)], writes=[(cat, hseg(m, tile))])

                        tl_ = list(tiles_out)
                        partA(0, tl_[0])
                        for it in range(len(tl_)):
                            if it + 1 < len(tl_):
                                partA(it + 1, tl_[it + 1])
                            partB(it, tl_[it])

                elif kind == "D":
                    g = j
                    Wdv, Wzd = W
                    with P.phase() as ph:
                        dvtm = P.buf("d_dvtm", [128, 18, 128], BF16, nseg=18, stack=ph)
                        pmg = P.buf("d_pm", [128, NMG, 128], BF16, dma=True, stack=ph)
                        pT = [P.buf("d_pT%d" % i, [128, 512], BF16, stack=ph) for i in range(2)]
                        zs = [P.buf("d_zs%d" % i, [128, 512], F32, stack=ph) for i in range(2)]
                        tb = [P.buf("d_t%d" % i, [128, 512], F32, stack=ph) for i in range(2)]
                        ng = GCNT[g]
                        icc = P.buf("d_icc", [128, 64], F32, dma=True, stack=ph)
                        tabc = P.buf("d_tabc", [128, CTX], F32, dma=True, stack=ph)
                        P.dma("sync", icc.t[:], icc_d[:, g * 64:(g + 1) * 64], writes=[(icc, 0)], sem_buf=icc)
                        P.dma("sync", tabc.t[:], tabc_d[:, g * CTX:(g + 1) * CTX], writes=[(tabc, 0)], sem_buf=tabc)
                        P.dma("sync", pmg.t[:, 0:ng, :], pm_d[:, GOFF[g]:GOFF[g] + ng, :], writes=[(pmg, 0)], sem_buf=pmg)
                        t128s = list(range(0 if not last else 2, 18))
                        grp = []
                        cur = []
                        for t1 in t128s:
                            cur.append(t1)
                            if len(cur) == 4 or t1 == t128s[-1] or t1 == 1:
                                grp.append(cur)
                                cur = []
                        for gi, lst in enumerate(grp):
                            ps = P.ps()
                            for q, t1 in enumerate(lst):
                                tile = 0 if t1 < 2 else 1 + (t1 - 2) // 4
                                for k in range(8):
                                    P.op("tensor", lambda e: e.matmul(ps.t[:, q * 128:(q + 1) * 128],
                                                                      lhsT=hT.t[:, k, t1 * 128:(t1 + 1) * 128],
                                                                      rhs=Wdv.t[:, k, :], start=(k == 0), stop=(k == 7)),
                                         reads=[(Wdv, 0), (hT, hseg(k, tile))], writes=[(ps, 0)],
                                         inc=(k == 7 and q == len(lst) - 1))
                            n_ = len(lst)
                            dst = dvtm.t[:, lst[0]:lst[0] + n_, :]
                            srcp = ps.t[:, 0:n_ * 128].rearrange("p (q c) -> p q c", q=n_)
                            if gi % 2 == 0:
                                act(dst, srcp, AF.Identity, [(ps, 0)], [(dvtm, lst)])
                            else:
                                P.op("vector", lambda e: e.tensor_copy(out=dst, in_=srcp), reads=[(ps, 0)],
                                     writes=[(dvtm, lst)])
                        for it, tile in enumerate(tiles_out):
                            off, wd = TILES[tile]
                            n128 = wd // 128
                            t0 = off // 128
                            ps = P.ps()
                            nmm = sum(len(PLAN[g][t0 + q]) for q in range(n128))
                            cnt = 0
                            for q in range(n128):
                                lst = PLAN[g][t0 + q]
                                for idx, (i_, m_) in enumerate(lst):
                                    cnt += 1
                                    ml = m_ - GOFF[g]
                                    P.op("tensor", lambda e: e.matmul(ps.t[:, q * 128:(q + 1) * 128],
                                                                      lhsT=dvtm.t[:, i_, :], rhs=pmg.t[:, ml, :],
                                                                      start=(idx == 0), stop=(idx == len(lst) - 1)),
                                         reads=[(dvtm, i_), (pmg, 0)], writes=[(ps, 0)], inc=(cnt == nmm))
                            p_ = pT[it % 2]
                            ps3 = P.ps()
                            inproj(Wzd, tile, ps3)
                            if tile == 0:
                                tt("vector", p_.t[:, :wd], ps.t[:, :wd], tabc.t[:, :], ALU.mult,
                                   [(ps, 0), (tabc, 0)], [(p_, 0)])
                            else:
                                rbase = (off - CTX) // GRID_W
                                a = 0
                                while a < 8:
                                    b = a
                                    while b + 1 < 8 and IRC[g][rbase + b + 1] == IRC[g][rbase + a]:
                                        b += 1
                                    nr = b - a + 1
                                    sc_ = float(IRC[g][rbase + a])
                                    o_ap = p_.t[:, a * 64:(b + 1) * 64].rearrange("p (r c) -> p r c", c=64)
                                    i_ap = ps.t[:, a * 64:(b + 1) * 64].rearrange("p (r c) -> p r c", c=64)
                                    t_ap = icc.t[:, :].unsqueeze(1).broadcast_to([128, nr, 64])
                                    P.op("vector", lambda e: e.scalar_tensor_tensor(out=o_ap, in0=i_ap, scalar=sc_,
                                                                                    in1=t_ap, op0=ALU.mult, op1=ALU.mult),
                                         reads=[(ps, 0), (icc, 0)], writes=[(p_, 0)])
                                    a = b + 1
                            ps2 = P.ps()
                            P.op("tensor", lambda e: e.matmul(ps2.t[:, :wd], lhsT=dw.t[:, g, :], rhs=p_.t[:, :wd],
                                                              start=True, stop=True),
                                 reads=[(dw, 0), (p_, 0)], writes=[(ps2, 0)])
                            z_ = zs[it % 2]
                            t_ = tb[it % 2]
                            act(z_.t[:, :wd], ps3.t[:, :wd], AF.Silu, [(ps3, 0)], [(z_, 0)])
                            act(t_.t[:, :wd], ps2.t[:, :wd], AF.Identity, [(ps2, 0), (V, 0)], [(t_, 0)],
                                scale=V.t[:, g, 52:53], bias=V.t[:, g, 57:58])
                            tt("gpsimd", cat.t[:, 4 + g, off:off + wd], t_.t[:, :wd], z_.t[:, :wd], ALU.mult,
                               [(t_, 0), (z_, 0)], [(cat, hseg(4 + g, tile))])

                elif kind == "WO":
                    half = j

                    def load_wo(m):
                        b = woring[woi[0] % 2]
                        woi[0] += 1
                        P.dma("gpsimd", b.t[:], kp(wout_d[l, half * 1024:(half + 1) * 1024, m * 128:(m + 1) * 128]),
                              writes=[(b, 0)], sem_buf=b)
                        return b

                    nxt = load_wo(0)
                    for m in range(8):
                        wo = nxt
                        if m + 1 < 8:
                            nxt = load_wo(m + 1)
                        for tile in tiles_out:
                            off, wd = TILES[tile]
                            col = 1 if tile == 0 else 0
                            ps = P.ps()
                            for k in range(8):
                                P.op("tensor", lambda e: e.matmul(ps.t[:, :wd], lhsT=wo.t[:, k, :],
                                                                  rhs=cat.t[:, k, off:off + wd], start=(k == 0), stop=(k == 7)),
                                     reads=[(wo, 0), (cat, hseg(k, tile))], writes=[(ps, 0)], inc=(k == 7))
                            P.op("vector", lambda e: e.scalar_tensor_tensor(out=res.t[:, m, off:off + wd], in0=ps.t[:, :wd],
                                                                            scalar=MT.t[:, l, 2, m, col:col + 1],
                                                                            in1=res.t[:, m, off:off + wd],
                                                                            op0=ALU.mult, op1=ALU.add),
                                 reads=[(ps, 0), (MT, l), (res, hseg(m, tile))], writes=[(res, hseg(m, tile))])

            if l + 1 < NL:
                mod_consume()
                mod_finish(l + 1)

        with P.phase() as ph:
            tmp = [P.buf("f_tmp%d" % i, [128, 512], F32, stack=ph) for i in range(2)]
            ost = [P.buf("f_ost%d" % i, [128, D], F32, dma=True, stack=ph) for i in range(2)]
            oi = 0
            rb = rms_alloc(ph, "f")
            for it_, tile in enumerate([1, 2, 3, 4]):
                off, wd = TILES[tile]
                rs = rms_stats(tile, rb, it_)
                for k in range(8):
                    tb = tmp[k % 2]
                    tt("gpsimd", tb.t[:, :wd], res.t[:, k, off:off + wd], rs.t[:, :wd], ALU.mult,
                       [(res, hseg(k, tile)), (rs, 0)], [(tb, 0)])
                    act(res.t[:, k, off:off + wd], tb.t[:, :wd], AF.Identity, [(tb, 0), (VD, 0)],
                        [(res, hseg(k, tile))], scale=VD.t[:, k, 2:3])
                for q in range(4):
                    o_ = ost[oi % 2]
                    oi += 1
                    c0 = off + q * 128
                    for half in range(2):
                        ps = P.ps()
                        for kk in range(4):
                            k = half * 4 + kk
                            P.op("tensor", lambda e: e.transpose(out=ps.t[:, kk * 128:(kk + 1) * 128],
                                                                 in_=res.t[:, k, c0:c0 + 128], identity=ident_f.t[:]),
                                 reads=[(res, hseg(k, tile)), (ident_f, 0)], writes=[(ps, 0)], inc=(kk == 3))
                        if half == 0:
                            P.op("vector", lambda e: e.tensor_copy(out=o_.t[:, 0:512], in_=ps.t[:, 0:512]),
                                 reads=[(ps, 0)], writes=[(o_, 0)])
                        else:
                            act(o_.t[:, 512:1024], ps.t[:, 0:512], AF.Identity, [(ps, 0)], [(o_, 0)])
                    r0 = (tile - 1) * 512 + q * 128
                    P.dma("sync", out_d[r0:r0 + 128, :], o_.t[:], reads=[(o_, 0)], sem_buf=o_)
            P.wait_all("sync", ost)
    return nc


def prep_inputs(inp, NL=DEPTH):
    f = lambda a: np.ascontiguousarray(np.asarray(a, dtype=np.float32))
    x = f(inp["x"])
    B = x.shape[0]
    c = f(inp["c"])
    ctx = f(inp["ctx"])
    c_ctx = f(inp["c_ctx"])
    mod_b = f(inp["mod_b"])
    norm_g = f(inp["norm_g"])
    final_g = f(inp["final_g"])
    vec5 = np.zeros((DEPTH, NV, BW), np.float32)
    gm = np.zeros((DEPTH, 4, 128, 4, 128), np.float32)
    a_conv, a_br, a_bi, a_lam = f(inp["a_conv"]), f(inp["a_br"]), f(inp["a_bi"]), f(inp["a_lam"])
    b_conv, c_conv = f(inp["b_conv"]), f(inp["c_conv"])
    a_wr, a_wi = f(inp["a_wr"]), f(inp["a_wi"])
    nl = a_conv.shape[0]
    for l in range(nl):
        vec5[l, 0:4] = a_conv[l, 0]
        vec5[l, 4:8] = a_conv[l, 1]
        vec5[l, 8:10] = a_br[l]
        vec5[l, 10:12] = a_bi[l]
        vec5[l, 12:14] = a_lam[l]
        vec5[l, 14:17] = b_conv[l]
        vec5[l, 17:48] = c_conv[l]
        vec5[l, 48] = f(inp["c_ln_g"])[l]
        vec5[l, 49] = f(inp["c_ln_b"])[l]
        vec5[l, 50] = f(inp["c_pw_b"])[l]
        vec5[l, 51] = f(inp["d_b"])[l].reshape(-1)
        vec5[l, 52] = f(inp["d_scale"])[l]
        for j in range(4):
            for n in range(2):
                for gi, wsrc in enumerate((a_wr, a_wi)):
                    for hh in range(2):
                        gm[l, j, hh * 64:(hh + 1) * 64, n * 2 + gi, hh * 64:(hh + 1) * 64] = wsrc[l, n, 2 * j + hh]
    shared = {
        "vec5": vec5, "gm": gm,
        "mod_w": f(inp["mod_w"]), "w_in": f(inp["w_in"]), "w_out": f(inp["w_out"]),
        "c_pw": f(inp["c_pw"]), "d_w": f(inp["d_w"]),
        "pmats": PM,
        "icc": np.ascontiguousarray(np.broadcast_to(ICC.reshape(1, -1), (128, 256))).astype(np.float32),
        "tabc": np.ascontiguousarray(np.broadcast_to(TABC.reshape(1, -1), (128, 4 * CTX))).astype(np.float32),
    }
    for k_ in ("mod_w", "w_in", "w_out", "c_pw", "d_w"):
        a = shared[k_]
        if a.shape[0] < DEPTH:
            pad = np.zeros((DEPTH - a.shape[0],) + a.shape[1:], np.float32)
            shared[k_] = np.concatenate([a, pad], 0)
    maps = []
    for b in range(B):
        vecD = np.zeros((3 + 4 * DEPTH, D), np.float32)
        vecD[0] = c[b]
        vecD[1] = c_ctx
        vecD[2] = final_g
        for l in range(nl):
            vecD[3 + 4 * l] = norm_g[l]
            vecD[3 + 4 * l + 1:3 + 4 * l + 4] = mod_b[l].reshape(3, D)
        m = dict(shared)
        m["x"] = x[b]
        m["ctx"] = ctx[b]
        m["vecD"] = vecD
        maps.append(m)
    return maps


_NC_CACHE = {}


def kernel(**inputs):
    maps = prep_inputs(inputs)
    if "nc" not in _NC_CACHE:
        _NC_CACHE["nc"] = build(DEPTH)
    nc = _NC_CACHE["nc"]
    res = run_bass_kernel_spmd(nc, maps, core_ids=list(range(8)))
    out = np.stack([np.asarray(r["out"], dtype=np.float32) for r in res.results], 0)
    return out
```
